# Optimizing a Trainium2 kernel written in Bass

```python
import jax, jax.numpy as jnp
from jax import lax
import numpy as np

D_MODEL = 1024
BATCH = 16
SEQ = 2048
DEPTH = 2

GRID_W = 64
CTX_LEN = 256
BRANCH_W = D_MODEL // 4
D_MIX = 4 * BRANCH_W

MLA_V_DIM = 64
MLA_HEADS = BRANCH_W // MLA_V_DIM
MLA_NOPE_DIM = 64
MLA_ROPE_DIM = 32
MLA_QK_DIM = MLA_NOPE_DIM + MLA_ROPE_DIM
MLA_Q_RANK = 192
MLA_KV_RANK = 128

GQA_HEAD_DIM = 64
GQA_HEADS = BRANCH_W // GQA_HEAD_DIM
GQA_KV_HEADS = GQA_HEADS // 2

CHUNK = 128
CM_GROUPS = 4
CM_GDIM = BRANCH_W // CM_GROUPS

FNET_GROUPS = 4

Q_BLOCK = 128
ROPE_THETA = 10000.0
EPS = 1e-6

IN_SPLITS = (
    MLA_Q_RANK, MLA_KV_RANK, MLA_ROPE_DIM, BRANCH_W,
    GQA_HEADS * GQA_HEAD_DIM, GQA_KV_HEADS * GQA_HEAD_DIM, GQA_KV_HEADS * GQA_HEAD_DIM, BRANCH_W,
    BRANCH_W, BRANCH_W, BRANCH_W,
    BRANCH_W, BRANCH_W,
)
IN_WIDTH = sum(IN_SPLITS)
SPLIT_IDX = tuple(sum(IN_SPLITS[: i + 1]) for i in range(len(IN_SPLITS) - 1))

kernel_name = "hybrid_parallel_groups_mla_gqa_gmlp_fnet_dit"


def rmsnorm(x, g):
    xf = x.astype(jnp.float32)
    y = xf * lax.rsqrt(jnp.mean(xf * xf, axis=-1, keepdims=True) + EPS)
    return (y * g.astype(jnp.float32)).astype(x.dtype)


def layernorm(x, g, b):
    xf = x.astype(jnp.float32)
    mu = jnp.mean(xf, axis=-1, keepdims=True)
    var = jnp.mean(jnp.square(xf - mu), axis=-1, keepdims=True)
    y = (xf - mu) * lax.rsqrt(var + EPS)
    return (y * g.astype(jnp.float32) + b.astype(jnp.float32)).astype(x.dtype)


def rope_1d(x, pos):
    d = x.shape[-1]
    inv = ROPE_THETA ** (-jnp.arange(0, d, 2, dtype=jnp.float32) / d)
    ang = pos.astype(jnp.float32)[:, None] * inv[None, :]
    cos = jnp.concatenate([jnp.cos(ang), jnp.cos(ang)], -1)[:, None, :]
    sin = jnp.concatenate([jnp.sin(ang), jnp.sin(ang)], -1)[:, None, :]
    xf = x.astype(jnp.float32)
    rot = jnp.concatenate([-xf[..., d // 2:], xf[..., : d // 2]], -1)
    return (xf * cos + rot * sin).astype(x.dtype)


def axial_rope(x, row, col):
    half = x.shape[-1] // 2
    return jnp.concatenate([rope_1d(x[..., :half], row), rope_1d(x[..., half:], col)], -1)


def attend(q, k, v):
    B, S, H, dk = q.shape
    Hk, dv = k.shape[2], v.shape[-1]
    G = H // Hk
    scale = dk ** -0.5
    nb = S // Q_BLOCK
    qb = q.reshape(B, nb, Q_BLOCK, Hk, G, dk).transpose(1, 0, 2, 3, 4, 5)

    def block(qblk):
        s = jnp.einsum("bqkgd,blkd->bkgql", qblk, k).astype(jnp.float32) * scale
        pr = jax.nn.softmax(s, axis=-1).astype(v.dtype)
        return jnp.einsum("bkgql,blkd->bqkgd", pr, v)

    out = lax.map(block, qb)
    return out.transpose(1, 0, 2, 3, 4, 5).reshape(B, S, H * dv)


def attn_features(parts, p, row, col):
    cq, ckv, kr = parts[0], parts[1], parts[2]
    q2, k2, v2 = parts[4], parts[5], parts[6]
    B, S, _ = cq.shape
    q1 = (rmsnorm(cq, p["mla_q_norm"]) @ p["mla_w_uq"]).reshape(B, S, MLA_HEADS, MLA_QK_DIM)
    kv = (rmsnorm(ckv, p["mla_kv_norm"]) @ p["mla_w_ukv"]).reshape(B, S, MLA_HEADS, MLA_NOPE_DIM + MLA_V_DIM)
    k_nope, v1 = kv[..., :MLA_NOPE_DIM], kv[..., MLA_NOPE_DIM:]
    k_rope = jnp.broadcast_to(kr[:, :, None, :], (B, S, MLA_HEADS, MLA_ROPE_DIM))
    k1 = jnp.concatenate([k_nope, k_rope], -1)
    q1 = rmsnorm(q1, p["mla_qn"])
    k1 = rmsnorm(k1, p["mla_kn"])
    q2 = rmsnorm(q2.reshape(B, S, GQA_HEADS, GQA_HEAD_DIM), p["gqa_qn"])
    k2 = rmsnorm(k2.reshape(B, S, GQA_KV_HEADS, GQA_HEAD_DIM), p["gqa_kn"])
    v2 = v2.reshape(B, S, GQA_KV_HEADS, GQA_HEAD_DIM)
    if row is not None:
        q1 = jnp.concatenate([q1[..., :MLA_NOPE_DIM], axial_rope(q1[..., MLA_NOPE_DIM:], row, col)], -1)
        k1 = jnp.concatenate([k1[..., :MLA_NOPE_DIM], axial_rope(k1[..., MLA_NOPE_DIM:], row, col)], -1)
        q2 = axial_rope(q2, row, col)
        k2 = axial_rope(k2, row, col)
    return q1, k1, v1, q2, k2, v2


def chunk_mlp(u, v, p):
    B, S, W = v.shape
    vn = layernorm(v, p["cm_ln_g"], p["cm_ln_b"]).reshape(B, S // CHUNK, CHUNK, CM_GROUPS, CM_GDIM)
    s = jnp.einsum("gpq,bnqgc->bnpgc", p["cm_w_s"], vn) + p["cm_b_s"].T[:, :, None]
    return u * s.reshape(B, S, W)


def fourier(f, w_f):
    B, S, W = f.shape
    ff = f.astype(jnp.float32).reshape(B, S, FNET_GROUPS, W // FNET_GROUPS)
    y = jnp.fft.fft2(ff, axes=(1, 3), norm="ortho").real.astype(f.dtype).reshape(B, S, W)
    return y @ w_f


def merge(parts, att_a, att_b, p):
    ga, gb, u, vc, gc, f, gd = parts[3], parts[7], parts[8], parts[9], parts[10], parts[11], parts[12]
    ya = att_a * jax.nn.silu(ga)
    yb = att_b * jax.nn.silu(gb)
    yc = chunk_mlp(u, vc, p) * jax.nn.silu(gc)
    yd = fourier(f, p["fnet_w"]) * jax.nn.silu(gd)
    return jnp.concatenate([ya, yb, yc, yd], -1) @ p["w_out"]


def layer(x, ctx, c, c_ctx, p, row, col, update_ctx):
    sh, sc, gt = jnp.split(jax.nn.silu(c) @ p["w_mod"] + p["b_mod"], 3, axis=-1)
    shc, scc, gtc = jnp.split(jax.nn.silu(c_ctx) @ p["w_mod"] + p["b_mod"], 3, axis=-1)
    hx = rmsnorm(x, p["norm_g"]) * (1.0 + sc[:, None, :]) + sh[:, None, :]
    hc = rmsnorm(ctx, p["norm_g"]) * (1.0 + scc) + shc
    px = jnp.split(hx @ p["w_in"], SPLIT_IDX, axis=-1)
    pc = jnp.split(hc @ p["w_in"], SPLIT_IDX, axis=-1)
    qa_x, ka_x, va_x, qb_x, kb_x, vb_x = attn_features(px, p, row, col)
    qa_c, ka_c, va_c, qb_c, kb_c, vb_c = attn_features(pc, p, None, None)
    att_a = attend(qa_x, jnp.concatenate([ka_x, ka_c], 1), jnp.concatenate([va_x, va_c], 1))
    att_b = attend(qb_x, jnp.concatenate([kb_x, kb_c], 1), jnp.concatenate([vb_x, vb_c], 1))
    x_new = x + gt[:, None, :] * merge(px, att_a, att_b, p)
    if update_ctx:
        att_ac = attend(qa_c, ka_c, va_c)
        att_bc = attend(qb_c, kb_c, vb_c)
        ctx = ctx + gtc * merge(pc, att_ac, att_bc, p)
    return x_new, ctx


def setup_inputs(seed: int = 0) -> dict:
    key = jax.random.key(seed)
    ks = jax.random.split(key, 24)
    f32 = jnp.float32

    def nrm(k, shape, scale):
        return jax.random.normal(k, shape, f32) * scale

    def gain(k, shape):
        return 1.0 + 0.02 * jax.random.normal(k, shape, f32)

    L, D = DEPTH, D_MODEL
    return {
        "x": nrm(ks[0], (BATCH, SEQ, D), 1.0),
        "c": nrm(ks[1], (BATCH, D), 1.0),
        "ctx": nrm(ks[2], (BATCH, CTX_LEN, D), 1.0),
        "c_ctx": nrm(ks[3], (D,), 1.0),
        "norm_g": gain(ks[4], (L, D)),
        "w_mod": nrm(ks[5], (L, D, 3 * D), 0.5 * D ** -0.5),
        "b_mod": nrm(ks[6], (L, 3 * D), 0.02),
        "w_in": nrm(ks[7], (L, D, IN_WIDTH), D ** -0.5),
        "mla_q_norm": gain(ks[8], (L, MLA_Q_RANK)),
        "mla_w_uq": nrm(ks[9], (L, MLA_Q_RANK, MLA_HEADS * MLA_QK_DIM), MLA_Q_RANK ** -0.5),
        "mla_kv_norm": gain(ks[10], (L, MLA_KV_RANK)),
        "mla_w_ukv": nrm(ks[11], (L, MLA_KV_RANK, MLA_HEADS * (MLA_NOPE_DIM + MLA_V_DIM)), MLA_KV_RANK ** -0.5),
        "mla_qn": gain(ks[12], (L, MLA_QK_DIM)),
        "mla_kn": gain(ks[13], (L, MLA_QK_DIM)),
        "gqa_qn": gain(ks[14], (L, GQA_HEAD_DIM)),
        "gqa_kn": gain(ks[15], (L, GQA_HEAD_DIM)),
        "cm_ln_g": gain(ks[16], (L, BRANCH_W)),
        "cm_ln_b": nrm(ks[17], (L, BRANCH_W), 0.02),
        "cm_w_s": nrm(ks[18], (L, CM_GROUPS, CHUNK, CHUNK), CHUNK ** -0.5),
        "cm_b_s": gain(ks[19], (L, CM_GROUPS, CHUNK)),
        "fnet_w": nrm(ks[20], (L, BRANCH_W, BRANCH_W), BRANCH_W ** -0.5),
        "w_out": nrm(ks[21], (L, D_MIX, D), D_MIX ** -0.5),
    }


def reference(x, c, ctx, c_ctx, norm_g, w_mod, b_mod, w_in, mla_q_norm, mla_w_uq, mla_kv_norm,
              mla_w_ukv, mla_qn, mla_kn, gqa_qn, gqa_kn, cm_ln_g, cm_ln_b, cm_w_s, cm_b_s,
              fnet_w, w_out):
    S = x.shape[1]
    ROWS = S // GRID_W
    row = jnp.repeat(jnp.arange(ROWS, dtype=jnp.int32), GRID_W)
    col = jnp.tile(jnp.arange(GRID_W, dtype=jnp.int32), ROWS)
    for l in range(DEPTH):
        p = dict(norm_g=norm_g[l], w_mod=w_mod[l], b_mod=b_mod[l], w_in=w_in[l],
                 mla_q_norm=mla_q_norm[l], mla_w_uq=mla_w_uq[l], mla_kv_norm=mla_kv_norm[l],
                 mla_w_ukv=mla_w_ukv[l], mla_qn=mla_qn[l], mla_kn=mla_kn[l],
                 gqa_qn=gqa_qn[l], gqa_kn=gqa_kn[l], cm_ln_g=cm_ln_g[l], cm_ln_b=cm_ln_b[l],
                 cm_w_s=cm_w_s[l], cm_b_s=cm_b_s[l], fnet_w=fnet_w[l], w_out=w_out[l])
        x, ctx = layer(x, ctx, c, c_ctx, p, row, col, l < DEPTH - 1)
    return x
```

```python
import contextlib
import numpy as np
import ml_dtypes
import concourse.bass as bass
import concourse.mybir as mybir
from concourse.bass_types import AP
from concourse.bass_utils import run_bass_kernel_spmd

F32 = mybir.dt.float32
BF16 = mybir.dt.bfloat16
AF = mybir.ActivationFunctionType
ALU = mybir.AluOpType
AX = mybir.AxisListType

D = 1024
DEPTH = 2
GRID_W = 64
EPS = 1e-6
THETA = 10000.0
NCORES = 8
USE_HALF = True
C_CQ, C_CKV, C_KR, C_GA, C_Q2, C_K2, C_V2, C_GB, C_U, C_VC, C_GC, C_F, C_GD = (
    0, 192, 320, 352, 608, 864, 992, 1120, 1376, 1632, 1888, 2144, 2400)
INW = 2656


class Buf:
    __slots__ = ("name", "w", "r")

    def __init__(self, name):
        self.name = name
        self.w = None
        self.r = []


class Trk:
    def __init__(self, nc, es):
        self.nc = nc
        self.es = es
        self.eng = {"pe": nc.tensor, "act": nc.scalar, "dve": nc.vector,
                    "pool": nc.gpsimd, "sp": nc.sync}
        self.sem = {}
        self.cnt = {}
        self.waited = {k: {} for k in self.eng}
        for k in self.eng:
            self.sem[k] = es.enter_context(nc.semaphore("s_" + k))
            self.cnt[k] = 0
        self.dsem = {}
        self.dcnt = {}
        self.prog = {k: [] for k in self.eng}

    def dma_sem(self, name):
        if name not in self.dsem:
            self.dsem[name] = self.es.enter_context(self.nc.semaphore("d_" + name))
            self.dcnt[name] = 0

    def _sem_of(self, key):
        return self.sem[key] if key in self.sem else self.dsem[key]

    def _wait(self, e, tok):
        key, val = tok
        if key == e and e == "pe":
            return
        if self.waited[e].get(key, 0) >= val:
            return
        self.prog[e].append(("w", self._sem_of(key), val))
        self.waited[e][key] = val

    def _deps(self, e, reads, writes):
        for b in reads:
            if b.w is not None:
                self._wait(e, b.w)
        for b in writes:
            if b.w is not None and b.w[0] != e:
                self._wait(e, b.w)
            for t in b.r:
                if t[0] != e:
                    self._wait(e, t)

    def _mark(self, tok, reads, writes):
        for b in reads:
            b.r.append(tok)
            if len(b.r) > 24:
                b.r = b.r[-24:] if False else b.r
        for b in writes:
            b.w = tok
            b.r = []

    def op(self, e, fn, reads=(), writes=(), inc=True):
        self._deps(e, reads, writes)
        if inc:
            self.cnt[e] += 1
            self.prog[e].append(("i", fn, self.sem[e], 1))
            tok = (e, self.cnt[e])
        else:
            self.prog[e].append(("i", fn, None, 0))
            tok = (e, self.cnt[e] + 1)
        self._mark(tok, reads, writes)
        return tok

    def dma(self, q, semname, out, in_, reads=(), writes=(), **kw):
        semname = semname + "_" + q
        self.dma_sem(semname)
        self._deps(q, reads, writes)
        self.dcnt[semname] += 16
        self.prog[q].append(("i", (lambda e, o=out, i=in_, k=kw: e.dma_start(out=o, in_=i, **k)),
                             self.dsem[semname], 16))
        tok = (semname, self.dcnt[semname])
        self._mark(tok, reads, writes)
        return tok

    def barrier(self):
        keys = [k for k in self.eng]
        for e in keys:
            for f in keys:
                if f != e and self.cnt[f] > 0:
                    self._wait(e, (f, self.cnt[f]))
            for dn in self.dsem:
                if self.dcnt[dn] > 0:
                    self._wait(e, (dn, self.dcnt[dn]))

    def emit(self):
        nc = self.nc
        with nc.Block() as block:
            for k, starter in (("sp", block.sync), ("act", block.scalar), ("dve", block.vector),
                               ("pool", block.gpsimd), ("pe", block.tensor)):
                prog = self.prog[k]
                if not prog:
                    continue

                def body(eng, prog=prog):
                    for it in prog:
                        if it[0] == "w":
                            eng.wait_ge(it[1], it[2])
                        else:
                            inst = it[1](eng)
                            if it[2] is not None:
                                inst.then_inc(it[2], it[3])
                starter(body)


def rr(ap, s, **kw):
    return ap.rearrange(s, **kw)


def build(NB, S, CTX):
    NT, NC = S // 128, CTX // 128
    NK = NT + NC
    KT = NK * 128
    QB = min(512, S)
    NQB = S // QB
    KB = min(512, S)
    NKB = S // KB
    AG = 2
    nc = bass.Bass("TRN2", target_bir_lowering=False)
    dt = nc.dram_tensor

    def din(name, shape, dty=F32):
        return dt(name, list(shape), dty, kind="ExternalInput").ap()

    x_d = din("x", [NB, S, D]); c_d = din("c", [NB + 1, D]); ctx_d = din("ctx", [NB, CTX, D])
    norm_g_d = din("norm_g", [DEPTH, D]); w_mod_d = din("w_mod", [DEPTH, D, 3 * D])
    b_mod_d = din("b_mod", [DEPTH, 3 * D]); w_in_d = din("w_in", [DEPTH, D, INW])
    qnorm_d = din("mla_q_norm", [DEPTH, 192]); wuq_d = din("mla_w_uq", [DEPTH, 192, 384])
    kvnorm_d = din("mla_kv_norm", [DEPTH, 128]); wukv_d = din("mla_w_ukv", [DEPTH, 128, 512])
    mqn_d = din("mla_qn", [DEPTH, 96]); mkn_d = din("mla_kn", [DEPTH, 96])
    gqn_d = din("gqa_qn", [DEPTH, 64]); gkn_d = din("gqa_kn", [DEPTH, 64])
    lng_d = din("cm_ln_g", [DEPTH, 256]); lnb_d = din("cm_ln_b", [DEPTH, 256])
    cmw_d = din("cm_w_s", [DEPTH, 4, 128, 128]); cmb_d = din("cm_b_s", [DEPTH, 512])
    wf_d = din("fnet_w", [DEPTH, 256, 256]); wout_d = din("w_out", [DEPTH, D, D])
    ident_d = din("ident", [128, 128], BF16)
    ecs_d = din("ecs", [256, 512], BF16)
    ropeA_d = din("ropeA", [128, NT, 2, 32], BF16)
    ropeB_d = din("ropeB", [128, NT, 2, 64], BF16)
    KBH = min(512, S // 2)
    if not USE_HALF:
        tabL_d = din("tabL", [NKB, 128, NT, 2, KB], BF16)
    tabH_d = din("tabH", [(S // 2) // KBH, 128, NT, 2, KBH], BF16)
    altc_d = din("altc", [128, 1], BF16)
    tabC_d = din("tabC", [1, 128, NC, 2, CTX], BF16)
    out_d = dt("out", [NB, S, D], F32, kind="ExternalOutput").ap()
    ctxw_d = dt("ctxw", [NB, CTX, D], F32, kind="Internal").ap()
    hxs_d = dt("hxs", [128, 8, S + CTX], BF16, kind="Internal").ap()
    mods_d = dt("mods", [NB + 1, 3 * D], F32, kind="Internal").ap()

    es = contextlib.ExitStack()
    T = Trk(nc, es)

    def sb(name, shape, dty):
        return es.enter_context(nc.sbuf_tensor("sb_" + name, list(shape), dty))

    def psum(name, shape, dty):
        return es.enter_context(nc.psum_tensor("ps_" + name, list(shape), dty))

    Win = sb("Win", [128, 8, INW], BF16); bWin = Buf("Win")
    Wout = sb("Wout", [128, 8, D], BF16); bWout = Buf("Wout")
    Wuq = sb("Wuq", [128, 2, 384], BF16); bWuq = Buf("Wuq")
    Wukv = sb("Wukv", [128, 512], BF16); bWukv = Buf("Wukv")
    WsT = sb("WsT", [128, 4, 128], BF16); bWsT = Buf("WsT")
    brow = sb("brow", [1, 512], BF16); bbrow = Buf("brow")
    ones_r = sb("ones_r", [1, 128], BF16); bones = Buf("ones_r")
    Wcs = sb("Wcs", [128, 2, 512], BF16); bWcs = Buf("Wcs")
    bEcs = Buf("Ecs")
    ident = sb("ident", [128, 128], BF16); bident = Buf("ident")
    gbc = sb("gbc", [128, 832], F32); bgbc = Buf("gbc")
    GQN, GKN, GQ2, GK2, LNG, LNB = 0, 96, 192, 256, 320, 576
    ropeA = sb("ropeA", [128, NT, 2, 32], BF16); ropeB = sb("ropeB", [128, NT, 2, 64], BF16)
    brope = Buf("rope")
    epsT = sb("epsT", [128, 1], F32); beps = Buf("eps")
    identF = sb("identF", [64, 64], F32); bidentF = Buf("identF")
    altc = sb("altc", [128, 1], BF16); baltc = Buf("altc")
    modAB = sb("modAB", [128, NB + 1, 3, 8], F32); bmodAB = Buf("modAB")
    ngcol = sb("ngcol", [128, 8], F32)
    gtbc = sb("gtbc", [128, 1, D], F32); _bgt = Buf("gt0"); bgt = [_bgt, _bgt]
    K1T = sb("K1T", [128, 4, KT], BF16)
    V1a = sb("V1a", [128, NK, 384], BF16)
    K2T = sb("K2T", [128, KT], BF16)
    V2a = sb("V2a", [128, NK, 320], BF16)
    bKV = [Buf("kv%d" % k) for k in range(NK)]
    ydT = sb("ydT", [128, 2, S], BF16); bydT = [Buf("yd%d" % k) for k in range(NKB)]
    ydTc = sb("ydTc", [128, 2, CTX], BF16); bydTc = Buf("ydTc")
    Gc = sb("Gc", [128, NC, 512], BF16); bGc = [Buf("Gc%d" % k) for k in range(NC)]
    xt = [sb("xt%d" % i, [128, D], F32) for i in range(2)]; bxt = [Buf("xt0"), Buf("xt1")]
    _xn = sb("xn0", [128, D], BF16); xn = [_xn, _xn]; _bxn = Buf("xn0"); bxn = [_bxn, _bxn]
    junk = sb("junk", [128, D], BF16); bjunk = Buf("junk")
    st = [sb("st%d" % i, [128, 16], F32) for i in range(2)]; bst = [Buf("st0"), Buf("st1")]
    hxT = sb("hxT", [128, 8, 512], BF16); bhx = [Buf("hx%d" % i) for i in range(4)]; bhxd = [Buf("hxd%d" % i) for i in range(4)]
    bHX = {}
    fT = sb("fT", [128, 2, 512], BF16); bfT = Buf("fT")
    pA = [psum("pA%d" % i, [128, 512], F32) for i in range(2)]; bpA = [Buf("pA0"), Buf("pA1")]
    pB = [psum("pB%d" % i, [128, 512], F32) for i in range(2)]; bpB = [Buf("pB0"), Buf("pB1")]
    pO = [psum("pO%d" % i, [128, 512], F32) for i in range(2)]; bpO = [Buf("pO0"), Buf("pO1")]
    pT = psum("pT", [128, 1024], BF16); bpT = Buf("pT")
    pT2 = psum("pT2", [128, 1024], BF16); bpT2 = Buf("pT2")

    class SC:
        pass
    SCS = []
    for k_ in range(2):
        o_ = SC()
        for nm_ in ("tmA", "tmB", "tmC", "tmD"):
            setattr(o_, nm_, sb("%s_%d" % (nm_, k_), [128, 512], F32)); setattr(o_, "b" + nm_, Buf("%s_%d" % (nm_, k_)))
        o_.tb16 = sb("tb16_%d" % k_, [128, 512], BF16); o_.btb16 = Buf("tb16_%d" % k_)
        o_.tT = sb("tT_%d" % k_, [128, 256], BF16); o_.btT = Buf("tT_%d" % k_)
        o_.sm = [sb("sm%d_%d" % (k_, i), [128, 32], F32) for i in range(2)]
        o_.bsm = [Buf("sm%d_%d" % (k_, i)) for i in range(2)]
        o_.smi = 0

        def nsm(o_=o_):
            i = o_.smi
            o_.smi = 1 - i
            return o_.sm[i], o_.bsm[i]
        o_.nsm = nsm
        SCS.append(o_)
    for k_ in range(2):
        SCS[k_].P, SCS[k_].bP, SCS[k_].Q, SCS[k_].bQ = pA[k_], bpA[k_], pB[k_], bpB[k_]
    SCS[0].pT, SCS[0].bpT, SCS[1].pT, SCS[1].bpT = pT2, bpT2, pT, bpT
    tmA, btmA, tmB, btmB, tmC, btmC, tmD, btmD = (SCS[0].tmA, SCS[0].btmA, SCS[0].tmB, SCS[0].btmB,
                                                    SCS[0].tmC, SCS[0].btmC, SCS[0].tmD, SCS[0].btmD)
    tb16, btb16, tT, btT = SCS[0].tb16, SCS[0].btb16, SCS[0].tT, SCS[0].btT
    sm, bsm = SCS[0].sm, SCS[0].bsm
    rinv, brinv = SCS[1].tmA, SCS[1].btmA
    tmpo, btmpo = tmA, btmA
    _res = sb("res0", [128, D], F32); res = [_res, _res]; bres = [Buf("res0"), Buf("res0b")]
    RS = max(NT * 512 + 2 * AG * 2 * KB, 512 * (4 + 2 + 4 + 2 + 4 + 4 + 4))
    R = sb("R", [128, RS], BF16)
    G = rr(R[:, 0:NT * 512], "p (a n) -> p a n", a=NT); bG = [Buf("G%d" % k) for k in range(NT)]
    tabv = [rr(R[:, NT * 512 + i * AG * 2 * KB: NT * 512 + (i + 1) * AG * 2 * KB],
               "p (a j n) -> p a j n", a=AG, j=2) for i in range(2)]
    btab = [Buf("tab0"), Buf("tab1")]
    o = 0
    q1T = rr(R[:, o:o + 2048], "p (h n) -> p h n", h=4); o += 2048; bq1T = Buf("q1T")
    q2T = rr(R[:, o:o + 1024], "p (h n) -> p h n", h=2); o += 1024; bq2T = Buf("q2T")
    gT = rr(R[:, o:o + 2048], "p (h n) -> p h n", h=4); o += 2048; bgT = Buf("gT")
    ugT = rr(R[:, o:o + 1024], "p (h n) -> p h n", h=2); o += 1024; bugT = Buf("ugT")
    ccT = rr(R[:, o:o + 2048], "p (h n) -> p h n", h=4); o += 2048
    bcc = [Buf("cc%d" % k) for k in range(4)]
    ycT = rr(R[:, o:o + 2048], "p (y h n) -> p y h n", y=2, h=2); o += 2048
    byc = [Buf("yc0"), Buf("yc1")]
    probs = [R[:, o + i * 512: o + (i + 1) * 512] for i in range(4)]; o += 2048
    bprobs = [Buf("pr%d" % i) for i in range(4)]
    sgc = SCS[0].tb16; bsgc = SCS[0].btb16
    Ecs = rr(R[:, 8192:9216], "p (c n) -> p c n", c=2)

    SRING = [(pA[0], bpA[0]), (pA[1], bpA[1]), (pB[0], bpB[0])]
    pT_, bpT_ = pT, bpT
    ctr = {"xt": 0, "st": 0, "pa": 0, "pb": 0, "res": 0, "sm": 0, "tab": 0, "pr": 0, "po": 0}

    def nxt(k, n=2):
        v = ctr[k]
        ctr[k] = (v + 1) % n
        return v

    def mm(out, lhsT, rhs, start, stop, reads, writes, inc=True):
        T.op("pe", lambda e: e.matmul(out, lhsT=lhsT, rhs=rhs, start=start, stop=stop),
             reads=reads, writes=writes, inc=inc)

    def tr(out, in_, reads, writes):
        n = in_.shape[0]
        T.op("pe", lambda e: e.transpose(out, in_, ident[0:n, 0:n]), reads=list(reads) + [bident], writes=writes)

    def act(out, in_, func, reads, writes, **kw):
        T.op("act", lambda e: e.activation(out=out, in_=in_, func=func, **kw), reads=reads, writes=writes)

    def tt(e, out, in0, in1, op, reads, writes):
        T.op(e, lambda en: en.tensor_tensor(out=out, in0=in0, in1=in1, op=op), reads=reads, writes=writes)

    def ts(e, out, in0, s1, s2, op0, op1, reads, writes):
        if op1 is None:
            T.op(e, lambda en: en.tensor_scalar(out=out, in0=in0, scalar1=s1, scalar2=None, op0=op0),
                 reads=reads, writes=writes)
        else:
            T.op(e, lambda en: en.tensor_scalar(out=out, in0=in0, scalar1=s1, scalar2=s2, op0=op0, op1=op1),
                 reads=reads, writes=writes)

    def cp(e, out, in_, reads, writes):
        if e == "act":
            T.op("act", lambda en: en.copy(out=out, in_=in_), reads=reads, writes=writes)
        else:
            T.op(e, lambda en: en.tensor_copy(out=out, in_=in_), reads=reads, writes=writes)

    def rsqrt_cols(dst, src, scale, s_buf, reads):
        act(dst, src, AF.Sqrt, reads=list(reads) + [s_buf, beps], writes=[s_buf], scale=scale, bias=epsT[:, 0:1])
        T.op("dve", lambda e: e.reciprocal(out=dst, in_=dst), reads=[s_buf], writes=[s_buf])

    def bc(ap, shape):
        return ap.to_broadcast(list(shape))

    def rope(xv, outv, tab, ti, H, dh, x_buf, out_bufs, scratch, s_buf):
        half = dh // 4
        t1 = rr(scratch[:, 0:H * dh], "p (h d) -> p h d", h=H)
        t2 = rr(scratch[:, 256:256 + H * dh], "p (h d) -> p h d", h=H)
        cosv = bc(tab[:, ti, 0, :].unsqueeze(1), [128, H, dh])
        tt("pool", t1, xv, cosv, ALU.mult, reads=[x_buf, brope], writes=[s_buf])
        for j in range(2):
            for g in range(2):
                o_ = t2[:, :, g * 2 * half + j * half: g * 2 * half + (j + 1) * half]
                i_ = xv[:, :, g * 2 * half + (1 - j) * half: g * 2 * half + (2 - j) * half]
                s_ = bc(tab[:, ti, 1, g * 2 * half + j * half: g * 2 * half + (j + 1) * half].unsqueeze(1),
                        [128, H, half])
                tt("pool", o_, i_, s_, ALU.mult, reads=[x_buf, brope, s_buf], writes=[s_buf])
        for (ov, hs) in outv:
            tt("pool", ov, t1[:, hs, :], t2[:, hs, :], ALU.add, reads=[s_buf], writes=out_bufs)

    T.dma("sp", "c0", ident[:], ident_d[:, :], writes=[bident])
    T.dma("sp", "c0", altc[:], altc_d[:, :], writes=[baltc])
    T.dma("sp", "c0", ropeA[:], ropeA_d[:, :, :, :], writes=[brope])
    T.dma("sp", "c0", ropeB[:], ropeB_d[:, :, :, :], writes=[brope])
    ctok = ("c0_sp", T.dcnt["c0_sp"])
    for b in (bident, brope, baltc):
        b.w = ctok
    T.op("dve", lambda e: e.memset(epsT[:], EPS), writes=[beps])
    T.op("dve", lambda e: e.tensor_copy(out=identF[:, :], in_=ident[0:64, 0:64]), reads=[bident], writes=[bidentF])
    T.op("dve", lambda e: e.memset(ones_r[:], 1.0), writes=[bones])
    T.op("pool", lambda e: e.memset(V1a[:], 1.0), writes=bKV)
    T.op("pool", lambda e: e.memset(V2a[:], 1.0), writes=bKV)

    def load_layer(l):
        T.barrier()
        wb = [bWin, bWout, bWsT, bbrow, bWcs, bgbc, bWuq, bWukv, bmodAB]
        stg = rr(tb16[:, 0:512], "p (g q) -> p g q", g=4)
        for g in range(4):
            T.dma("pool", "w", stg[:, g, :], cmw_d[l, g, :, :], writes=[btb16])
        T.dma("pool", "w", brow[:], cmb_d[l:l + 1, :], writes=[bbrow])
        wfs = rr(fT[:, :, 0:256], "p c n -> p c n")
        T.dma("pool", "w", wfs, rr(wf_d[l], "(c p) n -> p c n", p=128), writes=[bfT])
        T.dma("sp", "w", Ecs, rr(ecs_d, "(c p) n -> p c n", p=128), writes=[bEcs])
        T.dma("sp", "w", tmA[:, 0:384], wuq_d[l, 0:128, :], writes=[btmA])
        T.dma("sp", "w", tmB[0:64, 0:384], wuq_d[l, 128:192, :], writes=[btmB])
        T.dma("sp", "w", tmC[:, 0:512], wukv_d[l, :, :], writes=[btmC])
        smt = sm[0]
        T.dma("sp", "w", smt[:, 0:1], AP(qnorm_d.tensor, l * 192, [[1, 128], [1, 1]]), writes=[bsm[0]])
        T.dma("sp", "w", smt[0:64, 1:2], AP(qnorm_d.tensor, l * 192 + 128, [[1, 64], [1, 1]]), writes=[bsm[0]])
        T.dma("sp", "w", smt[:, 2:3], AP(kvnorm_d.tensor, l * 128, [[1, 128], [1, 1]]), writes=[bsm[0]])
        for (off, src, n) in ((GQN, mqn_d, 96), (GKN, mkn_d, 96), (GQ2, gqn_d, 64), (GK2, gkn_d, 64),
                              (LNG, lng_d, 256), (LNB, lnb_d, 256)):
            T.dma("sp", "w", gbc[:, off:off + n], AP(src.tensor, l * n, [[0, 128], [1, n]]), writes=[bgbc])
        nrow = 8 * (NB + 2)
        T.dma("sp", "w", tmD[0:8, 0:128], AP(norm_g_d.tensor, l * D, [[128, 8], [1, 128]]), writes=[btmD])
        for v in range(NB + 1):
            T.dma("sp", "w", tmD[8 + 8 * v:16 + 8 * v, 0:128], AP(c_d.tensor, v * D, [[128, 8], [1, 128]]), writes=[btmD])
        cT = rr(tmD[:, 256:256 + 8 * (NB + 1)], "p (c v) -> p c v", c=8)
        NV = NB + 1
        for j_ in range(3):
            T.dma("sp", "w", res[0][32 * j_:32 * j_ + NV, 0:1024],
                  AP(b_mod_d.tensor, l * 3 * D + 1024 * j_, [[0, NV], [1, 1024]]), writes=[bres[0]])
        wtok = ("w_sp", T.dcnt["w_sp"])
        for b in [bgbc, bmodAB, btmA, btmB, btmC, bsm[0], btmD, bres[0], bEcs]:
            b.w = wtok
        wtokp = ("w_pool", T.dcnt["w_pool"])
        for b in [btb16, bbrow, bfT]:
            b.w = wtokp
        pa = nxt("pa")
        T.op("pe", lambda e, pa=pa: e.transpose(pA[pa][:, 0:nrow], tmD[0:nrow, 0:128], identF[0:nrow, 0:nrow]),
             reads=[btmD, bidentF], writes=[bpA[pa]])
        cp("dve", ngcol[:], pA[pa][:, 0:8], [bpA[pa]], [bmodAB])
        cp("dve", cT, rr(pA[pa][:, 8:nrow], "p (v c) -> p c v", c=8), [bpA[pa]], [btmD])
        ts("dve", gbc[:, GQN:GQN + 96], gbc[:, GQN:GQN + 96], 96 ** -0.5, None, ALU.mult, None, [bgbc], [bgbc])
        ts("dve", gbc[:, GQ2:GQ2 + 64], gbc[:, GQ2:GQ2 + 64], 64 ** -0.5, None, ALU.mult, None, [bgbc], [bgbc])
        ts("dve", Wuq[:, 0, :], tmA[:, 0:384], smt[:, 0:1], None, ALU.mult, None, [btmA, bsm[0]], [bWuq])
        ts("dve", Wuq[0:64, 1, :], tmB[0:64, 0:384], smt[0:64, 1:2], None, ALU.mult, None, [btmB, bsm[0]], [bWuq])
        ts("dve", Wukv[:, :], tmC[:, 0:512], smt[:, 2:3], None, ALU.mult, None, [btmC, bsm[0]], [bWukv])
        for g in range(4):
            tr(pT2[:, g * 128:(g + 1) * 128], stg[:, g, :], [btb16], [bpT2])
        cp("dve", WsT[:, :, :], rr(pT2[:, 0:512], "p (g q) -> p g q", g=4), [bpT2], [bWsT])
        for cc in range(2):
            for part in range(2):
                pa = nxt("pa")
                for kc in range(2):
                    mm(pA[pa][:, 0:256], Ecs[:, kc, part * 256 + cc * 128: part * 256 + (cc + 1) * 128],
                       wfs[:, kc, :], kc == 0, kc == 1, [bEcs, bfT], [bpA[pa]])
                cp("dve", Wcs[:, cc, part * 256:(part + 1) * 256], pA[pa][:, 0:256], [bpA[pa]], [bWcs])
        scT = rr(tb16[:, 0:8 * (NB + 1)], "p (c v) -> p c v", c=8)
        act(scT, cT, AF.Silu, [btmD, bpT2], [btb16])
        stg_t = [(xt[0], bxt[0]), (xt[1], bxt[1]), (gtbc[:, 0, :], bgt[0])]
        cast_e = ["act", "act", "act"]
        pieces = []
        for c in range(8):
            for (c0, c1) in ((0, 1024), (1024, 2048), (2048, INW)):
                pieces.append((Win[:, c, c0:c1], w_in_d[l, c * 128:(c + 1) * 128, c0:c1], c1 - c0, bWin))
        for c in range(8):
            pieces.append((Wout[:, c, :], wout_d[l, c * 128:(c + 1) * 128, :], D, bWout))
        for pi, (dst, src, wdt, wbuf) in enumerate(pieces):
            st_, bst_ = stg_t[pi % 3]
            T.dma("sp", "sg%d" % (pi % 3), st_[:, 0:wdt], src, writes=[bst_])
            cp(cast_e[pi % 3], dst, st_[:, 0:wdt], [bst_], [wbuf])
        wm = [rr(R[:, i * 4096:(i + 1) * 4096], "p (c n) -> p c n", c=8) for i in range(2)]
        bwm = [Buf("wm0"), Buf("wm1")]
        for nb_ in range(6):
            k = nb_ % 2
            T.dma("pool", "wm%d" % k, wm[k], rr(w_mod_d[l][:, nb_ * 512:(nb_ + 1) * 512], "(c p) n -> p c n", p=128),
                  writes=[bwm[k]])
            pa = nxt("pa")
            for kc in range(8):
                mm(pA[pa][0:NV, :], scT[:, kc, 0:NV], wm[k][:, kc, :], kc == 0, kc == 7,
                   [btb16, bwm[k]], [bpA[pa]], inc=(kc == 7))
            sl = nxt("sm")
            mrow = tmpo if sl == 0 else rinv
            bmrow = btmpo if sl == 0 else brinv
            tt("dve", mrow[0:NV, :], pA[pa][0:NV, :],
               res[0][32 * (nb_ // 2):32 * (nb_ // 2) + NV, (nb_ % 2) * 512:(nb_ % 2 + 1) * 512], ALU.add,
               [bpA[pa], bres[0]], [bmrow])
            T.dma("pool", "mo%d" % sl, mods_d[0:NV, nb_ * 512:(nb_ + 1) * 512], mrow[0:NV, :], reads=[bmrow])
        T.barrier()
        nr2 = 16 * (NB + 1)
        for v in range(NB + 1):
            T.dma("sp", "mi", tmD[16 * v:16 * v + 16, 0:128], AP(mods_d.tensor, v * 3 * D, [[128, 16], [1, 128]]), writes=[btmD])
        btmD.w = ("mi_sp", T.dcnt["mi_sp"])
        pa = nxt("pa")
        T.op("pe", lambda e, pa=pa: e.transpose(pA[pa][:, 0:nr2], tmD[0:nr2, 0:128], identF[0:nr2, 0:nr2]),
             reads=[btmD, bidentF], writes=[bpA[pa]])
        cp("dve", modAB[:, :, 1:3, :], rr(pA[pa][:, 0:nr2], "p (v j c) -> p v j c", j=2, c=8), [bpA[pa]], [bmodAB])
        for v in range(NB + 1):
            ts("dve", modAB[:, v, 2, :], modAB[:, v, 2, :], 1.0, None, ALU.add, None, [bmodAB], [bmodAB])
            tt("dve", modAB[:, v, 0, :], modAB[:, v, 2, :], ngcol[:], ALU.mult, [bmodAB], [bmodAB])
        T.barrier()

    def norm_load(src_ap, src_buf):
        k = nxt("xt")
        T.dma("sp", "xt%d" % k, xt[k][:], src_ap, reads=[src_buf], writes=[bxt[k]])
        return k

    def norm_tile_g(src_ap, src_buf, v, slot, k=None, after_xn=None, tb=None, btb=None):
        pT, bpT = (pT_, bpT_) if tb is None else (tb, btb)
        if k is None:
            k = norm_load(src_ap, src_buf)
        s = st[k]
        act(junk[:], xt[k][:], AF.Square, [bxt[k]], [bjunk, bst[k]], accum_out=s[:, 0:1])
        rsqrt_cols(s[:, 1:2], s[:, 0:1], 1.0 / D, bst[k], [])
        yield
        ts("dve", xn[k][:], xt[k][:], s[:, 1:2], None, ALU.mult, None, [bxt[k], bst[k]], [bxn[k]])
        if after_xn is not None:
            after_xn()
        for c in range(8):
            tr(pT[:, c * 128:(c + 1) * 128], xn[k][:, c * 128:(c + 1) * 128], [bxn[k]], [bpT])
        yield
        for c in range(8):
            o_ = hxT[:, c, slot * 128:(slot + 1) * 128]
            i_ = pT[:, c * 128:(c + 1) * 128]
            if c % 2 == 0:
                ts("dve", o_, i_, modAB[:, v, 0, c:c + 1], modAB[:, v, 1, c:c + 1], ALU.mult, ALU.add,
                   [bpT, bmodAB], [bhx[slot], bhxd[slot]])
            else:
                act(o_, i_, AF.Identity, [bpT, bmodAB] + ([bhxd[slot]] if c == 7 else []), [bhx[slot]],
                    scale=modAB[:, v, 0, c:c + 1], bias=modAB[:, v, 1, c:c + 1])
                yield

    def norm_tile(src_ap, src_buf, v, slot):
        for _ in norm_tile_g(src_ap, src_buf, v, slot):
            pass

    def proj_tm(pbank, pbuf, col0, slot, w0, w1):
        n = w1 - w0
        for c in range(8):
            mm(pbank[:, col0:col0 + n], hxT[:, c, slot * 128:(slot + 1) * 128], Win[:, c, w0:w1],
               c == 0, c == 7, [bhx[slot], bWin], [pbuf], inc=(c == 7))

    def proj_fm(pbank, pbuf, w0, N, nslots):
        for c in range(8):
            mm(pbank[:, 0:N], Win[:, c, w0:w0 + 128], hxT[:, c, 0:N], c == 0, c == 7,
               [bWin] + bhx[0:nslots], [pbuf], inc=(c == 7))

    def kv_tile(slot, kt, is_ctx, ti, sc):
        tmA, btmA, tmB, btmB, tmC, btmC, tmD, btmD = sc.tmA, sc.btmA, sc.tmB, sc.btmB, sc.tmC, sc.btmC, sc.tmD, sc.btmD
        tb16, btb16, tT, btT = sc.tb16, sc.btb16, sc.tT, sc.btT
        pT2, bpT2 = sc.pT, sc.bpT
        P_, bP = sc.P, sc.bP
        proj_tm(P_, bP, 0, slot, C_CKV, C_KR + 32)
        proj_tm(P_, bP, 160, slot, C_K2, C_V2 + 128)
        yield
        s, bs = sc.nsm()
        act(junk[:, 0:128], P_[:, 0:128], AF.Square, [bP], [bjunk, bs], accum_out=s[:, 0:1])
        rsqrt_cols(s[:, 1:2], s[:, 0:1], 1.0 / 128, bs, [])
        ts("dve", tb16[:, 0:128], P_[:, 0:128], s[:, 1:2], None, ALU.mult, None, [bP, bs], [btb16])
        tr(pT2[:, 0:128], tb16[:, 0:128], [btb16], [bpT2])
        cp("dve", tT[:, 0:128], pT2[:, 0:128], [bpT2], [btT])
        yield
        KV, bKVp = sc.Q, sc.bQ
        mm(KV[:, 0:512], tT[:, 0:128], Wukv[:, :], True, True, [btT, bWukv], [bKVp])
        KV3 = rr(KV[:, 0:512], "p (h d) -> p h d", h=4)
        yield
        V1k = V1a[:, kt, :]
        cp("act", rr(V1k[:, 0:384], "p (a n) -> p a n", a=2)[:, :, 0:64], KV3[:, 0:4:2, 64:128], [bKVp], [bKV[kt]])
        cp("act", rr(V1k[:, 0:384], "p (a n) -> p a n", a=2)[:, :, 128:192], KV3[:, 1:4:2, 64:128], [bKVp], [bKV[kt]])
        sq = rr(tmA[:, 0:256], "p (h d) -> p h d", h=4)
        act(sq, KV3[:, :, 0:64], AF.Square, [bKVp], [btmA])
        T.op("dve", lambda e: e.tensor_reduce(out=s[:, 4:8], in_=sq, axis=AX.X, op=ALU.add), reads=[btmA], writes=[bs])
        act(junk[:, 128:160], P_[:, 128:160], AF.Square, [bP], [bjunk, bs], accum_out=s[:, 2:3])
        ts("dve", s[:, 8:12], s[:, 4:8], s[:, 2:3], None, ALU.add, None, [bs], [bs])
        rsqrt_cols(s[:, 12:16], s[:, 8:12], 1.0 / 96, bs, [])
        yield
        k1tm = rr(tb16[:, 128:512], "p (h d) -> p h d", h=4)
        tmp1 = rr(tmB[:, 0:256], "p (h d) -> p h d", h=4)
        tt("dve", tmp1, KV3[:, :, 0:64], bc(gbc[:, GKN:GKN + 64].unsqueeze(1), [128, 4, 64]), ALU.mult,
           [bKVp, bgbc], [btmB])
        yield
        tt("pool", k1tm[:, :, 0:64], tmp1, bc(s[:, 12:16].unsqueeze(2), [128, 4, 64]), ALU.mult,
           [btmB, bs], [btb16])
        krg = rr(tmC[:, 0:32], "p (h d) -> p h d", h=1)
        tt("dve", krg, rr(P_[:, 128:160], "p (h d) -> p h d", h=1), gbc[:, GKN + 64:GKN + 96].unsqueeze(1), ALU.mult,
           [bP, bgbc], [btmC])
        if not is_ctx:
            krr = rr(tmC[:, 32:64], "p (h d) -> p h d", h=1)
            rope(krg, [(krr, slice(0, 1))], ropeA, ti, 1, 32, btmC, [btmC], tmD, btmD)
        else:
            krr = krg
        tt("pool", k1tm[:, :, 64:96], bc(krr, [128, 4, 32]), bc(s[:, 12:16].unsqueeze(2), [128, 4, 32]), ALU.mult,
           [btmC, bs], [btb16])
        for h in range(4):
            tr(pT2[0:96, 128 + h * 128:256 + h * 128], k1tm[:, h, :], [btb16], [bpT2])
        cp("dve", K1T[0:96, :, kt * 128:(kt + 1) * 128], rr(pT2[0:96, 128:640], "p (h n) -> p h n", h=4),
           [bpT2], [bKV[kt]])
        yield
        k2v = rr(P_[:, 160:288], "p (h d) -> p h d", h=2)
        sq2 = rr(tmA[:, 256:384], "p (h d) -> p h d", h=2)
        act(sq2, k2v, AF.Square, [bP], [btmA])
        s2, bs2 = sc.nsm()
        T.op("dve", lambda e: e.tensor_reduce(out=s2[:, 0:2], in_=sq2, axis=AX.X, op=ALU.add), reads=[btmA], writes=[bs2])
        rsqrt_cols(s2[:, 2:4], s2[:, 0:2], 1.0 / 64, bs2, [])
        yield
        tmp2 = rr(tmB[:, 256:384], "p (h d) -> p h d", h=2)
        tt("dve", tmp2, k2v, bc(gbc[:, GK2:GK2 + 64].unsqueeze(1), [128, 2, 64]), ALU.mult, [bP, bgbc], [btmB])
        k2r = rr(tT[:, 128:256], "p (h d) -> p h d", h=2)
        if not is_ctx:
            k2n = rr(tmC[:, 128:256], "p (h d) -> p h d", h=2)
            tt("pool", k2n, tmp2, bc(s2[:, 2:4].unsqueeze(2), [128, 2, 64]), ALU.mult, [btmB, bs2], [btmC])
            rope(k2n, [(k2r, slice(0, 2))], ropeB, ti, 2, 64, btmC, [btT], tmD, btmD)
        else:
            tt("pool", k2r, tmp2, bc(s2[:, 2:4].unsqueeze(2), [128, 2, 64]), ALU.mult, [btmB, bs2], [btT])
        tr(pT2[:, 640:768], tT[:, 128:256], [btT], [bpT2])
        yield
        cp("act", K2T[:, kt * 128:(kt + 1) * 128], pT2[:, 640:768], [bpT2], [bKV[kt]])
        cp("act", rr(V2a[:, kt, 64:320], "p (a n) -> p a n", a=2)[:, :, 0:64],
           rr(P_[:, 288:416], "p (h d) -> p h d", h=2), [bP], [bKV[kt]])

    def g_tile(slot, Gdst, Gbuf):
        pa = nxt("pa")
        for kc in range(2):
            mm(pA[pa][:, 0:512], fT[:, kc, slot * 128:(slot + 1) * 128], Wcs[:, kc, :], kc == 0, kc == 1,
               [bfT, bWcs], [bpA[pa]])
        cp("act", Gdst, pA[pa][:, 0:512], [bpA[pa]], [Gbuf])

    def run_gens(items):
        pending = list(items)
        active = []
        free_sets = [SCS[0], SCS[1]]
        while pending or active:
            k = 0
            while k < len(pending):
                needs, f = pending[k][0], pending[k][1]
                wgt = pending[k][2] if len(pending[k]) > 2 else 1
                rdy = pending[k][3] if len(pending[k]) > 3 else None
                if rdy is not None and not rdy():
                    k += 1
                    continue
                if needs:
                    if free_sets:
                        sc = free_sets.pop(0)
                        active.append((f(sc), sc, wgt))
                        pending.pop(k)
                        continue
                    k += 1
                else:
                    active.append((f(), None, wgt))
                    pending.pop(k)
                    continue
            for it in list(active):
                try:
                    for _ in range(it[2]):
                        next(it[0])
                except StopIteration:
                    active.remove(it)
                    if it[1] is not None:
                        free_sets.append(it[1])

    def phase1_block(src_tiles, v, is_ctx, kt0, ti0, yd_dst, yd_bufs, Gv, Gb, hxcol):
        n = len(src_tiles)
        N = n * 128
        pTn = pO[1][:, :].bitcast(BF16)
        done = [0]

        def norm_all():
            ks = {}
            for i in range(min(2, n)):
                ks[i] = norm_load(*src_tiles[i])
            for i, (sap, sbuf_) in enumerate(src_tiles):
                def pre(i=i):
                    if i + 2 < n:
                        ks[i + 2] = norm_load(*src_tiles[i + 2])
                for _ in norm_tile_g(sap, sbuf_, v, i, k=ks[i], after_xn=pre, tb=pTn, btb=bpO[1]):
                    yield
                done[0] = i + 1
            hb = bHX.setdefault(hxcol, Buf("HX%d" % hxcol))
            T.dma("sp", "hs", hxs_d[:, :, hxcol:hxcol + N], hxT[:, :, 0:N], reads=[bhx[i_] for i_ in range(n)],
                  writes=[hb])

        def fm_gen():
            for oc in range(2):
                pb = 0
                proj_fm(pO[pb], bpO[pb], C_F + oc * 128, N, n)
                cp("dve", fT[:, oc, 0:N], pO[pb][:, 0:N], [bpO[pb]], [bfT])
                yield
            for oc in range(2):
                pb = 0
                proj_fm(pO[pb], bpO[pb], C_GD + oc * 128, N, n)
                act(yd_dst[:, oc, :], pO[pb][:, 0:N], AF.Silu, [bpO[pb]], yd_bufs)
                yield
        items = [(False, norm_all)]
        items += [(True, (lambda sc, i=i: kv_tile(i, kt0 + i, is_ctx, ti0 + i, sc)), 2, (lambda i=i: done[0] > i))
                  for i in range(n)]
        items.append((False, fm_gen, 1, (lambda: done[0] >= n)))
        run_gens(items)
        for i in range(n):
            g_tile(i, Gv[:, ti0 + i if not is_ctx else i, :], Gb[ti0 + i if not is_ctx else i])

    def fourier(tab_d, nkb, kb_n, na, Gv, Gb, ydv, ydbufs):
        ag = min(AG, na)
        for kb in range(nkb):
            P0, P1 = pO[0], pO[1]
            for a0 in range(0, na, ag):
                k = nxt("tab")
                tv = tabv[k] if kb_n == KB else rr(R[:, NT * 512 + k * AG * 2 * KB: NT * 512 + k * AG * 2 * KB + ag * 2 * kb_n],
                                                   "p (a j n) -> p a j n", a=ag, j=2)
                T.dma("sp", "tb%d" % k, tv[:, 0:ag, :, :], tab_d[kb, :, a0:a0 + ag, :, :], writes=[btab[k]])
                for a in range(a0, a0 + ag):
                    for cc, (PP, bPP) in enumerate(((P0, bpO[0]), (P1, bpO[1]))):
                        mm(PP[:, 0:kb_n], Gv[:, a, cc * 128:(cc + 1) * 128], tv[:, a - a0, 0, :],
                           a == 0, False, [Gb[a], btab[k]], [bPP], inc=False)
                        mm(PP[:, 0:kb_n], Gv[:, a, 256 + cc * 128:256 + (cc + 1) * 128], tv[:, a - a0, 1, :],
                           False, a == na - 1, [Gb[a], btab[k]], [bPP])
            for cc in range(2):
                tt("dve", ydv[:, cc, kb * kb_n:(kb + 1) * kb_n], pO[cc][:, 0:kb_n],
                   ydv[:, cc, kb * kb_n:(kb + 1) * kb_n], ALU.mult, [bpO[cc], ydbufs[kb]], [ydbufs[kb]])

    def fourier_half(Gv, Gb, ydv, ydbufs):
        Hh = S // 2
        KBh = min(512, Hh)
        nkb = Hh // KBh
        ag = min(AG, NT)
        acc = [(pO[0], bpO[0]), (pO[1], bpO[1]), (pA[0], bpA[0]), (pA[1], bpA[1])]
        for kb in range(nkb):
            for a0 in range(0, NT, ag):
                k = nxt("tab")
                base = NT * 512 + k * AG * 2 * KB
                tv = rr(R[:, base: base + ag * 2 * KBh], "p (a j n) -> p a j n", a=ag, j=2)
                T.dma("sp", "tb%d" % k, tv[:, 0:ag, :, :], tabH_d[kb, :, a0:a0 + ag, :, :], writes=[btab[k]])
                for a in range(a0, a0 + ag):
                    for part in range(2):
                        for cc in range(2):
                            PP, bPP = acc[part * 2 + cc]
                            mm(PP[:, 0:KBh], Gv[:, a, part * 256 + cc * 128: part * 256 + (cc + 1) * 128],
                               tv[:, a - a0, part, :], a == 0, a == NT - 1, [Gb[a], btab[k]], [bPP])
            k0 = kb * KBh
            for cc in range(2):
                sc = SCS[cc]
                Pc, bPc = acc[cc]
                Ps, bPs = acc[2 + cc]
                cp("act", sc.tmA[:, 0:KBh], Ps[:, 0:KBh], [bPs], [sc.btmA])
                tt("dve", sc.tmB[:, 0:KBh], Pc[:, 0:KBh], sc.tmA[:, 0:KBh], ALU.subtract, [bPc, sc.btmA], [sc.btmB])
                tt("dve", sc.tmC[:, 0:KBh], Pc[:, 0:KBh], sc.tmA[:, 0:KBh], ALU.add, [bPc, sc.btmA], [sc.btmC])
                fwd = ydv[:, cc, k0:k0 + KBh]
                tt("pool", fwd, sc.tmB[:, 0:KBh], fwd, ALU.mult, [sc.btmB] + list(ydbufs), list(ydbufs))
                j0 = 1 if kb == 0 else 0
                cnt = KBh - j0
                hi = S - k0 - j0
                rev = ydv[:, cc, hi:hi - cnt:-1]
                tt("dve", rev, sc.tmC[:, j0:KBh], rev, ALU.mult, [sc.btmC] + list(ydbufs), list(ydbufs))
        for cc in range(2):
            for a in range(NT):
                mm(pB[0][:, cc:cc + 1], Gv[:, a, cc * 128:(cc + 1) * 128], altc[:, 0:1], a == 0, a == NT - 1,
                   [Gb[a], baltc], [bpB[0]])
        for cc in range(2):
            tt("dve", ydv[:, cc, Hh:Hh + 1], pB[0][:, cc:cc + 1], ydv[:, cc, Hh:Hh + 1], ALU.mult,
               [bpB[0]] + list(ydbufs), list(ydbufs))

    def make_block(src_tiles, dst_tiles, v, is_ctx, ti0, key_tiles, ydv, ydoff, ydbufs, yset, hxcol):
        n = len(src_tiles)
        N = n * 128

        def N1():
            T.dma("sp", "hl", hxT[:, :, 0:N], hxs_d[:, :, hxcol:hxcol + N], reads=[bHX[hxcol]],
                  writes=[bhx[i_] for i_ in range(n)])
            yield
            for oc in range(2):
                pb = 1
                proj_fm(pB[pb], bpB[pb], C_U + oc * 128, N, n)
                cp("dve", ugT[:, oc, 0:N], pB[pb][:, 0:N], [bpB[pb]], [bugT])
                yield
                pb = 1
                proj_fm(pB[pb], bpB[pb], C_GC + oc * 128, N, n)
                act(sgc[:, 0:N], pB[pb][:, 0:N], AF.Silu, [bpB[pb]], [bsgc])
                tt("pool", ugT[:, oc, 0:N], ugT[:, oc, 0:N], sgc[:, 0:N], ALU.mult, [bugT, bsgc], [bugT])
                yield
        def gates_ab():
            for j, (w0, dstv, dbuf) in enumerate(((C_GA, gT[:, 0], bgT), (C_GA + 128, gT[:, 1], bgT),
                                                  (C_GB, gT[:, 2], bgT), (C_GB + 128, gT[:, 3], bgT))):
                pb = nxt("po")
                proj_fm(pO[pb], bpO[pb], w0, N, n)
                act(dstv[:, 0:N], pO[pb][:, 0:N], AF.Silu, [bpO[pb]], [dbuf])
                yield
        def q_tile(i, sc):
            ti = ti0 + i
            tmA, btmA, tmB, btmB, tmC, btmC, tmD, btmD = sc.tmA, sc.btmA, sc.tmB, sc.btmB, sc.tmC, sc.btmC, sc.tmD, sc.btmD
            tb16, btb16, tT, btT = sc.tb16, sc.btb16, sc.tT, sc.btT
            pT2, bpT2 = sc.pT, sc.bpT
            P_, bP = sc.P, sc.bP
            proj_tm(P_, bP, 0, i, C_CQ, C_CQ + 192)
            proj_tm(P_, bP, 192, i, C_Q2, C_Q2 + 256)
            yield
            s, bs = sc.nsm()
            act(junk[:, 0:192], P_[:, 0:192], AF.Square, [bP], [bjunk, bs], accum_out=s[:, 0:1])
            rsqrt_cols(s[:, 1:2], s[:, 0:1], 1.0 / 192, bs, [])
            ts("dve", tb16[:, 0:192], P_[:, 0:192], s[:, 1:2], None, ALU.mult, None, [bP, bs], [btb16])
            tr(pT2[:, 0:128], tb16[:, 0:128], [btb16], [bpT2])
            tr(pT2[0:64, 128:256], tb16[:, 128:192], [btb16], [bpT2])
            cp("dve", tT[:, 0:128], pT2[:, 0:128], [bpT2], [btT])
            cp("dve", tT[0:64, 128:256], pT2[0:64, 128:256], [bpT2], [btT])
            yield
            Q, bQ = sc.Q, sc.bQ
            mm(Q[:, 0:384], tT[:, 0:128], Wuq[:, 0, :], True, False, [btT, bWuq], [bQ], inc=False)
            mm(Q[:, 0:384], tT[0:64, 128:256], Wuq[0:64, 1, :], False, True, [btT, bWuq], [bQ])
            Q3 = rr(Q[:, 0:384], "p (h d) -> p h d", h=4)
            yield
            sq = rr(tmA[:, 0:384], "p (h d) -> p h d", h=4)
            act(sq, Q3, AF.Square, [bQ], [btmA])
            T.op("dve", lambda e, s=s, sq=sq: e.tensor_reduce(out=s[:, 4:8], in_=sq, axis=AX.X, op=ALU.add),
                 reads=[btmA], writes=[bs])
            rsqrt_cols(s[:, 8:12], s[:, 4:8], 1.0 / 96, bs, [])
            yield
            tmpq = rr(tmB[:, 0:384], "p (h d) -> p h d", h=4)
            tt("dve", tmpq, Q3, bc(gbc[:, GQN:GQN + 96].unsqueeze(1), [128, 4, 96]), ALU.mult, [bQ, bgbc], [btmB])
            yield
            q1tm = rr(tb16[:, 0:384], "p (h d) -> p h d", h=4)
            tt("pool", q1tm[:, :, 0:64], tmpq[:, :, 0:64], bc(s[:, 8:12].unsqueeze(2), [128, 4, 64]), ALU.mult,
               [btmB, bs], [btb16])
            if not is_ctx:
                qr = rr(tmC[:, 0:128], "p (h d) -> p h d", h=4)
                tt("pool", qr, tmpq[:, :, 64:96], bc(s[:, 8:12].unsqueeze(2), [128, 4, 32]), ALU.mult,
                   [btmB, bs], [btmC])
                rope(qr, [(q1tm[:, :, 64:96], slice(0, 4))], ropeA, ti, 4, 32, btmC, [btb16], tmD, btmD)
            else:
                tt("pool", q1tm[:, :, 64:96], tmpq[:, :, 64:96], bc(s[:, 8:12].unsqueeze(2), [128, 4, 32]), ALU.mult,
                   [btmB, bs], [btb16])
            for h in range(4):
                tr(pT2[0:96, 256 + h * 128:384 + h * 128], q1tm[:, h, :], [btb16], [bpT2])
            cp("act", q1T[0:96, :, i * 128:(i + 1) * 128], rr(pT2[0:96, 256:768], "p (h n) -> p h n", h=4),
               [bpT2], [bq1T])
            yield
            q2v = rr(P_[:, 192:448], "p (h d) -> p h d", h=4)
            sq2 = rr(tmA[:, 0:256], "p (h d) -> p h d", h=4)
            act(sq2, q2v, AF.Square, [bP], [btmA])
            s2, bs2 = sc.nsm()
            T.op("dve", lambda e, s2=s2, sq2=sq2: e.tensor_reduce(out=s2[:, 0:4], in_=sq2, axis=AX.X, op=ALU.add),
                 reads=[btmA], writes=[bs2])
            rsqrt_cols(s2[:, 4:8], s2[:, 0:4], 1.0 / 64, bs2, [])
            yield
            tmp2 = rr(tmB[:, 0:256], "p (h d) -> p h d", h=4)
            tt("dve", tmp2, q2v, bc(gbc[:, GQ2:GQ2 + 64].unsqueeze(1), [128, 4, 64]), ALU.mult, [bP, bgbc], [btmB])
            yield
            q2r = rr(tb16[:, 0:256], "p (h d) -> p h d", h=4)
            pieces = [(q2r[:, 0:2, :], slice(0, 4, 2)), (q2r[:, 2:4, :], slice(1, 4, 2))]
            if not is_ctx:
                q2n = rr(tmC[:, 0:256], "p (h d) -> p h d", h=4)
                tt("pool", q2n, tmp2, bc(s2[:, 4:8].unsqueeze(2), [128, 4, 64]), ALU.mult, [btmB, bs2], [btmC])
                rope(q2n, pieces, ropeB, ti, 4, 64, btmC, [btb16], tmD, btmD)
            else:
                for (ov, hs) in pieces:
                    tt("pool", ov, tmp2[:, hs, :], bc(s2[:, 4:8].unsqueeze(2), [128, 4, 64])[:, hs, :], ALU.mult,
                       [btmB, bs2], [btb16])
            for j in range(2):
                tr(pT2[:, 768 + j * 128:896 + j * 128], tb16[:, j * 128:(j + 1) * 128], [btb16], [bpT2])
            cp("act", q2T[:, :, i * 128:(i + 1) * 128], rr(pT2[:, 768:1024], "p (h n) -> p h n", h=2),
               [bpT2], [bq2T])
            yield
            PV, bPV = sc.P, sc.bP
            proj_tm(PV, bPV, 0, i, C_VC, C_VC + 256)
            yield
            T.op("dve", lambda e, s2=s2, PV=PV: e.bn_stats(out=s2[:, 8:14], in_=PV[:, 0:256]), reads=[bPV], writes=[bs2])
            T.op("dve", lambda e, s2=s2: e.bn_aggr(out=s2[:, 14:16], in_=s2[:, 8:14]), reads=[bs2], writes=[bs2])
            rsqrt_cols(s2[:, 16:17], s2[:, 15:16], 1.0, bs2, [])
            yield
            ts("dve", tmA[:, 0:256], PV[:, 0:256], s2[:, 14:15], s2[:, 16:17], ALU.subtract, ALU.mult,
               [bPV, bs2], [btmA])
            tt("pool", tmA[:, 256:512], tmA[:, 0:256], gbc[:, LNG:LNG + 256], ALU.mult, [btmA, bgbc], [btmA])
            tt("pool", tb16[:, 256:512], tmA[:, 256:512], gbc[:, LNB:LNB + 256], ALU.add, [btmA, bgbc], [btb16])
            yield
            PS, bPS = sc.Q, sc.bQ
            for g in range(4):
                mm(PS[:, g * 128:(g + 1) * 128], tb16[:, 256 + (g // 2) * 128:384 + (g // 2) * 128], WsT[:, g, :],
                   True, False, [btb16, bWsT], [bPS], inc=False)
                mm(PS[:, g * 128:(g + 1) * 128], ones_r[0:1, 0:128], brow[0:1, g * 128:(g + 1) * 128],
                   False, True, [bones, bbrow], [bPS], inc=(g == 3))
            for r in range(2):
                r0 = r * 64
                in0 = rr(PS[r0:r0 + 64, :], "p (c g n) -> p c g n", c=2, g=2)[:, :, r, :]
                tt("dve", ycT[r0:r0 + 64, yset, :, i * 128:(i + 1) * 128], in0,
                   ugT[r0:r0 + 64, :, i * 128:(i + 1) * 128], ALU.mult, [bPS, bugT], [byc[yset]])
        def N2_items():
            items = [(True, (lambda sc, i=i: q_tile(i, sc)), 2) for i in range(n)]
            items.insert(min(2, n), (False, gates_ab))
            return items

        v1slot = (0, 64, 192, 256)
        v2slot = (64, 0, 192, 128)
        specs = []
        for h in range(4):
            specs.append((h, (lambda kt, h=h: K1T[0:96, h, kt * 128:(kt + 1) * 128]), q1T[0:96, h, 0:N],
                          V1a, v1slot[h], h // 2, h // 2))
        for h in range(4):
            base = (h // 2) * 64
            specs.append((h, (lambda kt, base=base: K2T[base:base + 64, kt * 128:(kt + 1) * 128]),
                          q2T[base:base + 64, h % 2, 0:N], V2a, v2slot[h], 2 + h // 2, 2 + h // 2))
        its = [(sp, idx, kt) for sp in specs for idx, kt in enumerate(key_tiles)]
        nk = len(key_tiles)
        sbank = {}

        def issue_S(j):
            (h, KTv, qv, Vv, slot0, gate_c, out_c), idx, kt = its[j]
            sbank[j] = SRING[j % 3]
            Sx, bS = sbank[j]
            mm(Sx[:, 0:N], KTv(kt), qv, True, True, [bKV[kt], bq1T, bq2T], [bS])

        def issue_rest(j):
            (h, KTv, qv, Vv, slot0, gate_c, out_c), idx, kt = its[j]
            Sx, bS = sbank.pop(j)
            po = h % 2
            O, bO = pO[po], bpO[po]
            pr = nxt("pr", 4)
            act(probs[pr][:, 0:N], Sx[:, 0:N], AF.Exp, [bS], [bprobs[pr]])
            mm(O[:, 0:N], Vv[:, kt, slot0:slot0 + 128], probs[pr][:, 0:N], idx == 0, idx == nk - 1,
               [bKV[kt], bprobs[pr]], [bO])
            if idx == nk - 1:
                nr = (h % 2) * 64
                sr = 64 - nr
                T.op("dve", lambda e: e.reciprocal(out=rinv[sr:sr + 64, 0:N], in_=O[sr:sr + 64, 0:N]),
                     reads=[bO], writes=[brinv])
                tt("dve", tmpo[nr:nr + 64, 0:N], O[nr:nr + 64, 0:N], rinv[sr:sr + 64, 0:N], ALU.mult,
                   [bO, brinv], [btmpo])
                tt("pool", ccT[nr:nr + 64, out_c, 0:N], tmpo[nr:nr + 64, 0:N], gT[nr:nr + 64, gate_c, 0:N], ALU.mult,
                   [btmpo, bgT], [bcc[out_c]])

        def A():
            issue_S(0)
            if len(its) > 1:
                issue_S(1)
            for j in range(len(its)):
                if j + 2 < len(its):
                    issue_S(j + 2)
                issue_rest(j)
                yield
        def W():
            for i in range(n):
                k = nxt("xt")
                sap, sbuf_ = src_tiles[i]
                T.dma("sp", "xt%d" % k, xt[k][:], sap, reads=[sbuf_], writes=[bxt[k]])
                rk = 0
                for half in range(2):
                    pb = nxt("po"); Wp, bW = pO[pb], bpO[pb]
                    for c in range(8):
                        if c < 4:
                            lhs = ccT[:, c, i * 128:(i + 1) * 128]; rd = [bcc[c]]
                        elif c < 6:
                            lhs = ycT[:, yset, c - 4, i * 128:(i + 1) * 128]; rd = [byc[yset]]
                        else:
                            lhs = ydv[:, c - 6, ydoff + i * 128: ydoff + (i + 1) * 128]; rd = list(ydbufs)
                        mm(Wp[:, 0:512], lhs, Wout[:, c, half * 512:(half + 1) * 512], c == 0, c == 7,
                           rd + [bWout], [bW], inc=(c == 7))
                    tt("dve", res[rk][:, half * 512:(half + 1) * 512], Wp[:, 0:512],
                       gtbc[:, 0, half * 512:(half + 1) * 512], ALU.mult, [bW, bgt[0]], [bres[rk]])
                    yield
                tt("pool", res[rk][:], res[rk][:], xt[k][:], ALU.add, [bres[rk], bxt[k]], [bres[rk]])
                dap, dbuf = dst_tiles[i]
                T.dma("sp", "rs%d" % rk, dap, res[rk][:], reads=[bres[rk]], writes=[dbuf])
                yield

        return N1, N2_items, A, W

    bXd = [[Buf("xd%d_%d" % (b, t)) for t in range(NT)] for b in range(NB)]
    bCd = [[Buf("cd%d_%d" % (b, t)) for t in range(NC)] for b in range(NB)]
    TPB = QB // 128
    for l in range(DEPTH):
        load_layer(l)
        upd = l < DEPTH - 1
        for b in range(NB):
            xs = x_d if l == 0 else out_d
            cs = ctx_d if l == 0 else ctxw_d
            xsrc = [(xs[b, t * 128:(t + 1) * 128, :], bXd[b][t]) for t in range(NT)]
            xdst = [(out_d[b, t * 128:(t + 1) * 128, :], bXd[b][t]) for t in range(NT)]
            csrc = [(cs[b, t * 128:(t + 1) * 128, :], bCd[b][t]) for t in range(NC)]
            cdst = [(ctxw_d[b, t * 128:(t + 1) * 128, :], bCd[b][t]) for t in range(NC)]
            T.dma("sp", "gt", gtbc[:, 0, :], AP(mods_d.tensor, b * 3 * D + 2 * D, [[0, 128], [1, D]]), writes=[bgt[0]])
            for qb in range(NQB):
                phase1_block(xsrc[qb * TPB:(qb + 1) * TPB], b, False, qb * TPB, qb * TPB,
                             ydT[:, :, qb * QB:(qb + 1) * QB], bydT, G, bG, qb * QB)
            phase1_block(csrc, NB, True, NT, 0, ydTc[:, :, :], [bydTc], Gc, bGc, S)
            if USE_HALF:
                fourier_half(G, bG, ydT, bydT)
            else:
                fourier(tabL_d, NKB, KB, NT, G, bG, ydT, bydT)
            if upd:
                fourier(tabC_d, 1, CTX, NC, Gc, bGc, ydTc, [bydTc])
            T.barrier()
            allk = list(range(NK))
            blocks = []
            for qb in range(NQB):
                blocks.append((make_block(xsrc[qb * TPB:(qb + 1) * TPB], xdst[qb * TPB:(qb + 1) * TPB], b, False,
                                          qb * TPB, allk, ydT, qb * QB, bydT, qb % 2, qb * QB), False, allk))
            if upd:
                blocks.append((make_block(csrc, cdst, NB, True, 0, list(range(NT, NK)), ydTc, 0, [bydTc], NQB % 2, S), True, list(range(NT, NK))))
            run_gens([(False, blocks[0][0][0])])
            run_gens(blocks[0][0][1]())
            for k in range(len(blocks)):
                (N1_, N2_, A_, W_) = blocks[k][0]
                nb_ = blocks[k + 1] if k + 1 < len(blocks) else None
                nA = 8 * len(blocks[k][2])
                wA = max(1, nA // 56)
                run_gens([(False, A_, wA)] + ([(False, nb_[0][0])] if nb_ else []))
                run_gens([(False, W_)] + (nb_[0][1]() if nb_ else []))
                if nb_ is not None and nb_[1]:
                    T.dma("sp", "gt", gtbc[:, 0, :], AP(mods_d.tensor, NB * 3 * D + 2 * D, [[0, 128], [1, D]]),
                          writes=[bgt[0]])
            T.barrier()
    T.barrier()
    T.emit()
    es.close()
    return nc


def _tables(S, CTX):
    bf = ml_dtypes.bfloat16
    NT, NC = S // 128, CTX // 128
    KB = min(512, S)
    ident = np.eye(128, dtype=np.float32).astype(bf)
    k = np.arange(64)
    ang = 2 * np.pi * np.outer(k, k) / 64.0
    Ec = np.kron(np.eye(4), np.cos(ang)); Es = np.kron(np.eye(4), np.sin(ang))
    ecs = np.concatenate([Ec, Es], 1).astype(np.float32).astype(bf)
    s = np.arange(S)
    row = (s // GRID_W).astype(np.float64); col = (s % GRID_W).astype(np.float64)

    def rt(d):
        inv = THETA ** (-np.arange(0, d, 2, dtype=np.float64) / d)
        outc, outs = [], []
        for pos in (row, col):
            a = pos[:, None] * inv[None, :]
            outc.append(np.concatenate([np.cos(a), np.cos(a)], -1))
            outs.append(np.concatenate([-np.sin(a), np.sin(a)], -1))
        return np.concatenate(outc, -1), np.concatenate(outs, -1)

    def lay(c, sn):
        t = np.stack([c, sn], 1)
        t = t.reshape(NT, 128, 2, -1).transpose(1, 0, 2, 3)
        return np.ascontiguousarray(t).astype(np.float32).astype(bf)

    ropeA = lay(*rt(16)); ropeB = lay(*rt(32))

    def dft(n, kb):
        ss = np.arange(n, dtype=np.int64)
        m = np.outer(ss, ss) % n
        a = 2 * np.pi * m / n
        sc = 1.0 / np.sqrt(n * 64.0)
        t = np.stack([np.cos(a) * sc, -np.sin(a) * sc], 0)
        nkb = n // kb
        t = t.reshape(2, n // 128, 128, nkb, kb).transpose(3, 2, 1, 0, 4)
        return np.ascontiguousarray(t).astype(np.float32).astype(bf)

    def dft_half(n):
        h = n // 2
        kb = min(512, h)
        ss = np.arange(n, dtype=np.int64)
        kk = np.arange(h, dtype=np.int64)
        a = 2 * np.pi * (np.outer(ss, kk) % n) / n
        sc = 1.0 / np.sqrt(n * 64.0)
        t = np.stack([np.cos(a) * sc, np.sin(a) * sc], 0)
        t = t.reshape(2, n // 128, 128, h // kb, kb).transpose(3, 2, 1, 0, 4)
        return np.ascontiguousarray(t).astype(np.float32).astype(bf)

    altc = (((-1.0) ** np.arange(128)) / np.sqrt(S * 64.0)).reshape(128, 1).astype(np.float32).astype(bf)
    d_ = dict(ident=ident, ecs=ecs, ropeA=ropeA, ropeB=ropeB, tabH=dft_half(S), altc=altc, tabC=dft(CTX, CTX))
    if not USE_HALF:
        d_["tabL"] = dft(S, KB)
    return d_


_CACHE = {}


def run(inputs, ncores):
    x = np.asarray(inputs["x"], np.float32)
    B, S, _ = x.shape
    ctx = np.asarray(inputs["ctx"], np.float32)
    CTX = ctx.shape[1]
    NB = B // ncores
    key = (NB, S, CTX)
    if key not in _CACHE:
        _CACHE[key] = (build(NB, S, CTX), _tables(S, CTX))
    nc, tabs = _CACHE[key]
    c = np.asarray(inputs["c"], np.float32)
    c_ctx = np.asarray(inputs["c_ctx"], np.float32)
    shared = {k: np.ascontiguousarray(np.asarray(inputs[k], np.float32)) for k in (
        "norm_g", "w_mod", "b_mod", "w_in", "mla_q_norm", "mla_w_uq", "mla_kv_norm", "mla_w_ukv",
        "mla_qn", "mla_kn", "gqa_qn", "gqa_kn", "cm_ln_g", "cm_ln_b", "cm_w_s", "fnet_w", "w_out")}
    shared["cm_b_s"] = np.ascontiguousarray(np.asarray(inputs["cm_b_s"], np.float32).reshape(DEPTH, 512))
    shared.update(tabs)
    in_maps = []
    for i in range(ncores):
        m = dict(shared)
        m["x"] = np.ascontiguousarray(x[i * NB:(i + 1) * NB])
        m["ctx"] = np.ascontiguousarray(ctx[i * NB:(i + 1) * NB])
        m["c"] = np.ascontiguousarray(np.concatenate([c[i * NB:(i + 1) * NB], c_ctx[None, :]], 0))
        in_maps.append(m)
    r = run_bass_kernel_spmd(nc, in_maps, core_ids=list(range(ncores)))
    return np.concatenate([np.asarray(r.results[i]["out"], np.float32) for i in range(ncores)], 0)


def kernel(**inputs):
    return run(inputs, NCORES)
```

```python
import contextlib
import numpy as np
import ml_dtypes
import concourse.bass as bass
import concourse.mybir as mybir
from concourse.bass_types import AP
from concourse.bass_utils import run_bass_kernel_spmd

F32 = mybir.dt.float32
BF16 = mybir.dt.bfloat16
AF = mybir.ActivationFunctionType
ALU = mybir.AluOpType
AX = mybir.AxisListType

D = 1024
DEPTH = 2
GRID_W = 64
EPS = 1e-6
THETA = 10000.0
NCORES = 8
USE_HALF = True
C_CQ, C_CKV, C_KR, C_GA, C_Q2, C_K2, C_V2, C_GB, C_U, C_VC, C_GC, C_F, C_GD = (
    0, 192, 320, 352, 608, 864, 992, 1120, 1376, 1632, 1888, 2144, 2400)
INW = 2656


class Buf:
    __slots__ = ("name", "w", "r")

    def __init__(self, name):
        self.name = name
        self.w = None
        self.r = []


class Trk:
    def __init__(self, nc, es):
        self.nc = nc
        self.es = es
        self.eng = {"pe": nc.tensor, "act": nc.scalar, "dve": nc.vector,
                    "pool": nc.gpsimd, "sp": nc.sync}
        self.sem = {}
        self.cnt = {}
        self.waited = {k: {} for k in self.eng}
        for k in self.eng:
            self.sem[k] = es.enter_context(nc.semaphore("s_" + k))
            self.cnt[k] = 0
        self.dsem = {}
        self.dcnt = {}
        self.prog = {k: [] for k in self.eng}

    def dma_sem(self, name):
        if name not in self.dsem:
            self.dsem[name] = self.es.enter_context(self.nc.semaphore("d_" + name))
            self.dcnt[name] = 0

    def _sem_of(self, key):
        return self.sem[key] if key in self.sem else self.dsem[key]

    def _wait(self, e, tok):
        key, val = tok
        if key == e and e == "pe":
            return
        if self.waited[e].get(key, 0) >= val:
            return
        self.prog[e].append(("w", self._sem_of(key), val))
        self.waited[e][key] = val

    def _deps(self, e, reads, writes):
        for b in reads:
            if b.w is not None:
                self._wait(e, b.w)
        for b in writes:
            if b.w is not None and b.w[0] != e:
                self._wait(e, b.w)
            for t in b.r:
                if t[0] != e:
                    self._wait(e, t)

    def _mark(self, tok, reads, writes):
        for b in reads:
            b.r.append(tok)
            if len(b.r) > 24:
                b.r = b.r[-24:] if False else b.r
        for b in writes:
            b.w = tok
            b.r = []

    def op(self, e, fn, reads=(), writes=(), inc=True):
        self._deps(e, reads, writes)
        if inc:
            self.cnt[e] += 1
            self.prog[e].append(("i", fn, self.sem[e], 1))
            tok = (e, self.cnt[e])
        else:
            self.prog[e].append(("i", fn, None, 0))
            tok = (e, self.cnt[e] + 1)
        self._mark(tok, reads, writes)
        return tok

    def dma(self, q, semname, out, in_, reads=(), writes=(), **kw):
        semname = semname + "_" + q
        self.dma_sem(semname)
        self._deps(q, reads, writes)
        self.dcnt[semname] += 16
        self.prog[q].append(("i", (lambda e, o=out, i=in_, k=kw: e.dma_start(out=o, in_=i, **k)),
                             self.dsem[semname], 16))
        tok = (semname, self.dcnt[semname])
        self._mark(tok, reads, writes)
        return tok

    def barrier(self):
        keys = [k for k in self.eng]
        for e in keys:
            for f in keys:
                if f != e and self.cnt[f] > 0:
                    self._wait(e, (f, self.cnt[f]))
            for dn in self.dsem:
                if self.dcnt[dn] > 0:
                    self._wait(e, (dn, self.dcnt[dn]))

    def emit(self):
        nc = self.nc
        with nc.Block() as block:
            for k, starter in (("sp", block.sync), ("act", block.scalar), ("dve", block.vector),
                               ("pool", block.gpsimd), ("pe", block.tensor)):
                prog = self.prog[k]
                if not prog:
                    continue

                def body(eng, prog=prog):
                    for it in prog:
                        if it[0] == "w":
                            eng.wait_ge(it[1], it[2])
                        else:
                            inst = it[1](eng)
                            if it[2] is not None:
                                inst.then_inc(it[2], it[3])
                starter(body)


def rr(ap, s, **kw):
    return ap.rearrange(s, **kw)


def build(NB, S, CTX):
    NT, NC = S // 128, CTX // 128
    NK = NT + NC
    KT = NK * 128
    QB = min(512, S)
    NQB = S // QB
    KB = min(512, S)
    NKB = S // KB
    AG = 2
    nc = bass.Bass("TRN2", target_bir_lowering=False)
    dt = nc.dram_tensor

    def din(name, shape, dty=F32):
        return dt(name, list(shape), dty, kind="ExternalInput").ap()

    x_d = din("x", [NB, S, D]); c_d = din("c", [NB + 1, D]); ctx_d = din("ctx", [NB, CTX, D])
    norm_g_d = din("norm_g", [DEPTH, D]); w_mod_d = din("w_mod", [DEPTH, D, 3 * D])
    b_mod_d = din("b_mod", [DEPTH, 3 * D]); w_in_d = din("w_in", [DEPTH, D, INW])
    qnorm_d = din("mla_q_norm", [DEPTH, 192]); wuq_d = din("mla_w_uq", [DEPTH, 192, 384])
    kvnorm_d = din("mla_kv_norm", [DEPTH, 128]); wukv_d = din("mla_w_ukv", [DEPTH, 128, 512])
    mqn_d = din("mla_qn", [DEPTH, 96]); mkn_d = din("mla_kn", [DEPTH, 96])
    gqn_d = din("gqa_qn", [DEPTH, 64]); gkn_d = din("gqa_kn", [DEPTH, 64])
    lng_d = din("cm_ln_g", [DEPTH, 256]); lnb_d = din("cm_ln_b", [DEPTH, 256])
    cmw_d = din("cm_w_s", [DEPTH, 4, 128, 128]); cmb_d = din("cm_b_s", [DEPTH, 512])
    wf_d = din("fnet_w", [DEPTH, 256, 256]); wout_d = din("w_out", [DEPTH, D, D])
    ident_d = din("ident", [128, 128], BF16)
    ecs_d = din("ecs", [256, 512], BF16)
    ropeA_d = din("ropeA", [128, NT, 2, 32], BF16)
    ropeB_d = din("ropeB", [128, NT, 2, 64], BF16)
    KBH = min(512, S // 2)
    if not USE_HALF:
        tabL_d = din("tabL", [NKB, 128, NT, 2, KB], BF16)
    tabH_d = din("tabH", [(S // 2) // KBH, 128, NT, 2, KBH], BF16)
    altc_d = din("altc", [128, 1], BF16)
    tabC_d = din("tabC", [1, 128, NC, 2, CTX], BF16)
    out_d = dt("out", [NB, S, D], F32, kind="ExternalOutput").ap()
    ctxw_d = dt("ctxw", [NB, CTX, D], F32, kind="Internal").ap()
    hxs_d = dt("hxs", [128, 8, S + CTX], BF16, kind="Internal").ap()
    mods_d = dt("mods", [NB + 1, 3 * D], F32, kind="Internal").ap()

    es = contextlib.ExitStack()
    T = Trk(nc, es)

    def sb(name, shape, dty):
        return es.enter_context(nc.sbuf_tensor("sb_" + name, list(shape), dty))

    def psum(name, shape, dty):
        return es.enter_context(nc.psum_tensor("ps_" + name, list(shape), dty))

    Win = sb("Win", [128, 8, INW], BF16); bWin = Buf("Win")
    Wout = sb("Wout", [128, 8, D], BF16); bWout = Buf("Wout")
    Wuq = sb("Wuq", [128, 2, 384], BF16); bWuq = Buf("Wuq")
    Wukv = sb("Wukv", [128, 512], BF16); bWukv = Buf("Wukv")
    WsT = sb("WsT", [128, 4, 128], BF16); bWsT = Buf("WsT")
    brow = sb("brow", [1, 512], BF16); bbrow = Buf("brow")
    ones_r = sb("ones_r", [1, 128], BF16); bones = Buf("ones_r")
    Wcs = sb("Wcs", [128, 2, 512], BF16); bWcs = Buf("Wcs")
    bEcs = Buf("Ecs")
    ident = sb("ident", [128, 128], BF16); bident = Buf("ident")
    gbc = sb("gbc", [128, 832], F32); bgbc = Buf("gbc")
    GQN, GKN, GQ2, GK2, LNG, LNB = 0, 96, 192, 256, 320, 576
    ropeA = sb("ropeA", [128, NT, 2, 32], BF16); ropeB = sb("ropeB", [128, NT, 2, 64], BF16)
    brope = Buf("rope")
    epsT = sb("epsT", [128, 1], F32); beps = Buf("eps")
    identF = sb("identF", [64, 64], F32); bidentF = Buf("identF")
    altc = sb("altc", [128, 1], BF16); baltc = Buf("altc")
    modAB = sb("modAB", [128, NB + 1, 3, 8], F32); bmodAB = Buf("modAB")
    ngcol = sb("ngcol", [128, 8], F32)
    gtbc = sb("gtbc", [128, 1, D], F32); _bgt = Buf("gt0"); bgt = [_bgt, _bgt]
    K1T = sb("K1T", [128, 4, KT], BF16)
    V1a = sb("V1a", [128, NK, 384], BF16)
    K2T = sb("K2T", [128, KT], BF16)
    V2a = sb("V2a", [128, NK, 320], BF16)
    bKV = [Buf("kv%d" % k) for k in range(NK)]
    ydT = sb("ydT", [128, 2, S], BF16); bydT = [Buf("yd%d" % k) for k in range(NKB)]
    ydTc = sb("ydTc", [128, 2, CTX], BF16); bydTc = Buf("ydTc")
    Gc = sb("Gc", [128, NC, 512], BF16); bGc = [Buf("Gc%d" % k) for k in range(NC)]
    xt = [sb("xt%d" % i, [128, D], F32) for i in range(2)]; bxt = [Buf("xt0"), Buf("xt1")]
    _xn = sb("xn0", [128, D], BF16); xn = [_xn, _xn]; _bxn = Buf("xn0"); bxn = [_bxn, _bxn]
    junk = sb("junk", [128, D], BF16); bjunk = Buf("junk")
    st = [sb("st%d" % i, [128, 16], F32) for i in range(2)]; bst = [Buf("st0"), Buf("st1")]
    hxT = sb("hxT", [128, 8, 512], BF16); bhx = [Buf("hx%d" % i) for i in range(4)]; bhxd = [Buf("hxd%d" % i) for i in range(4)]
    bHX = {}
    fT = sb("fT", [128, 2, 512], BF16); bfT = Buf("fT")
    pA = [psum("pA%d" % i, [128, 512], F32) for i in range(2)]; bpA = [Buf("pA0"), Buf("pA1")]
    pB = [psum("pB%d" % i, [128, 512], F32) for i in range(2)]; bpB = [Buf("pB0"), Buf("pB1")]
    pO = [psum("pO%d" % i, [128, 512], F32) for i in range(2)]; bpO = [Buf("pO0"), Buf("pO1")]
    pT = psum("pT", [128, 1024], BF16); bpT = Buf("pT")
    pT2 = psum("pT2", [128, 1024], BF16); bpT2 = Buf("pT2")

    class SC:
        pass
    SCS = []
    for k_ in range(2):
        o_ = SC()
        for nm_ in ("tmA", "tmB", "tmC", "tmD"):
            setattr(o_, nm_, sb("%s_%d" % (nm_, k_), [128, 512], F32)); setattr(o_, "b" + nm_, Buf("%s_%d" % (nm_, k_)))
        o_.tb16 = sb("tb16_%d" % k_, [128, 512], BF16); o_.btb16 = Buf("tb16_%d" % k_)
        o_.tT = sb("tT_%d" % k_, [128, 256], BF16); o_.btT = Buf("tT_%d" % k_)
        o_.sm = [sb("sm%d_%d" % (k_, i), [128, 32], F32) for i in range(2)]
        o_.bsm = [Buf("sm%d_%d" % (k_, i)) for i in range(2)]
        o_.smi = 0

        def nsm(o_=o_):
            i = o_.smi
            o_.smi = 1 - i
            return o_.sm[i], o_.bsm[i]
        o_.nsm = nsm
        SCS.append(o_)
    for k_ in range(2):
        SCS[k_].P, SCS[k_].bP, SCS[k_].Q, SCS[k_].bQ = pA[k_], bpA[k_], pB[k_], bpB[k_]
    SCS[0].pT, SCS[0].bpT, SCS[1].pT, SCS[1].bpT = pT2, bpT2, pT, bpT
    tmA, btmA, tmB, btmB, tmC, btmC, tmD, btmD = (SCS[0].tmA, SCS[0].btmA, SCS[0].tmB, SCS[0].btmB,
                                                    SCS[0].tmC, SCS[0].btmC, SCS[0].tmD, SCS[0].btmD)
    tb16, btb16, tT, btT = SCS[0].tb16, SCS[0].btb16, SCS[0].tT, SCS[0].btT
    sm, bsm = SCS[0].sm, SCS[0].bsm
    rinv, brinv = SCS[1].tmA, SCS[1].btmA
    tmpo, btmpo = tmA, btmA
    _res = sb("res0", [128, D], F32); res = [_res, _res]; bres = [Buf("res0"), Buf("res0b")]
    RS = max(NT * 512 + 2 * AG * 2 * KB, 512 * (4 + 2 + 4 + 2 + 4 + 4 + 4))
    R = sb("R", [128, RS], BF16)
    G = rr(R[:, 0:NT * 512], "p (a n) -> p a n", a=NT); bG = [Buf("G%d" % k) for k in range(NT)]
    tabv = [rr(R[:, NT * 512 + i * AG * 2 * KB: NT * 512 + (i + 1) * AG * 2 * KB],
               "p (a j n) -> p a j n", a=AG, j=2) for i in range(2)]
    btab = [Buf("tab0"), Buf("tab1")]
    o = 0
    q1T = rr(R[:, o:o + 2048], "p (h n) -> p h n", h=4); o += 2048; bq1T = Buf("q1T")
    q2T = rr(R[:, o:o + 1024], "p (h n) -> p h n", h=2); o += 1024; bq2T = Buf("q2T")
    gT = rr(R[:, o:o + 2048], "p (h n) -> p h n", h=4); o += 2048; bgT = Buf("gT")
    ugT = rr(R[:, o:o + 1024], "p (h n) -> p h n", h=2); o += 1024; bugT = Buf("ugT")
    ccT = rr(R[:, o:o + 2048], "p (h n) -> p h n", h=4); o += 2048
    bcc = [Buf("cc%d" % k) for k in range(4)]
    ycT = rr(R[:, o:o + 2048], "p (y h n) -> p y h n", y=2, h=2); o += 2048
    byc = [Buf("yc0"), Buf("yc1")]
    probs = [R[:, o + i * 512: o + (i + 1) * 512] for i in range(4)]; o += 2048
    bprobs = [Buf("pr%d" % i) for i in range(4)]
    sgc = SCS[0].tb16; bsgc = SCS[0].btb16
    Ecs = rr(R[:, 8192:9216], "p (c n) -> p c n", c=2)

    SRING = [(pA[0], bpA[0]), (pA[1], bpA[1]), (pB[0], bpB[0])]
    pT_, bpT_ = pT, bpT
    ctr = {"xt": 0, "st": 0, "pa": 0, "pb": 0, "res": 0, "sm": 0, "tab": 0, "pr": 0, "po": 0}

    def nxt(k, n=2):
        v = ctr[k]
        ctr[k] = (v + 1) % n
        return v

    def mm(out, lhsT, rhs, start, stop, reads, writes, inc=True):
        T.op("pe", lambda e: e.matmul(out, lhsT=lhsT, rhs=rhs, start=start, stop=stop),
             reads=reads, writes=writes, inc=inc)

    def tr(out, in_, reads, writes):
        n = in_.shape[0]
        T.op("pe", lambda e: e.transpose(out, in_, ident[0:n, 0:n]), reads=list(reads) + [bident], writes=writes)

    def act(out, in_, func, reads, writes, **kw):
        T.op("act", lambda e: e.activation(out=out, in_=in_, func=func, **kw), reads=reads, writes=writes)

    def tt(e, out, in0, in1, op, reads, writes):
        T.op(e, lambda en: en.tensor_tensor(out=out, in0=in0, in1=in1, op=op), reads=reads, writes=writes)

    def ts(e, out, in0, s1, s2, op0, op1, reads, writes):
        if op1 is None:
            T.op(e, lambda en: en.tensor_scalar(out=out, in0=in0, scalar1=s1, scalar2=None, op0=op0),
                 reads=reads, writes=writes)
        else:
            T.op(e, lambda en: en.tensor_scalar(out=out, in0=in0, scalar1=s1, scalar2=s2, op0=op0, op1=op1),
                 reads=reads, writes=writes)

    def cp(e, out, in_, reads, writes):
        if e == "act":
            T.op("act", lambda en: en.copy(out=out, in_=in_), reads=reads, writes=writes)
        else:
            T.op(e, lambda en: en.tensor_copy(out=out, in_=in_), reads=reads, writes=writes)

    def rsqrt_cols(dst, src, scale, s_buf, reads):
        act(dst, src, AF.Sqrt, reads=list(reads) + [s_buf, beps], writes=[s_buf], scale=scale, bias=epsT[:, 0:1])
        T.op("dve", lambda e: e.reciprocal(out=dst, in_=dst), reads=[s_buf], writes=[s_buf])

    def bc(ap, shape):
        return ap.to_broadcast(list(shape))

    def rope(xv, outv, tab, ti, H, dh, x_buf, out_bufs, scratch, s_buf):
        half = dh // 4
        t1 = rr(scratch[:, 0:H * dh], "p (h d) -> p h d", h=H)
        t2 = rr(scratch[:, 256:256 + H * dh], "p (h d) -> p h d", h=H)
        cosv = bc(tab[:, ti, 0, :].unsqueeze(1), [128, H, dh])
        tt("pool", t1, xv, cosv, ALU.mult, reads=[x_buf, brope], writes=[s_buf])
        for j in range(2):
            for g in range(2):
                o_ = t2[:, :, g * 2 * half + j * half: g * 2 * half + (j + 1) * half]
                i_ = xv[:, :, g * 2 * half + (1 - j) * half: g * 2 * half + (2 - j) * half]
                s_ = bc(tab[:, ti, 1, g * 2 * half + j * half: g * 2 * half + (j + 1) * half].unsqueeze(1),
                        [128, H, half])
                tt("pool", o_, i_, s_, ALU.mult, reads=[x_buf, brope, s_buf], writes=[s_buf])
        for (ov, hs) in outv:
            tt("pool", ov, t1[:, hs, :], t2[:, hs, :], ALU.add, reads=[s_buf], writes=out_bufs)

    T.dma("sp", "c0", ident[:], ident_d[:, :], writes=[bident])
    T.dma("sp", "c0", altc[:], altc_d[:, :], writes=[baltc])
    T.dma("sp", "c0", ropeA[:], ropeA_d[:, :, :, :], writes=[brope])
    T.dma("sp", "c0", ropeB[:], ropeB_d[:, :, :, :], writes=[brope])
    ctok = ("c0_sp", T.dcnt["c0_sp"])
    for b in (bident, brope, baltc):
        b.w = ctok
    T.op("dve", lambda e: e.memset(epsT[:], EPS), writes=[beps])
    T.op("dve", lambda e: e.tensor_copy(out=identF[:, :], in_=ident[0:64, 0:64]), reads=[bident], writes=[bidentF])
    T.op("dve", lambda e: e.memset(ones_r[:], 1.0), writes=[bones])
    T.op("pool", lambda e: e.memset(V1a[:], 1.0), writes=bKV)
    T.op("pool", lambda e: e.memset(V2a[:], 1.0), writes=bKV)

    def load_layer(l):
        T.barrier()
        wb = [bWin, bWout, bWsT, bbrow, bWcs, bgbc, bWuq, bWukv, bmodAB]
        stg = rr(tb16[:, 0:512], "p (g q) -> p g q", g=4)
        for g in range(4):
            T.dma("pool", "w", stg[:, g, :], cmw_d[l, g, :, :], writes=[btb16])
        T.dma("pool", "w", brow[:], cmb_d[l:l + 1, :], writes=[bbrow])
        wfs = rr(fT[:, :, 0:256], "p c n -> p c n")
        T.dma("pool", "w", wfs, rr(wf_d[l], "(c p) n -> p c n", p=128), writes=[bfT])
        T.dma("sp", "w", Ecs, rr(ecs_d, "(c p) n -> p c n", p=128), writes=[bEcs])
        T.dma("sp", "w", tmA[:, 0:384], wuq_d[l, 0:128, :], writes=[btmA])
        T.dma("sp", "w", tmB[0:64, 0:384], wuq_d[l, 128:192, :], writes=[btmB])
        T.dma("sp", "w", tmC[:, 0:512], wukv_d[l, :, :], writes=[btmC])
        smt = sm[0]
        T.dma("sp", "w", smt[:, 0:1], AP(qnorm_d.tensor, l * 192, [[1, 128], [1, 1]]), writes=[bsm[0]])
        T.dma("sp", "w", smt[0:64, 1:2], AP(qnorm_d.tensor, l * 192 + 128, [[1, 64], [1, 1]]), writes=[bsm[0]])
        T.dma("sp", "w", smt[:, 2:3], AP(kvnorm_d.tensor, l * 128, [[1, 128], [1, 1]]), writes=[bsm[0]])
        for (off, src, n) in ((GQN, mqn_d, 96), (GKN, mkn_d, 96), (GQ2, gqn_d, 64), (GK2, gkn_d, 64),
                              (LNG, lng_d, 256), (LNB, lnb_d, 256)):
            T.dma("sp", "w", gbc[:, off:off + n], AP(src.tensor, l * n, [[0, 128], [1, n]]), writes=[bgbc])
        nrow = 8 * (NB + 2)
        T.dma("sp", "w", tmD[0:8, 0:128], AP(norm_g_d.tensor, l * D, [[128, 8], [1, 128]]), writes=[btmD])
        for v in range(NB + 1):
            T.dma("sp", "w", tmD[8 + 8 * v:16 + 8 * v, 0:128], AP(c_d.tensor, v * D, [[128, 8], [1, 128]]), writes=[btmD])
        cT = rr(tmD[:, 256:256 + 8 * (NB + 1)], "p (c v) -> p c v", c=8)
        NV = NB + 1
        for j_ in range(3):
            T.dma("sp", "w", res[0][32 * j_:32 * j_ + NV, 0:1024],
                  AP(b_mod_d.tensor, l * 3 * D + 1024 * j_, [[0, NV], [1, 1024]]), writes=[bres[0]])
        wtok = ("w_sp", T.dcnt["w_sp"])
        for b in [bgbc, bmodAB, btmA, btmB, btmC, bsm[0], btmD, bres[0], bEcs]:
            b.w = wtok
        wtokp = ("w_pool", T.dcnt["w_pool"])
        for b in [btb16, bbrow, bfT]:
            b.w = wtokp
        pa = nxt("pa")
        T.op("pe", lambda e, pa=pa: e.transpose(pA[pa][:, 0:nrow], tmD[0:nrow, 0:128], identF[0:nrow, 0:nrow]),
             reads=[btmD, bidentF], writes=[bpA[pa]])
        cp("dve", ngcol[:], pA[pa][:, 0:8], [bpA[pa]], [bmodAB])
        cp("dve", cT, rr(pA[pa][:, 8:nrow], "p (v c) -> p c v", c=8), [bpA[pa]], [btmD])
        ts("dve", gbc[:, GQN:GQN + 96], gbc[:, GQN:GQN + 96], 96 ** -0.5, None, ALU.mult, None, [bgbc], [bgbc])
        ts("dve", gbc[:, GQ2:GQ2 + 64], gbc[:, GQ2:GQ2 + 64], 64 ** -0.5, None, ALU.mult, None, [bgbc], [bgbc])
        ts("dve", Wuq[:, 0, :], tmA[:, 0:384], smt[:, 0:1], None, ALU.mult, None, [btmA, bsm[0]], [bWuq])
        ts("dve", Wuq[0:64, 1, :], tmB[0:64, 0:384], smt[0:64, 1:2], None, ALU.mult, None, [btmB, bsm[0]], [bWuq])
        ts("dve", Wukv[:, :], tmC[:, 0:512], smt[:, 2:3], None, ALU.mult, None, [btmC, bsm[0]], [bWukv])
        for g in range(4):
            tr(pT2[:, g * 128:(g + 1) * 128], stg[:, g, :], [btb16], [bpT2])
        cp("dve", WsT[:, :, :], rr(pT2[:, 0:512], "p (g q) -> p g q", g=4), [bpT2], [bWsT])
        for cc in range(2):
            for part in range(2):
                pa = nxt("pa")
                for kc in range(2):
                    mm(pA[pa][:, 0:256], Ecs[:, kc, part * 256 + cc * 128: part * 256 + (cc + 1) * 128],
                       wfs[:, kc, :], kc == 0, kc == 1, [bEcs, bfT], [bpA[pa]])
                cp("dve", Wcs[:, cc, part * 256:(part + 1) * 256], pA[pa][:, 0:256], [bpA[pa]], [bWcs])
        scT = rr(tb16[:, 0:8 * (NB + 1)], "p (c v) -> p c v", c=8)
        act(scT, cT, AF.Silu, [btmD, bpT2], [btb16])
        stg_t = [(xt[0], bxt[0]), (xt[1], bxt[1]), (gtbc[:, 0, :], bgt[0])]
        cast_e = ["act", "act", "act"]
        pieces = []
        for c in range(8):
            for (c0, c1) in ((0, 1024), (1024, 2048), (2048, INW)):
                pieces.append((Win[:, c, c0:c1], w_in_d[l, c * 128:(c + 1) * 128, c0:c1], c1 - c0, bWin))
        for c in range(8):
            pieces.append((Wout[:, c, :], wout_d[l, c * 128:(c + 1) * 128, :], D, bWout))
        for pi, (dst, src, wdt, wbuf) in enumerate(pieces):
            st_, bst_ = stg_t[pi % 3]
            T.dma("sp", "sg%d" % (pi % 3), st_[:, 0:wdt], src, writes=[bst_])
            cp(cast_e[pi % 3], dst, st_[:, 0:wdt], [bst_], [wbuf])
        wm = [rr(R[:, i * 4096:(i + 1) * 4096], "p (c n) -> p c n", c=8) for i in range(2)]
        bwm = [Buf("wm0"), Buf("wm1")]
        for nb_ in range(6):
            k = nb_ % 2
            T.dma("pool", "wm%d" % k, wm[k], rr(w_mod_d[l][:, nb_ * 512:(nb_ + 1) * 512], "(c p) n -> p c n", p=128),
                  writes=[bwm[k]])
            pa = nxt("pa")
            for kc in range(8):
                mm(pA[pa][0:NV, :], scT[:, kc, 0:NV], wm[k][:, kc, :], kc == 0, kc == 7,
                   [btb16, bwm[k]], [bpA[pa]], inc=(kc == 7))
            sl = nxt("sm")
            mrow = tmpo if sl == 0 else rinv
            bmrow = btmpo if sl == 0 else brinv
            tt("dve", mrow[0:NV, :], pA[pa][0:NV, :],
               res[0][32 * (nb_ // 2):32 * (nb_ // 2) + NV, (nb_ % 2) * 512:(nb_ % 2 + 1) * 512], ALU.add,
               [bpA[pa], bres[0]], [bmrow])
            T.dma("pool", "mo%d" % sl, mods_d[0:NV, nb_ * 512:(nb_ + 1) * 512], mrow[0:NV, :], reads=[bmrow])
        T.barrier()
        nr2 = 16 * (NB + 1)
        for v in range(NB + 1):
            T.dma("sp", "mi", tmD[16 * v:16 * v + 16, 0:128], AP(mods_d.tensor, v * 3 * D, [[128, 16], [1, 128]]), writes=[btmD])
        btmD.w = ("mi_sp", T.dcnt["mi_sp"])
        pa = nxt("pa")
        T.op("pe", lambda e, pa=pa: e.transpose(pA[pa][:, 0:nr2], tmD[0:nr2, 0:128], identF[0:nr2, 0:nr2]),
             reads=[btmD, bidentF], writes=[bpA[pa]])
        cp("dve", modAB[:, :, 1:3, :], rr(pA[pa][:, 0:nr2], "p (v j c) -> p v j c", j=2, c=8), [bpA[pa]], [bmodAB])
        for v in range(NB + 1):
            ts("dve", modAB[:, v, 2, :], modAB[:, v, 2, :], 1.0, None, ALU.add, None, [bmodAB], [bmodAB])
            tt("dve", modAB[:, v, 0, :], modAB[:, v, 2, :], ngcol[:], ALU.mult, [bmodAB], [bmodAB])
        T.barrier()

    def norm_load(src_ap, src_buf):
        k = nxt("xt")
        T.dma("sp", "xt%d" % k, xt[k][:], src_ap, reads=[src_buf], writes=[bxt[k]])
        return k

    def norm_tile_g(src_ap, src_buf, v, slot, k=None, after_xn=None, tb=None, btb=None):
        pT, bpT = (pT_, bpT_) if tb is None else (tb, btb)
        if k is None:
            k = norm_load(src_ap, src_buf)
        s = st[k]
        act(junk[:], xt[k][:], AF.Square, [bxt[k]], [bjunk, bst[k]], accum_out=s[:, 0:1])
        rsqrt_cols(s[:, 1:2], s[:, 0:1], 1.0 / D, bst[k], [])
        yield
        ts("dve", xn[k][:], xt[k][:], s[:, 1:2], None, ALU.mult, None, [bxt[k], bst[k]], [bxn[k]])
        if after_xn is not None:
            after_xn()
        for c in range(8):
            tr(pT[:, c * 128:(c + 1) * 128], xn[k][:, c * 128:(c + 1) * 128], [bxn[k]], [bpT])
        yield
        for c in range(8):
            o_ = hxT[:, c, slot * 128:(slot + 1) * 128]
            i_ = pT[:, c * 128:(c + 1) * 128]
            if c % 2 == 0:
                ts("dve", o_, i_, modAB[:, v, 0, c:c + 1], modAB[:, v, 1, c:c + 1], ALU.mult, ALU.add,
                   [bpT, bmodAB], [bhx[slot], bhxd[slot]])
            else:
                act(o_, i_, AF.Identity, [bpT, bmodAB] + ([bhxd[slot]] if c == 7 else []), [bhx[slot]],
                    scale=modAB[:, v, 0, c:c + 1], bias=modAB[:, v, 1, c:c + 1])
                yield

    def norm_tile(src_ap, src_buf, v, slot):
        for _ in norm_tile_g(src_ap, src_buf, v, slot):
            pass

    def proj_tm(pbank, pbuf, col0, slot, w0, w1):
        n = w1 - w0
        for c in range(8):
            mm(pbank[:, col0:col0 + n], hxT[:, c, slot * 128:(slot + 1) * 128], Win[:, c, w0:w1],
               c == 0, c == 7, [bhx[slot], bWin], [pbuf], inc=(c == 7))

    def proj_fm(pbank, pbuf, w0, N, nslots):
        for c in range(8):
            mm(pbank[:, 0:N], Win[:, c, w0:w0 + 128], hxT[:, c, 0:N], c == 0, c == 7,
               [bWin] + bhx[0:nslots], [pbuf], inc=(c == 7))

    def kv_tile(slot, kt, is_ctx, ti, sc):
        tmA, btmA, tmB, btmB, tmC, btmC, tmD, btmD = sc.tmA, sc.btmA, sc.tmB, sc.btmB, sc.tmC, sc.btmC, sc.tmD, sc.btmD
        tb16, btb16, tT, btT = sc.tb16, sc.btb16, sc.tT, sc.btT
        pT2, bpT2 = sc.pT, sc.bpT
        P_, bP = sc.P, sc.bP
        proj_tm(P_, bP, 0, slot, C_CKV, C_KR + 32)
        proj_tm(P_, bP, 160, slot, C_K2, C_V2 + 128)
        yield
        s, bs = sc.nsm()
        act(junk[:, 0:128], P_[:, 0:128], AF.Square, [bP], [bjunk, bs], accum_out=s[:, 0:1])
        rsqrt_cols(s[:, 1:2], s[:, 0:1], 1.0 / 128, bs, [])
        ts("dve", tb16[:, 0:128], P_[:, 0:128], s[:, 1:2], None, ALU.mult, None, [bP, bs], [btb16])
        tr(pT2[:, 0:128], tb16[:, 0:128], [btb16], [bpT2])
        cp("dve", tT[:, 0:128], pT2[:, 0:128], [bpT2], [btT])
        yield
        KV, bKVp = sc.Q, sc.bQ
        mm(KV[:, 0:512], tT[:, 0:128], Wukv[:, :], True, True, [btT, bWukv], [bKVp])
        KV3 = rr(KV[:, 0:512], "p (h d) -> p h d", h=4)
        yield
        V1k = V1a[:, kt, :]
        cp("act", rr(V1k[:, 0:384], "p (a n) -> p a n", a=2)[:, :, 0:64], KV3[:, 0:4:2, 64:128], [bKVp], [bKV[kt]])
        cp("act", rr(V1k[:, 0:384], "p (a n) -> p a n", a=2)[:, :, 128:192], KV3[:, 1:4:2, 64:128], [bKVp], [bKV[kt]])
        sq = rr(tmA[:, 0:256], "p (h d) -> p h d", h=4)
        act(sq, KV3[:, :, 0:64], AF.Square, [bKVp], [btmA])
        T.op("dve", lambda e: e.tensor_reduce(out=s[:, 4:8], in_=sq, axis=AX.X, op=ALU.add), reads=[btmA], writes=[bs])
        act(junk[:, 128:160], P_[:, 128:160], AF.Square, [bP], [bjunk, bs], accum_out=s[:, 2:3])
        ts("dve", s[:, 8:12], s[:, 4:8], s[:, 2:3], None, ALU.add, None, [bs], [bs])
        rsqrt_cols(s[:, 12:16], s[:, 8:12], 1.0 / 96, bs, [])
        yield
        k1tm = rr(tb16[:, 128:512], "p (h d) -> p h d", h=4)
        tmp1 = rr(tmB[:, 0:256], "p (h d) -> p h d", h=4)
        tt("dve", tmp1, KV3[:, :, 0:64], bc(gbc[:, GKN:GKN + 64].unsqueeze(1), [128, 4, 64]), ALU.mult,
           [bKVp, bgbc], [btmB])
        yield
        tt("pool", k1tm[:, :, 0:64], tmp1, bc(s[:, 12:16].unsqueeze(2), [128, 4, 64]), ALU.mult,
           [btmB, bs], [btb16])
        krg = rr(tmC[:, 0:32], "p (h d) -> p h d", h=1)
        tt("dve", krg, rr(P_[:, 128:160], "p (h d) -> p h d", h=1), gbc[:, GKN + 64:GKN + 96].unsqueeze(1), ALU.mult,
           [bP, bgbc], [btmC])
        if not is_ctx:
            krr = rr(tmC[:, 32:64], "p (h d) -> p h d", h=1)
            rope(krg, [(krr, slice(0, 1))], ropeA, ti, 1, 32, btmC, [btmC], tmD, btmD)
        else:
            krr = krg
        tt("pool", k1tm[:, :, 64:96], bc(krr, [128, 4, 32]), bc(s[:, 12:16].unsqueeze(2), [128, 4, 32]), ALU.mult,
           [btmC, bs], [btb16])
        for h in range(4):
            tr(pT2[0:96, 128 + h * 128:256 + h * 128], k1tm[:, h, :], [btb16], [bpT2])
        cp("dve", K1T[0:96, :, kt * 128:(kt + 1) * 128], rr(pT2[0:96, 128:640], "p (h n) -> p h n", h=4),
           [bpT2], [bKV[kt]])
        yield
        k2v = rr(P_[:, 160:288], "p (h d) -> p h d", h=2)
        sq2 = rr(tmA[:, 256:384], "p (h d) -> p h d", h=2)
        act(sq2, k2v, AF.Square, [bP], [btmA])
        s2, bs2 = sc.nsm()
        T.op("dve", lambda e: e.tensor_reduce(out=s2[:, 0:2], in_=sq2, axis=AX.X, op=ALU.add), reads=[btmA], writes=[bs2])
        rsqrt_cols(s2[:, 2:4], s2[:, 0:2], 1.0 / 64, bs2, [])
        yield
        tmp2 = rr(tmB[:, 256:384], "p (h d) -> p h d", h=2)
        tt("dve", tmp2, k2v, bc(gbc[:, GK2:GK2 + 64].unsqueeze(1), [128, 2, 64]), ALU.mult, [bP, bgbc], [btmB])
        k2r = rr(tT[:, 128:256], "p (h d) -> p h d", h=2)
        if not is_ctx:
            k2n = rr(tmC[:, 128:256], "p (h d) -> p h d", h=2)
            tt("pool", k2n, tmp2, bc(s2[:, 2:4].unsqueeze(2), [128, 2, 64]), ALU.mult, [btmB, bs2], [btmC])
            rope(k2n, [(k2r, slice(0, 2))], ropeB, ti, 2, 64, btmC, [btT], tmD, btmD)
        else:
            tt("pool", k2r, tmp2, bc(s2[:, 2:4].unsqueeze(2), [128, 2, 64]), ALU.mult, [btmB, bs2], [btT])
        tr(pT2[:, 640:768], tT[:, 128:256], [btT], [bpT2])
        yield
        cp("act", K2T[:, kt * 128:(kt + 1) * 128], pT2[:, 640:768], [bpT2], [bKV[kt]])
        cp("act", rr(V2a[:, kt, 64:320], "p (a n) -> p a n", a=2)[:, :, 0:64],
           rr(P_[:, 288:416], "p (h d) -> p h d", h=2), [bP], [bKV[kt]])

    def g_tile(slot, Gdst, Gbuf):
        pa = nxt("pa")
        for kc in range(2):
            mm(pA[pa][:, 0:512], fT[:, kc, slot * 128:(slot + 1) * 128], Wcs[:, kc, :], kc == 0, kc == 1,
               [bfT, bWcs], [bpA[pa]])
        cp("act", Gdst, pA[pa][:, 0:512], [bpA[pa]], [Gbuf])

    def run_gens(items):
        pending = list(items)
        active = []
        free_sets = [SCS[0], SCS[1]]
        while pending or active:
            k = 0
            while k < len(pending):
                needs, f = pending[k][0], pending[k][1]
                wgt = pending[k][2] if len(pending[k]) > 2 else 1
                rdy = pending[k][3] if len(pending[k]) > 3 else None
                if rdy is not None and not rdy():
                    k += 1
                    continue
                if needs:
                    if free_sets:
                        sc = free_sets.pop(0)
                        active.append((f(sc), sc, wgt))
                        pending.pop(k)
                        continue
                    k += 1
                else:
                    active.append((f(), None, wgt))
                    pending.pop(k)
                    continue
            for it in list(active):
                try:
                    for _ in range(it[2]):
                        next(it[0])
                except StopIteration:
                    active.remove(it)
                    if it[1] is not None:
                        free_sets.append(it[1])

    def phase1_block(src_tiles, v, is_ctx, kt0, ti0, yd_dst, yd_bufs, Gv, Gb, hxcol, need_fm=True):
        n = len(src_tiles)
        N = n * 128
        pTn = pO[1][:, :].bitcast(BF16)
        done = [0]

        def norm_all():
            ks = {}
            for i in range(min(2, n)):
                ks[i] = norm_load(*src_tiles[i])
            for i, (sap, sbuf_) in enumerate(src_tiles):
                def pre(i=i):
                    if i + 2 < n:
                        ks[i + 2] = norm_load(*src_tiles[i + 2])
                for _ in norm_tile_g(sap, sbuf_, v, i, k=ks[i], after_xn=pre, tb=pTn, btb=bpO[1]):
                    yield
                done[0] = i + 1
            if need_fm:
                hb = bHX.setdefault(hxcol, Buf("HX%d" % hxcol))
                T.dma("sp", "hs", hxs_d[:, :, hxcol:hxcol + N], hxT[:, :, 0:N], reads=[bhx[i_] for i_ in range(n)],
                      writes=[hb])

        def fm_gen():
            for oc in range(2):
                pb = 0
                proj_fm(pO[pb], bpO[pb], C_F + oc * 128, N, n)
                cp("dve", fT[:, oc, 0:N], pO[pb][:, 0:N], [bpO[pb]], [bfT])
                yield
            for oc in range(2):
                pb = 0
                proj_fm(pO[pb], bpO[pb], C_GD + oc * 128, N, n)
                act(yd_dst[:, oc, :], pO[pb][:, 0:N], AF.Silu, [bpO[pb]], yd_bufs)
                yield
        items = [(False, norm_all)]
        items += [(True, (lambda sc, i=i: kv_tile(i, kt0 + i, is_ctx, ti0 + i, sc)), 1, (lambda i=i: done[0] > i))
                  for i in range(n)]
        if need_fm:
            items.append((False, fm_gen, 1, (lambda: done[0] >= n)))
        run_gens(items)
        if need_fm:
            for i in range(n):
                g_tile(i, Gv[:, ti0 + i if not is_ctx else i, :], Gb[ti0 + i if not is_ctx else i])

    def fourier(tab_d, nkb, kb_n, na, Gv, Gb, ydv, ydbufs):
        ag = min(AG, na)
        for kb in range(nkb):
            P0, P1 = pO[0], pO[1]
            for a0 in range(0, na, ag):
                k = nxt("tab")
                tv = tabv[k] if kb_n == KB else rr(R[:, NT * 512 + k * AG * 2 * KB: NT * 512 + k * AG * 2 * KB + ag * 2 * kb_n],
                                                   "p (a j n) -> p a j n", a=ag, j=2)
                T.dma("sp", "tb%d" % k, tv[:, 0:ag, :, :], tab_d[kb, :, a0:a0 + ag, :, :], writes=[btab[k]])
                for a in range(a0, a0 + ag):
                    for cc, (PP, bPP) in enumerate(((P0, bpO[0]), (P1, bpO[1]))):
                        mm(PP[:, 0:kb_n], Gv[:, a, cc * 128:(cc + 1) * 128], tv[:, a - a0, 0, :],
                           a == 0, False, [Gb[a], btab[k]], [bPP], inc=False)
                        mm(PP[:, 0:kb_n], Gv[:, a, 256 + cc * 128:256 + (cc + 1) * 128], tv[:, a - a0, 1, :],
                           False, a == na - 1, [Gb[a], btab[k]], [bPP])
            for cc in range(2):
                tt("dve", ydv[:, cc, kb * kb_n:(kb + 1) * kb_n], pO[cc][:, 0:kb_n],
                   ydv[:, cc, kb * kb_n:(kb + 1) * kb_n], ALU.mult, [bpO[cc], ydbufs[kb]], [ydbufs[kb]])

    def fourier_half(Gv, Gb, ydv, ydbufs):
        Hh = S // 2
        KBh = min(512, Hh)
        nkb = Hh // KBh
        ag = min(AG, NT)
        acc = [(pO[0], bpO[0]), (pO[1], bpO[1]), (pA[0], bpA[0]), (pA[1], bpA[1])]
        for kb in range(nkb):
            for a0 in range(0, NT, ag):
                k = nxt("tab")
                base = NT * 512 + k * AG * 2 * KB
                tv = rr(R[:, base: base + ag * 2 * KBh], "p (a j n) -> p a j n", a=ag, j=2)
                T.dma("sp", "tb%d" % k, tv[:, 0:ag, :, :], tabH_d[kb, :, a0:a0 + ag, :, :], writes=[btab[k]])
                for a in range(a0, a0 + ag):
                    for part in range(2):
                        for cc in range(2):
                            PP, bPP = acc[part * 2 + cc]
                            mm(PP[:, 0:KBh], Gv[:, a, part * 256 + cc * 128: part * 256 + (cc + 1) * 128],
                               tv[:, a - a0, part, :], a == 0, a == NT - 1, [Gb[a], btab[k]], [bPP])
            k0 = kb * KBh
            for cc in range(2):
                sc = SCS[cc]
                Pc, bPc = acc[cc]
                Ps, bPs = acc[2 + cc]
                cp("act", sc.tmA[:, 0:KBh], Ps[:, 0:KBh], [bPs], [sc.btmA])
                tt("dve", sc.tmB[:, 0:KBh], Pc[:, 0:KBh], sc.tmA[:, 0:KBh], ALU.subtract, [bPc, sc.btmA], [sc.btmB])
                tt("dve", sc.tmC[:, 0:KBh], Pc[:, 0:KBh], sc.tmA[:, 0:KBh], ALU.add, [bPc, sc.btmA], [sc.btmC])
                fwd = ydv[:, cc, k0:k0 + KBh]
                tt("pool", fwd, sc.tmB[:, 0:KBh], fwd, ALU.mult, [sc.btmB] + list(ydbufs), list(ydbufs))
                j0 = 1 if kb == 0 else 0
                cnt = KBh - j0
                hi = S - k0 - j0
                rev = ydv[:, cc, hi:hi - cnt:-1]
                tt("dve", rev, sc.tmC[:, j0:KBh], rev, ALU.mult, [sc.btmC] + list(ydbufs), list(ydbufs))
        for cc in range(2):
            for a in range(NT):
                mm(pB[0][:, cc:cc + 1], Gv[:, a, cc * 128:(cc + 1) * 128], altc[:, 0:1], a == 0, a == NT - 1,
                   [Gb[a], baltc], [bpB[0]])
        for cc in range(2):
            tt("dve", ydv[:, cc, Hh:Hh + 1], pB[0][:, cc:cc + 1], ydv[:, cc, Hh:Hh + 1], ALU.mult,
               [bpB[0]] + list(ydbufs), list(ydbufs))

    def make_block(src_tiles, dst_tiles, v, is_ctx, ti0, key_tiles, ydv, ydoff, ydbufs, yset, hxcol):
        n = len(src_tiles)
        N = n * 128

        def N1():
            T.dma("sp", "hl", hxT[:, :, 0:N], hxs_d[:, :, hxcol:hxcol + N], reads=[bHX[hxcol]],
                  writes=[bhx[i_] for i_ in range(n)])
            yield
            for oc in range(2):
                pb = 1
                proj_fm(pB[pb], bpB[pb], C_U + oc * 128, N, n)
                cp("dve", ugT[:, oc, 0:N], pB[pb][:, 0:N], [bpB[pb]], [bugT])
                yield
                pb = 1
                proj_fm(pB[pb], bpB[pb], C_GC + oc * 128, N, n)
                act(sgc[:, 0:N], pB[pb][:, 0:N], AF.Silu, [bpB[pb]], [bsgc])
                tt("pool", ugT[:, oc, 0:N], ugT[:, oc, 0:N], sgc[:, 0:N], ALU.mult, [bugT, bsgc], [bugT])
                yield
        def gates_ab():
            for j, (w0, dstv, dbuf) in enumerate(((C_GA, gT[:, 0], bgT), (C_GA + 128, gT[:, 1], bgT),
                                                  (C_GB, gT[:, 2], bgT), (C_GB + 128, gT[:, 3], bgT))):
                pb = nxt("po")
                proj_fm(pO[pb], bpO[pb], w0, N, n)
                act(dstv[:, 0:N], pO[pb][:, 0:N], AF.Silu, [bpO[pb]], [dbuf])
                yield
        def q_tile(i, sc):
            ti = ti0 + i
            tmA, btmA, tmB, btmB, tmC, btmC, tmD, btmD = sc.tmA, sc.btmA, sc.tmB, sc.btmB, sc.tmC, sc.btmC, sc.tmD, sc.btmD
            tb16, btb16, tT, btT = sc.tb16, sc.btb16, sc.tT, sc.btT
            pT2, bpT2 = sc.pT, sc.bpT
            P_, bP = sc.P, sc.bP
            proj_tm(P_, bP, 0, i, C_CQ, C_CQ + 192)
            proj_tm(P_, bP, 192, i, C_Q2, C_Q2 + 256)
            yield
            s, bs = sc.nsm()
            act(junk[:, 0:192], P_[:, 0:192], AF.Square, [bP], [bjunk, bs], accum_out=s[:, 0:1])
            rsqrt_cols(s[:, 1:2], s[:, 0:1], 1.0 / 192, bs, [])
            ts("dve", tb16[:, 0:192], P_[:, 0:192], s[:, 1:2], None, ALU.mult, None, [bP, bs], [btb16])
            tr(pT2[:, 0:128], tb16[:, 0:128], [btb16], [bpT2])
            tr(pT2[0:64, 128:256], tb16[:, 128:192], [btb16], [bpT2])
            cp("dve", tT[:, 0:128], pT2[:, 0:128], [bpT2], [btT])
            cp("dve", tT[0:64, 128:256], pT2[0:64, 128:256], [bpT2], [btT])
            yield
            Q, bQ = sc.Q, sc.bQ
            mm(Q[:, 0:384], tT[:, 0:128], Wuq[:, 0, :], True, False, [btT, bWuq], [bQ], inc=False)
            mm(Q[:, 0:384], tT[0:64, 128:256], Wuq[0:64, 1, :], False, True, [btT, bWuq], [bQ])
            Q3 = rr(Q[:, 0:384], "p (h d) -> p h d", h=4)
            yield
            sq = rr(tmA[:, 0:384], "p (h d) -> p h d", h=4)
            act(sq, Q3, AF.Square, [bQ], [btmA])
            T.op("dve", lambda e, s=s, sq=sq: e.tensor_reduce(out=s[:, 4:8], in_=sq, axis=AX.X, op=ALU.add),
                 reads=[btmA], writes=[bs])
            rsqrt_cols(s[:, 8:12], s[:, 4:8], 1.0 / 96, bs, [])
            yield
            tmpq = rr(tmB[:, 0:384], "p (h d) -> p h d", h=4)
            tt("dve", tmpq, Q3, bc(gbc[:, GQN:GQN + 96].unsqueeze(1), [128, 4, 96]), ALU.mult, [bQ, bgbc], [btmB])
            yield
            q1tm = rr(tb16[:, 0:384], "p (h d) -> p h d", h=4)
            tt("pool", q1tm[:, :, 0:64], tmpq[:, :, 0:64], bc(s[:, 8:12].unsqueeze(2), [128, 4, 64]), ALU.mult,
               [btmB, bs], [btb16])
            if not is_ctx:
                qr = rr(tmC[:, 0:128], "p (h d) -> p h d", h=4)
                tt("pool", qr, tmpq[:, :, 64:96], bc(s[:, 8:12].unsqueeze(2), [128, 4, 32]), ALU.mult,
                   [btmB, bs], [btmC])
                rope(qr, [(q1tm[:, :, 64:96], slice(0, 4))], ropeA, ti, 4, 32, btmC, [btb16], tmD, btmD)
            else:
                tt("pool", q1tm[:, :, 64:96], tmpq[:, :, 64:96], bc(s[:, 8:12].unsqueeze(2), [128, 4, 32]), ALU.mult,
                   [btmB, bs], [btb16])
            for h in range(4):
                tr(pT2[0:96, 256 + h * 128:384 + h * 128], q1tm[:, h, :], [btb16], [bpT2])
            cp("act", q1T[0:96, :, i * 128:(i + 1) * 128], rr(pT2[0:96, 256:768], "p (h n) -> p h n", h=4),
               [bpT2], [bq1T])
            yield
            q2v = rr(P_[:, 192:448], "p (h d) -> p h d", h=4)
            sq2 = rr(tmA[:, 0:256], "p (h d) -> p h d", h=4)
            act(sq2, q2v, AF.Square, [bP], [btmA])
            s2, bs2 = sc.nsm()
            T.op("dve", lambda e, s2=s2, sq2=sq2: e.tensor_reduce(out=s2[:, 0:4], in_=sq2, axis=AX.X, op=ALU.add),
                 reads=[btmA], writes=[bs2])
            rsqrt_cols(s2[:, 4:8], s2[:, 0:4], 1.0 / 64, bs2, [])
            yield
            tmp2 = rr(tmB[:, 0:256], "p (h d) -> p h d", h=4)
            tt("dve", tmp2, q2v, bc(gbc[:, GQ2:GQ2 + 64].unsqueeze(1), [128, 4, 64]), ALU.mult, [bP, bgbc], [btmB])
            yield
            q2r = rr(tb16[:, 0:256], "p (h d) -> p h d", h=4)
            pieces = [(q2r[:, 0:2, :], slice(0, 4, 2)), (q2r[:, 2:4, :], slice(1, 4, 2))]
            if not is_ctx:
                q2n = rr(tmC[:, 0:256], "p (h d) -> p h d", h=4)
                tt("pool", q2n, tmp2, bc(s2[:, 4:8].unsqueeze(2), [128, 4, 64]), ALU.mult, [btmB, bs2], [btmC])
                rope(q2n, pieces, ropeB, ti, 4, 64, btmC, [btb16], tmD, btmD)
            else:
                for (ov, hs) in pieces:
                    tt("pool", ov, tmp2[:, hs, :], bc(s2[:, 4:8].unsqueeze(2), [128, 4, 64])[:, hs, :], ALU.mult,
                       [btmB, bs2], [btb16])
            for j in range(2):
                tr(pT2[:, 768 + j * 128:896 + j * 128], tb16[:, j * 128:(j + 1) * 128], [btb16], [bpT2])
            cp("act", q2T[:, :, i * 128:(i + 1) * 128], rr(pT2[:, 768:1024], "p (h n) -> p h n", h=2),
               [bpT2], [bq2T])
            yield
            PV, bPV = sc.P, sc.bP
            proj_tm(PV, bPV, 0, i, C_VC, C_VC + 256)
            yield
            T.op("dve", lambda e, s2=s2, PV=PV: e.bn_stats(out=s2[:, 8:14], in_=PV[:, 0:256]), reads=[bPV], writes=[bs2])
            T.op("dve", lambda e, s2=s2: e.bn_aggr(out=s2[:, 14:16], in_=s2[:, 8:14]), reads=[bs2], writes=[bs2])
            rsqrt_cols(s2[:, 16:17], s2[:, 15:16], 1.0, bs2, [])
            yield
            ts("dve", tmA[:, 0:256], PV[:, 0:256], s2[:, 14:15], s2[:, 16:17], ALU.subtract, ALU.mult,
               [bPV, bs2], [btmA])
            tt("pool", tmA[:, 256:512], tmA[:, 0:256], gbc[:, LNG:LNG + 256], ALU.mult, [btmA, bgbc], [btmA])
            tt("pool", tb16[:, 256:512], tmA[:, 256:512], gbc[:, LNB:LNB + 256], ALU.add, [btmA, bgbc], [btb16])
            yield
            PS, bPS = sc.Q, sc.bQ
            for g in range(4):
                mm(PS[:, g * 128:(g + 1) * 128], tb16[:, 256 + (g // 2) * 128:384 + (g // 2) * 128], WsT[:, g, :],
                   True, False, [btb16, bWsT], [bPS], inc=False)
                mm(PS[:, g * 128:(g + 1) * 128], ones_r[0:1, 0:128], brow[0:1, g * 128:(g + 1) * 128],
                   False, True, [bones, bbrow], [bPS], inc=(g == 3))
            for r in range(2):
                r0 = r * 64
                in0 = rr(PS[r0:r0 + 64, :], "p (c g n) -> p c g n", c=2, g=2)[:, :, r, :]
                tt("dve", ycT[r0:r0 + 64, yset, :, i * 128:(i + 1) * 128], in0,
                   ugT[r0:r0 + 64, :, i * 128:(i + 1) * 128], ALU.mult, [bPS, bugT], [byc[yset]])
        def N2_items():
            items = [(True, (lambda sc, i=i: q_tile(i, sc)), 2) for i in range(n)]
            items.insert(min(2, n), (False, gates_ab))
            return items

        v1slot = (0, 64, 192, 256)
        v2slot = (64, 0, 192, 128)
        specs = []
        for h in range(4):
            specs.append((h, (lambda kt, h=h: K1T[0:96, h, kt * 128:(kt + 1) * 128]), q1T[0:96, h, 0:N],
                          V1a, v1slot[h], h // 2, h // 2))
        for h in range(4):
            base = (h // 2) * 64
            specs.append((h, (lambda kt, base=base: K2T[base:base + 64, kt * 128:(kt + 1) * 128]),
                          q2T[base:base + 64, h % 2, 0:N], V2a, v2slot[h], 2 + h // 2, 2 + h // 2))
        its = [(sp, idx, kt) for sp in specs for idx, kt in enumerate(key_tiles)]
        nk = len(key_tiles)
        sbank = {}

        def issue_S(j):
            (h, KTv, qv, Vv, slot0, gate_c, out_c), idx, kt = its[j]
            sbank[j] = SRING[j % 3]
            Sx, bS = sbank[j]
            mm(Sx[:, 0:N], KTv(kt), qv, True, True, [bKV[kt], bq1T, bq2T], [bS])

        def issue_rest(j):
            (h, KTv, qv, Vv, slot0, gate_c, out_c), idx, kt = its[j]
            Sx, bS = sbank.pop(j)
            po = h % 2
            O, bO = pO[po], bpO[po]
            pr = nxt("pr", 4)
            act(probs[pr][:, 0:N], Sx[:, 0:N], AF.Exp, [bS], [bprobs[pr]])
            mm(O[:, 0:N], Vv[:, kt, slot0:slot0 + 128], probs[pr][:, 0:N], idx == 0, idx == nk - 1,
               [bKV[kt], bprobs[pr]], [bO])
            if idx == nk - 1:
                nr = (h % 2) * 64
                sr = 64 - nr
                T.op("dve", lambda e: e.reciprocal(out=rinv[sr:sr + 64, 0:N], in_=O[sr:sr + 64, 0:N]),
                     reads=[bO], writes=[brinv])
                tt("dve", tmpo[nr:nr + 64, 0:N], O[nr:nr + 64, 0:N], rinv[sr:sr + 64, 0:N], ALU.mult,
                   [bO, brinv], [btmpo])
                tt("pool", ccT[nr:nr + 64, out_c, 0:N], tmpo[nr:nr + 64, 0:N], gT[nr:nr + 64, gate_c, 0:N], ALU.mult,
                   [btmpo, bgT], [bcc[out_c]])

        def A():
            issue_S(0)
            if len(its) > 1:
                issue_S(1)
            for j in range(len(its)):
                if j + 2 < len(its):
                    issue_S(j + 2)
                issue_rest(j)
                yield
        def W():
            for i in range(n):
                k = nxt("xt")
                sap, sbuf_ = src_tiles[i]
                T.dma("sp", "xt%d" % k, xt[k][:], sap, reads=[sbuf_], writes=[bxt[k]])
                rk = 0
                for half in range(2):
                    pb = nxt("po"); Wp, bW = pO[pb], bpO[pb]
                    for c in range(8):
                        if c < 4:
                            lhs = ccT[:, c, i * 128:(i + 1) * 128]; rd = [bcc[c]]
                        elif c < 6:
                            lhs = ycT[:, yset, c - 4, i * 128:(i + 1) * 128]; rd = [byc[yset]]
                        else:
                            lhs = ydv[:, c - 6, ydoff + i * 128: ydoff + (i + 1) * 128]; rd = list(ydbufs)
                        mm(Wp[:, 0:512], lhs, Wout[:, c, half * 512:(half + 1) * 512], c == 0, c == 7,
                           rd + [bWout], [bW], inc=(c == 7))
                    tt("dve", res[rk][:, half * 512:(half + 1) * 512], Wp[:, 0:512],
                       gtbc[:, 0, half * 512:(half + 1) * 512], ALU.mult, [bW, bgt[0]], [bres[rk]])
                    yield
                tt("pool", res[rk][:], res[rk][:], xt[k][:], ALU.add, [bres[rk], bxt[k]], [bres[rk]])
                dap, dbuf = dst_tiles[i]
                T.dma("sp", "rs%d" % rk, dap, res[rk][:], reads=[bres[rk]], writes=[dbuf])
                yield

        return N1, N2_items, A, W

    bXd = [[Buf("xd%d_%d" % (b, t)) for t in range(NT)] for b in range(NB)]
    bCd = [[Buf("cd%d_%d" % (b, t)) for t in range(NC)] for b in range(NB)]
    TPB = QB // 128
    for l in range(DEPTH):
        load_layer(l)
        upd = l < DEPTH - 1
        for b in range(NB):
            xs = x_d if l == 0 else out_d
            cs = ctx_d if l == 0 else ctxw_d
            xsrc = [(xs[b, t * 128:(t + 1) * 128, :], bXd[b][t]) for t in range(NT)]
            xdst = [(out_d[b, t * 128:(t + 1) * 128, :], bXd[b][t]) for t in range(NT)]
            csrc = [(cs[b, t * 128:(t + 1) * 128, :], bCd[b][t]) for t in range(NC)]
            cdst = [(ctxw_d[b, t * 128:(t + 1) * 128, :], bCd[b][t]) for t in range(NC)]
            T.dma("sp", "gt", gtbc[:, 0, :], AP(mods_d.tensor, b * 3 * D + 2 * D, [[0, 128], [1, D]]), writes=[bgt[0]])
            for qb in range(NQB):
                phase1_block(xsrc[qb * TPB:(qb + 1) * TPB], b, False, qb * TPB, qb * TPB,
                             ydT[:, :, qb * QB:(qb + 1) * QB], bydT, G, bG, qb * QB)
            phase1_block(csrc, NB, True, NT, 0, ydTc[:, :, :], [bydTc], Gc, bGc, S, need_fm=upd)
            if USE_HALF:
                fourier_half(G, bG, ydT, bydT)
            else:
                fourier(tabL_d, NKB, KB, NT, G, bG, ydT, bydT)
            if upd:
                fourier(tabC_d, 1, CTX, NC, Gc, bGc, ydTc, [bydTc])
            T.barrier()
            allk = list(range(NK))
            blocks = []
            for qb in range(NQB):
                blocks.append((make_block(xsrc[qb * TPB:(qb + 1) * TPB], xdst[qb * TPB:(qb + 1) * TPB], b, False,
                                          qb * TPB, allk, ydT, qb * QB, bydT, qb % 2, qb * QB), False, allk))
            if upd:
                blocks.append((make_block(csrc, cdst, NB, True, 0, list(range(NT, NK)), ydTc, 0, [bydTc], NQB % 2, S), True, list(range(NT, NK))))
            run_gens([(False, blocks[0][0][0])])
            run_gens(blocks[0][0][1]())
            for k in range(len(blocks)):
                (N1_, N2_, A_, W_) = blocks[k][0]
                nb_ = blocks[k + 1] if k + 1 < len(blocks) else None
                nA = 8 * len(blocks[k][2])
                wA = max(1, nA // 56)
                run_gens([(False, A_, wA)] + ([(False, nb_[0][0])] if nb_ else []))
                run_gens([(False, W_)] + (nb_[0][1]() if nb_ else []))
                if nb_ is not None and nb_[1]:
                    T.dma("sp", "gt", gtbc[:, 0, :], AP(mods_d.tensor, NB * 3 * D + 2 * D, [[0, 128], [1, D]]),
                          writes=[bgt[0]])
            T.barrier()
    T.barrier()
    T.emit()
    es.close()
    return nc


def _tables(S, CTX):
    bf = ml_dtypes.bfloat16
    NT, NC = S // 128, CTX // 128
    KB = min(512, S)
    ident = np.eye(128, dtype=np.float32).astype(bf)
    k = np.arange(64)
    ang = 2 * np.pi * np.outer(k, k) / 64.0
    Ec = np.kron(np.eye(4), np.cos(ang)); Es = np.kron(np.eye(4), np.sin(ang))
    ecs = np.concatenate([Ec, Es], 1).astype(np.float32).astype(bf)
    s = np.arange(S)
    row = (s // GRID_W).astype(np.float64); col = (s % GRID_W).astype(np.float64)

    def rt(d):
        inv = THETA ** (-np.arange(0, d, 2, dtype=np.float64) / d)
        outc, outs = [], []
        for pos in (row, col):
            a = pos[:, None] * inv[None, :]
            outc.append(np.concatenate([np.cos(a), np.cos(a)], -1))
            outs.append(np.concatenate([-np.sin(a), np.sin(a)], -1))
        return np.concatenate(outc, -1), np.concatenate(outs, -1)

    def lay(c, sn):
        t = np.stack([c, sn], 1)
        t = t.reshape(NT, 128, 2, -1).transpose(1, 0, 2, 3)
        return np.ascontiguousarray(t).astype(np.float32).astype(bf)

    ropeA = lay(*rt(16)); ropeB = lay(*rt(32))

    def dft(n, kb):
        ss = np.arange(n, dtype=np.int64)
        m = np.outer(ss, ss) % n
        a = 2 * np.pi * m / n
        sc = 1.0 / np.sqrt(n * 64.0)
        t = np.stack([np.cos(a) * sc, -np.sin(a) * sc], 0)
        nkb = n // kb
        t = t.reshape(2, n // 128, 128, nkb, kb).transpose(3, 2, 1, 0, 4)
        return np.ascontiguousarray(t).astype(np.float32).astype(bf)

    def dft_half(n):
        h = n // 2
        kb = min(512, h)
        ss = np.arange(n, dtype=np.int64)
        kk = np.arange(h, dtype=np.int64)
        a = 2 * np.pi * (np.outer(ss, kk) % n) / n
        sc = 1.0 / np.sqrt(n * 64.0)
        t = np.stack([np.cos(a) * sc, np.sin(a) * sc], 0)
        t = t.reshape(2, n // 128, 128, h // kb, kb).transpose(3, 2, 1, 0, 4)
        return np.ascontiguousarray(t).astype(np.float32).astype(bf)

    altc = (((-1.0) ** np.arange(128)) / np.sqrt(S * 64.0)).reshape(128, 1).astype(np.float32).astype(bf)
    d_ = dict(ident=ident, ecs=ecs, ropeA=ropeA, ropeB=ropeB, tabH=dft_half(S), altc=altc, tabC=dft(CTX, CTX))
    if not USE_HALF:
        d_["tabL"] = dft(S, KB)
    return d_


_CACHE = {}


def run(inputs, ncores):
    x = np.asarray(inputs["x"], np.float32)
    B, S, _ = x.shape
    ctx = np.asarray(inputs["ctx"], np.float32)
    CTX = ctx.shape[1]
    NB = B // ncores
    key = (NB, S, CTX)
    if key not in _CACHE:
        _CACHE[key] = (build(NB, S, CTX), _tables(S, CTX))
    nc, tabs = _CACHE[key]
    c = np.asarray(inputs["c"], np.float32)
    c_ctx = np.asarray(inputs["c_ctx"], np.float32)
    shared = {k: np.ascontiguousarray(np.asarray(inputs[k], np.float32)) for k in (
        "norm_g", "w_mod", "b_mod", "w_in", "mla_q_norm", "mla_w_uq", "mla_kv_norm", "mla_w_ukv",
        "mla_qn", "mla_kn", "gqa_qn", "gqa_kn", "cm_ln_g", "cm_ln_b", "cm_w_s", "fnet_w", "w_out")}
    shared["cm_b_s"] = np.ascontiguousarray(np.asarray(inputs["cm_b_s"], np.float32).reshape(DEPTH, 512))
    shared.update(tabs)
    in_maps = []
    for i in range(ncores):
        m = dict(shared)
        m["x"] = np.ascontiguousarray(x[i * NB:(i + 1) * NB])
        m["ctx"] = np.ascontiguousarray(ctx[i * NB:(i + 1) * NB])
        m["c"] = np.ascontiguousarray(np.concatenate([c[i * NB:(i + 1) * NB], c_ctx[None, :]], 0))
        in_maps.append(m)
    r = run_bass_kernel_spmd(nc, in_maps, core_ids=list(range(ncores)))
    return np.concatenate([np.asarray(r.results[i]["out"], np.float32) for i in range(ncores)], 0)


def kernel(**inputs):
    return run(inputs, NCORES)
```

```python
import contextlib
import numpy as np
import ml_dtypes
import concourse.bass as bass
import concourse.mybir as mybir
from concourse.bass_types import AP
from concourse.bass_utils import run_bass_kernel_spmd

F32 = mybir.dt.float32
BF16 = mybir.dt.bfloat16
AF = mybir.ActivationFunctionType
ALU = mybir.AluOpType
AX = mybir.AxisListType

D = 1024
DEPTH = 2
GRID_W = 64
EPS = 1e-6
THETA = 10000.0
NCORES = 8
USE_HALF = True
C_CQ, C_CKV, C_KR, C_GA, C_Q2, C_K2, C_V2, C_GB, C_U, C_VC, C_GC, C_F, C_GD = (
    0, 192, 320, 352, 608, 864, 992, 1120, 1376, 1632, 1888, 2144, 2400)
INW = 2656


class Buf:
    __slots__ = ("name", "w", "r")

    def __init__(self, name):
        self.name = name
        self.w = None
        self.r = []


class Trk:
    def __init__(self, nc, es):
        self.nc = nc
        self.es = es
        self.eng = {"pe": nc.tensor, "act": nc.scalar, "dve": nc.vector,
                    "pool": nc.gpsimd, "sp": nc.sync}
        self.sem = {}
        self.cnt = {}
        self.waited = {k: {} for k in self.eng}
        for k in self.eng:
            self.sem[k] = es.enter_context(nc.semaphore("s_" + k))
            self.cnt[k] = 0
        self.dsem = {}
        self.dcnt = {}
        self.prog = {k: [] for k in self.eng}

    def dma_sem(self, name):
        if name not in self.dsem:
            self.dsem[name] = self.es.enter_context(self.nc.semaphore("d_" + name))
            self.dcnt[name] = 0

    def _sem_of(self, key):
        return self.sem[key] if key in self.sem else self.dsem[key]

    def _wait(self, e, tok):
        key, val = tok
        if key == e and e == "pe":
            return
        if self.waited[e].get(key, 0) >= val:
            return
        self.prog[e].append(("w", self._sem_of(key), val))
        self.waited[e][key] = val

    def _deps(self, e, reads, writes):
        for b in reads:
            if b.w is not None:
                self._wait(e, b.w)
        for b in writes:
            if b.w is not None and b.w[0] != e:
                self._wait(e, b.w)
            for t in b.r:
                if t[0] != e:
                    self._wait(e, t)

    def _mark(self, tok, reads, writes):
        for b in reads:
            b.r.append(tok)
            if len(b.r) > 24:
                b.r = b.r[-24:] if False else b.r
        for b in writes:
            b.w = tok
            b.r = []

    def op(self, e, fn, reads=(), writes=(), inc=True):
        self._deps(e, reads, writes)
        if inc:
            self.cnt[e] += 1
            self.prog[e].append(("i", fn, self.sem[e], 1))
            tok = (e, self.cnt[e])
        else:
            self.prog[e].append(("i", fn, None, 0))
            tok = (e, self.cnt[e] + 1)
        self._mark(tok, reads, writes)
        return tok

    def dma(self, q, semname, out, in_, reads=(), writes=(), **kw):
        semname = semname + "_" + q
        self.dma_sem(semname)
        self._deps(q, reads, writes)
        self.dcnt[semname] += 16
        self.prog[q].append(("i", (lambda e, o=out, i=in_, k=kw: e.dma_start(out=o, in_=i, **k)),
                             self.dsem[semname], 16))
        tok = (semname, self.dcnt[semname])
        self._mark(tok, reads, writes)
        return tok

    def barrier(self):
        keys = [k for k in self.eng]
        for e in keys:
            for f in keys:
                if f != e and self.cnt[f] > 0:
                    self._wait(e, (f, self.cnt[f]))
            for dn in self.dsem:
                if self.dcnt[dn] > 0:
                    self._wait(e, (dn, self.dcnt[dn]))

    def emit(self):
        nc = self.nc
        with nc.Block() as block:
            for k, starter in (("sp", block.sync), ("act", block.scalar), ("dve", block.vector),
                               ("pool", block.gpsimd), ("pe", block.tensor)):
                prog = self.prog[k]
                if not prog:
                    continue

                def body(eng, prog=prog):
                    for it in prog:
                        if it[0] == "w":
                            eng.wait_ge(it[1], it[2])
                        else:
                            inst = it[1](eng)
                            if it[2] is not None:
                                inst.then_inc(it[2], it[3])
                starter(body)


def rr(ap, s, **kw):
    return ap.rearrange(s, **kw)


def build(NB, S, CTX):
    NT, NC = S // 128, CTX // 128
    NK = NT + NC
    KT = NK * 128
    QB = min(512, S)
    NQB = S // QB
    KB = min(512, S)
    NKB = S // KB
    AG = 2
    nc = bass.Bass("TRN2", target_bir_lowering=False)
    dt = nc.dram_tensor

    def din(name, shape, dty=F32):
        return dt(name, list(shape), dty, kind="ExternalInput").ap()

    x_d = din("x", [NB, S, D]); c_d = din("c", [NB + 1, D]); ctx_d = din("ctx", [NB, CTX, D])
    norm_g_d = din("norm_g", [DEPTH, D]); w_mod_d = din("w_mod", [DEPTH, D, 3 * D])
    b_mod_d = din("b_mod", [DEPTH, 3 * D]); w_in_d = din("w_in", [DEPTH, D, INW])
    qnorm_d = din("mla_q_norm", [DEPTH, 192]); wuq_d = din("mla_w_uq", [DEPTH, 192, 384])
    kvnorm_d = din("mla_kv_norm", [DEPTH, 128]); wukv_d = din("mla_w_ukv", [DEPTH, 128, 512])
    mqn_d = din("mla_qn", [DEPTH, 96]); mkn_d = din("mla_kn", [DEPTH, 96])
    gqn_d = din("gqa_qn", [DEPTH, 64]); gkn_d = din("gqa_kn", [DEPTH, 64])
    lng_d = din("cm_ln_g", [DEPTH, 256]); lnb_d = din("cm_ln_b", [DEPTH, 256])
    cmw_d = din("cm_w_s", [DEPTH, 4, 128, 128]); cmb_d = din("cm_b_s", [DEPTH, 512])
    wf_d = din("fnet_w", [DEPTH, 256, 256]); wout_d = din("w_out", [DEPTH, D, D])
    ident_d = din("ident", [128, 128], BF16)
    ecs_d = din("ecs", [256, 512], BF16)
    ropeA_d = din("ropeA", [128, NT, 2, 32], BF16)
    ropeB_d = din("ropeB", [128, NT, 2, 64], BF16)
    KBH = min(512, S // 2)
    if not USE_HALF:
        tabL_d = din("tabL", [NKB, 128, NT, 2, KB], BF16)
    tabH_d = din("tabH", [(S // 2) // KBH, 128, NT, 2, KBH], BF16)
    altc_d = din("altc", [128, 1], BF16)
    tabC_d = din("tabC", [1, 128, NC, 2, CTX], BF16)
    out_d = dt("out", [NB, S, D], F32, kind="ExternalOutput").ap()
    ctxw_d = dt("ctxw", [NB, CTX, D], F32, kind="Internal").ap()
    hxs_d = dt("hxs", [128, 8, S + CTX], BF16, kind="Internal").ap()
    mods_d = dt("mods", [NB + 1, 3 * D], F32, kind="Internal").ap()

    es = contextlib.ExitStack()
    T = Trk(nc, es)

    def sb(name, shape, dty):
        return es.enter_context(nc.sbuf_tensor("sb_" + name, list(shape), dty))

    def psum(name, shape, dty):
        return es.enter_context(nc.psum_tensor("ps_" + name, list(shape), dty))

    Win = sb("Win", [128, 8, INW], BF16); bWin = Buf("Win")
    Wout = sb("Wout", [128, 8, D], BF16); bWout = Buf("Wout")
    Wuq = sb("Wuq", [128, 2, 384], BF16); bWuq = Buf("Wuq")
    Wukv = sb("Wukv", [128, 512], BF16); bWukv = Buf("Wukv")
    WsT = sb("WsT", [128, 4, 128], BF16); bWsT = Buf("WsT")
    brow = sb("brow", [1, 512], BF16); bbrow = Buf("brow")
    ones_r = sb("ones_r", [1, 128], BF16); bones = Buf("ones_r")
    Wcs = sb("Wcs", [128, 2, 512], BF16); bWcs = Buf("Wcs")
    bEcs = Buf("Ecs")
    ident = sb("ident", [128, 128], BF16); bident = Buf("ident")
    gbc = sb("gbc", [128, 832], F32); bgbc = Buf("gbc")
    GQN, GKN, GQ2, GK2, LNG, LNB = 0, 96, 192, 256, 320, 576
    ropeA = sb("ropeA", [128, NT, 2, 32], BF16); ropeB = sb("ropeB", [128, NT, 2, 64], BF16)
    brope = Buf("rope")
    epsT = sb("epsT", [128, 1], F32); beps = Buf("eps")
    identF = sb("identF", [64, 64], F32); bidentF = Buf("identF")
    altc = sb("altc", [128, 1], BF16); baltc = Buf("altc")
    modAB = sb("modAB", [128, NB + 1, 3, 8], F32); bmodAB = Buf("modAB")
    ngcol = sb("ngcol", [128, 8], F32)
    gtbc = sb("gtbc", [128, 1, D], F32); _bgt = Buf("gt0"); bgt = [_bgt, _bgt]
    K1T = sb("K1T", [128, 4, KT], BF16)
    V1a = sb("V1a", [128, NK, 384], BF16)
    K2T = sb("K2T", [128, KT], BF16)
    V2a = sb("V2a", [128, NK, 320], BF16)
    bKV = [Buf("kv%d" % k) for k in range(NK)]
    ydT = sb("ydT", [128, 2, S], BF16); bydT = [Buf("yd%d" % k) for k in range(NKB)]
    ydTc = sb("ydTc", [128, 2, CTX], BF16); bydTc = Buf("ydTc")
    Gc = sb("Gc", [128, NC, 512], BF16); bGc = [Buf("Gc%d" % k) for k in range(NC)]
    xt = [sb("xt%d" % i, [128, D], F32) for i in range(2)]; bxt = [Buf("xt0"), Buf("xt1")]
    _xn = sb("xn0", [128, D], BF16); xn = [_xn, _xn]; _bxn = Buf("xn0"); bxn = [_bxn, _bxn]
    junk = sb("junk", [128, D], BF16); bjunk = Buf("junk")
    st = [sb("st%d" % i, [128, 16], F32) for i in range(2)]; bst = [Buf("st0"), Buf("st1")]
    hxT = sb("hxT", [128, 8, 512], BF16); bhx = [Buf("hx%d" % i) for i in range(4)]; bhxd = [Buf("hxd%d" % i) for i in range(4)]
    bHX = {}
    fT = sb("fT", [128, 2, 512], BF16); bfT = Buf("fT")
    pA = [psum("pA%d" % i, [128, 512], F32) for i in range(2)]; bpA = [Buf("pA0"), Buf("pA1")]
    pB = [psum("pB%d" % i, [128, 512], F32) for i in range(2)]; bpB = [Buf("pB0"), Buf("pB1")]
    pO = [psum("pO%d" % i, [128, 512], F32) for i in range(2)]; bpO = [Buf("pO0"), Buf("pO1")]
    pT = psum("pT", [128, 1024], BF16); bpT = Buf("pT")
    pT2 = psum("pT2", [128, 1024], BF16); bpT2 = Buf("pT2")

    class SC:
        pass
    SCS = []
    for k_ in range(2):
        o_ = SC()
        for nm_ in ("tmA", "tmB", "tmC", "tmD"):
            setattr(o_, nm_, sb("%s_%d" % (nm_, k_), [128, 512], F32)); setattr(o_, "b" + nm_, Buf("%s_%d" % (nm_, k_)))
        o_.tb16 = sb("tb16_%d" % k_, [128, 512], BF16); o_.btb16 = Buf("tb16_%d" % k_)
        o_.tT = sb("tT_%d" % k_, [128, 256], BF16); o_.btT = Buf("tT_%d" % k_)
        o_.sm = [sb("sm%d_%d" % (k_, i), [128, 32], F32) for i in range(2)]
        o_.bsm = [Buf("sm%d_%d" % (k_, i)) for i in range(2)]
        o_.smi = 0

        def nsm(o_=o_):
            i = o_.smi
            o_.smi = 1 - i
            return o_.sm[i], o_.bsm[i]
        o_.nsm = nsm
        SCS.append(o_)
    for k_ in range(2):
        SCS[k_].P, SCS[k_].bP, SCS[k_].Q, SCS[k_].bQ = pA[k_], bpA[k_], pB[k_], bpB[k_]
    SCS[0].pT, SCS[0].bpT, SCS[1].pT, SCS[1].bpT = pT2, bpT2, pT, bpT
    tmA, btmA, tmB, btmB, tmC, btmC, tmD, btmD = (SCS[0].tmA, SCS[0].btmA, SCS[0].tmB, SCS[0].btmB,
                                                    SCS[0].tmC, SCS[0].btmC, SCS[0].tmD, SCS[0].btmD)
    tb16, btb16, tT, btT = SCS[0].tb16, SCS[0].btb16, SCS[0].tT, SCS[0].btT
    sm, bsm = SCS[0].sm, SCS[0].bsm
    rinv, brinv = SCS[1].tmA, SCS[1].btmA
    tmpo, btmpo = tmA, btmA
    _res = sb("res0", [128, D], F32); res = [_res, _res]; bres = [Buf("res0"), Buf("res0b")]
    RS = max(NT * 512 + 2 * AG * 2 * KB, 512 * (4 + 2 + 4 + 2 + 4 + 4 + 4))
    R = sb("R", [128, RS], BF16)
    G = rr(R[:, 0:NT * 512], "p (a n) -> p a n", a=NT); bG = [Buf("G%d" % k) for k in range(NT)]
    tabv = [rr(R[:, NT * 512 + i * AG * 2 * KB: NT * 512 + (i + 1) * AG * 2 * KB],
               "p (a j n) -> p a j n", a=AG, j=2) for i in range(2)]
    btab = [Buf("tab0"), Buf("tab1")]
    o = 0
    q1T = rr(R[:, o:o + 2048], "p (h n) -> p h n", h=4); o += 2048; bq1T = Buf("q1T")
    q2T = rr(R[:, o:o + 1024], "p (h n) -> p h n", h=2); o += 1024; bq2T = Buf("q2T")
    gT = rr(R[:, o:o + 2048], "p (h n) -> p h n", h=4); o += 2048; bgT = Buf("gT")
    ugT = rr(R[:, o:o + 1024], "p (h n) -> p h n", h=2); o += 1024; bugT = Buf("ugT")
    ccT = rr(R[:, o:o + 2048], "p (h n) -> p h n", h=4); o += 2048
    bcc = [Buf("cc%d" % k) for k in range(4)]
    ycT = rr(R[:, o:o + 2048], "p (y h n) -> p y h n", y=2, h=2); o += 2048
    byc = [Buf("yc0"), Buf("yc1")]
    probs = [R[:, o + i * 512: o + (i + 1) * 512] for i in range(4)]; o += 2048
    bprobs = [Buf("pr%d" % i) for i in range(4)]
    sgc = SCS[0].tb16; bsgc = SCS[0].btb16
    Ecs = rr(R[:, 8192:9216], "p (c n) -> p c n", c=2)

    SRING = [(pA[0], bpA[0]), (pA[1], bpA[1]), (pB[0], bpB[0])]
    pT_, bpT_ = pT, bpT
    ctr = {"xt": 0, "st": 0, "pa": 0, "pb": 0, "res": 0, "sm": 0, "tab": 0, "pr": 0, "po": 0}

    def nxt(k, n=2):
        v = ctr[k]
        ctr[k] = (v + 1) % n
        return v

    def mm(out, lhsT, rhs, start, stop, reads, writes, inc=True):
        T.op("pe", lambda e: e.matmul(out, lhsT=lhsT, rhs=rhs, start=start, stop=stop),
             reads=reads, writes=writes, inc=inc)

    def tr(out, in_, reads, writes):
        n = in_.shape[0]
        T.op("pe", lambda e: e.transpose(out, in_, ident[0:n, 0:n]), reads=list(reads) + [bident], writes=writes)

    def act(out, in_, func, reads, writes, **kw):
        T.op("act", lambda e: e.activation(out=out, in_=in_, func=func, **kw), reads=reads, writes=writes)

    def tt(e, out, in0, in1, op, reads, writes):
        T.op(e, lambda en: en.tensor_tensor(out=out, in0=in0, in1=in1, op=op), reads=reads, writes=writes)

    def ts(e, out, in0, s1, s2, op0, op1, reads, writes):
        if op1 is None:
            T.op(e, lambda en: en.tensor_scalar(out=out, in0=in0, scalar1=s1, scalar2=None, op0=op0),
                 reads=reads, writes=writes)
        else:
            T.op(e, lambda en: en.tensor_scalar(out=out, in0=in0, scalar1=s1, scalar2=s2, op0=op0, op1=op1),
                 reads=reads, writes=writes)

    def cp(e, out, in_, reads, writes):
        if e == "act":
            T.op("act", lambda en: en.copy(out=out, in_=in_), reads=reads, writes=writes)
        else:
            T.op(e, lambda en: en.tensor_copy(out=out, in_=in_), reads=reads, writes=writes)

    def rsqrt_cols(dst, src, scale, s_buf, reads):
        act(dst, src, AF.Sqrt, reads=list(reads) + [s_buf, beps], writes=[s_buf], scale=scale, bias=epsT[:, 0:1])
        T.op("dve", lambda e: e.reciprocal(out=dst, in_=dst), reads=[s_buf], writes=[s_buf])

    def bc(ap, shape):
        return ap.to_broadcast(list(shape))

    def rope(xv, outv, tab, ti, H, dh, x_buf, out_bufs, scratch, s_buf):
        half = dh // 4
        t1 = rr(scratch[:, 0:H * dh], "p (h d) -> p h d", h=H)
        t2 = rr(scratch[:, 256:256 + H * dh], "p (h d) -> p h d", h=H)
        cosv = bc(tab[:, ti, 0, :].unsqueeze(1), [128, H, dh])
        tt("pool", t1, xv, cosv, ALU.mult, reads=[x_buf, brope], writes=[s_buf])
        for j in range(2):
            for g in range(2):
                o_ = t2[:, :, g * 2 * half + j * half: g * 2 * half + (j + 1) * half]
                i_ = xv[:, :, g * 2 * half + (1 - j) * half: g * 2 * half + (2 - j) * half]
                s_ = bc(tab[:, ti, 1, g * 2 * half + j * half: g * 2 * half + (j + 1) * half].unsqueeze(1),
                        [128, H, half])
                tt("pool", o_, i_, s_, ALU.mult, reads=[x_buf, brope, s_buf], writes=[s_buf])
        for (ov, hs) in outv:
            tt("pool", ov, t1[:, hs, :], t2[:, hs, :], ALU.add, reads=[s_buf], writes=out_bufs)

    T.dma("sp", "c0", ident[:], ident_d[:, :], writes=[bident])
    T.dma("sp", "c0", altc[:], altc_d[:, :], writes=[baltc])
    T.dma("sp", "c0", ropeA[:], ropeA_d[:, :, :, :], writes=[brope])
    T.dma("sp", "c0", ropeB[:], ropeB_d[:, :, :, :], writes=[brope])
    ctok = ("c0_sp", T.dcnt["c0_sp"])
    for b in (bident, brope, baltc):
        b.w = ctok
    T.op("dve", lambda e: e.memset(epsT[:], EPS), writes=[beps])
    T.op("dve", lambda e: e.tensor_copy(out=identF[:, :], in_=ident[0:64, 0:64]), reads=[bident], writes=[bidentF])
    T.op("dve", lambda e: e.memset(ones_r[:], 1.0), writes=[bones])
    T.op("pool", lambda e: e.memset(V1a[:], 1.0), writes=bKV)
    T.op("pool", lambda e: e.memset(V2a[:], 1.0), writes=bKV)

    def load_layer(l):
        T.barrier()
        wb = [bWin, bWout, bWsT, bbrow, bWcs, bgbc, bWuq, bWukv, bmodAB]
        stg = rr(tb16[:, 0:512], "p (g q) -> p g q", g=4)
        for g in range(4):
            T.dma("pool", "w", stg[:, g, :], cmw_d[l, g, :, :], writes=[btb16])
        T.dma("pool", "w", brow[:], cmb_d[l:l + 1, :], writes=[bbrow])
        wfs = rr(fT[:, :, 0:256], "p c n -> p c n")
        T.dma("pool", "w", wfs, rr(wf_d[l], "(c p) n -> p c n", p=128), writes=[bfT])
        T.dma("sp", "w", Ecs, rr(ecs_d, "(c p) n -> p c n", p=128), writes=[bEcs])
        T.dma("sp", "w", tmA[:, 0:384], wuq_d[l, 0:128, :], writes=[btmA])
        T.dma("sp", "w", tmB[0:64, 0:384], wuq_d[l, 128:192, :], writes=[btmB])
        T.dma("sp", "w", tmC[:, 0:512], wukv_d[l, :, :], writes=[btmC])
        smt = sm[0]
        T.dma("sp", "w", smt[:, 0:1], AP(qnorm_d.tensor, l * 192, [[1, 128], [1, 1]]), writes=[bsm[0]])
        T.dma("sp", "w", smt[0:64, 1:2], AP(qnorm_d.tensor, l * 192 + 128, [[1, 64], [1, 1]]), writes=[bsm[0]])
        T.dma("sp", "w", smt[:, 2:3], AP(kvnorm_d.tensor, l * 128, [[1, 128], [1, 1]]), writes=[bsm[0]])
        for (off, src, n) in ((GQN, mqn_d, 96), (GKN, mkn_d, 96), (GQ2, gqn_d, 64), (GK2, gkn_d, 64),
                              (LNG, lng_d, 256), (LNB, lnb_d, 256)):
            T.dma("sp", "w", gbc[:, off:off + n], AP(src.tensor, l * n, [[0, 128], [1, n]]), writes=[bgbc])
        nrow = 8 * (NB + 2)
        T.dma("sp", "w", tmD[0:8, 0:128], AP(norm_g_d.tensor, l * D, [[128, 8], [1, 128]]), writes=[btmD])
        for v in range(NB + 1):
            T.dma("sp", "w", tmD[8 + 8 * v:16 + 8 * v, 0:128], AP(c_d.tensor, v * D, [[128, 8], [1, 128]]), writes=[btmD])
        cT = rr(tmD[:, 256:256 + 8 * (NB + 1)], "p (c v) -> p c v", c=8)
        NV = NB + 1
        for j_ in range(3):
            T.dma("sp", "w", res[0][32 * j_:32 * j_ + NV, 0:1024],
                  AP(b_mod_d.tensor, l * 3 * D + 1024 * j_, [[0, NV], [1, 1024]]), writes=[bres[0]])
        wtok = ("w_sp", T.dcnt["w_sp"])
        for b in [bgbc, bmodAB, btmA, btmB, btmC, bsm[0], btmD, bres[0], bEcs]:
            b.w = wtok
        wtokp = ("w_pool", T.dcnt["w_pool"])
        for b in [btb16, bbrow, bfT]:
            b.w = wtokp
        pa = nxt("pa")
        T.op("pe", lambda e, pa=pa: e.transpose(pA[pa][:, 0:nrow], tmD[0:nrow, 0:128], identF[0:nrow, 0:nrow]),
             reads=[btmD, bidentF], writes=[bpA[pa]])
        cp("dve", ngcol[:], pA[pa][:, 0:8], [bpA[pa]], [bmodAB])
        cp("dve", cT, rr(pA[pa][:, 8:nrow], "p (v c) -> p c v", c=8), [bpA[pa]], [btmD])
        ts("dve", gbc[:, GQN:GQN + 96], gbc[:, GQN:GQN + 96], 96 ** -0.5, None, ALU.mult, None, [bgbc], [bgbc])
        ts("dve", gbc[:, GQ2:GQ2 + 64], gbc[:, GQ2:GQ2 + 64], 64 ** -0.5, None, ALU.mult, None, [bgbc], [bgbc])
        ts("dve", Wuq[:, 0, :], tmA[:, 0:384], smt[:, 0:1], None, ALU.mult, None, [btmA, bsm[0]], [bWuq])
        ts("dve", Wuq[0:64, 1, :], tmB[0:64, 0:384], smt[0:64, 1:2], None, ALU.mult, None, [btmB, bsm[0]], [bWuq])
        ts("dve", Wukv[:, :], tmC[:, 0:512], smt[:, 2:3], None, ALU.mult, None, [btmC, bsm[0]], [bWukv])
        for g in range(4):
            tr(pT2[:, g * 128:(g + 1) * 128], stg[:, g, :], [btb16], [bpT2])
        cp("dve", WsT[:, :, :], rr(pT2[:, 0:512], "p (g q) -> p g q", g=4), [bpT2], [bWsT])
        for cc in range(2):
            for part in range(2):
                pa = nxt("pa")
                for kc in range(2):
                    mm(pA[pa][:, 0:256], Ecs[:, kc, part * 256 + cc * 128: part * 256 + (cc + 1) * 128],
                       wfs[:, kc, :], kc == 0, kc == 1, [bEcs, bfT], [bpA[pa]])
                cp("dve", Wcs[:, cc, part * 256:(part + 1) * 256], pA[pa][:, 0:256], [bpA[pa]], [bWcs])
        scT = rr(tb16[:, 0:8 * (NB + 1)], "p (c v) -> p c v", c=8)
        act(scT, cT, AF.Silu, [btmD, bpT2], [btb16])
        stg_t = [(xt[0], bxt[0]), (xt[1], bxt[1]), (gtbc[:, 0, :], bgt[0])]
        cast_e = ["act", "act", "act"]
        pieces = []
        for c in range(8):
            for (c0, c1) in ((0, 1024), (1024, 2048), (2048, INW)):
                pieces.append((Win[:, c, c0:c1], w_in_d[l, c * 128:(c + 1) * 128, c0:c1], c1 - c0, bWin))
        for c in range(8):
            pieces.append((Wout[:, c, :], wout_d[l, c * 128:(c + 1) * 128, :], D, bWout))
        for pi, (dst, src, wdt, wbuf) in enumerate(pieces):
            st_, bst_ = stg_t[pi % 3]
            T.dma("sp", "sg%d" % (pi % 3), st_[:, 0:wdt], src, writes=[bst_])
            cp(cast_e[pi % 3], dst, st_[:, 0:wdt], [bst_], [wbuf])
        wm = [rr(R[:, i * 4096:(i + 1) * 4096], "p (c n) -> p c n", c=8) for i in range(2)]
        bwm = [Buf("wm0"), Buf("wm1")]
        for nb_ in range(6):
            k = nb_ % 2
            T.dma("pool", "wm%d" % k, wm[k], rr(w_mod_d[l][:, nb_ * 512:(nb_ + 1) * 512], "(c p) n -> p c n", p=128),
                  writes=[bwm[k]])
            pa = nxt("pa")
            for kc in range(8):
                mm(pA[pa][0:NV, :], scT[:, kc, 0:NV], wm[k][:, kc, :], kc == 0, kc == 7,
                   [btb16, bwm[k]], [bpA[pa]], inc=(kc == 7))
            sl = nxt("sm")
            mrow = tmpo if sl == 0 else rinv
            bmrow = btmpo if sl == 0 else brinv
            tt("dve", mrow[0:NV, :], pA[pa][0:NV, :],
               res[0][32 * (nb_ // 2):32 * (nb_ // 2) + NV, (nb_ % 2) * 512:(nb_ % 2 + 1) * 512], ALU.add,
               [bpA[pa], bres[0]], [bmrow])
            T.dma("pool", "mo%d" % sl, mods_d[0:NV, nb_ * 512:(nb_ + 1) * 512], mrow[0:NV, :], reads=[bmrow])
        T.barrier()
        nr2 = 16 * (NB + 1)
        for v in range(NB + 1):
            T.dma("sp", "mi", tmD[16 * v:16 * v + 16, 0:128], AP(mods_d.tensor, v * 3 * D, [[128, 16], [1, 128]]), writes=[btmD])
        btmD.w = ("mi_sp", T.dcnt["mi_sp"])
        pa = nxt("pa")
        T.op("pe", lambda e, pa=pa: e.transpose(pA[pa][:, 0:nr2], tmD[0:nr2, 0:128], identF[0:nr2, 0:nr2]),
             reads=[btmD, bidentF], writes=[bpA[pa]])
        cp("dve", modAB[:, :, 1:3, :], rr(pA[pa][:, 0:nr2], "p (v j c) -> p v j c", j=2, c=8), [bpA[pa]], [bmodAB])
        for v in range(NB + 1):
            ts("dve", modAB[:, v, 2, :], modAB[:, v, 2, :], 1.0, None, ALU.add, None, [bmodAB], [bmodAB])
            tt("dve", modAB[:, v, 0, :], modAB[:, v, 2, :], ngcol[:], ALU.mult, [bmodAB], [bmodAB])
        T.barrier()

    def norm_load(src_ap, src_buf):
        k = nxt("xt")
        T.dma("sp", "xt%d" % k, xt[k][:], src_ap, reads=[src_buf], writes=[bxt[k]])
        return k

    def norm_tile_g(src_ap, src_buf, v, slot, k=None, after_xn=None, tb=None, btb=None):
        pT, bpT = (pT_, bpT_) if tb is None else (tb, btb)
        if k is None:
            k = norm_load(src_ap, src_buf)
        s = st[k]
        act(junk[:], xt[k][:], AF.Square, [bxt[k]], [bjunk, bst[k]], accum_out=s[:, 0:1])
        rsqrt_cols(s[:, 1:2], s[:, 0:1], 1.0 / D, bst[k], [])
        yield
        ts("dve", xn[k][:], xt[k][:], s[:, 1:2], None, ALU.mult, None, [bxt[k], bst[k]], [bxn[k]])
        if after_xn is not None:
            after_xn()
        for c in range(8):
            tr(pT[:, c * 128:(c + 1) * 128], xn[k][:, c * 128:(c + 1) * 128], [bxn[k]], [bpT])
        yield
        for c in range(8):
            o_ = hxT[:, c, slot * 128:(slot + 1) * 128]
            i_ = pT[:, c * 128:(c + 1) * 128]
            if c % 2 == 0:
                ts("dve", o_, i_, modAB[:, v, 0, c:c + 1], modAB[:, v, 1, c:c + 1], ALU.mult, ALU.add,
                   [bpT, bmodAB], [bhx[slot], bhxd[slot]])
            else:
                act(o_, i_, AF.Identity, [bpT, bmodAB] + ([bhxd[slot]] if c == 7 else []), [bhx[slot]],
                    scale=modAB[:, v, 0, c:c + 1], bias=modAB[:, v, 1, c:c + 1])
                yield

    def norm_tile(src_ap, src_buf, v, slot):
        for _ in norm_tile_g(src_ap, src_buf, v, slot):
            pass

    def proj_tm(pbank, pbuf, col0, slot, w0, w1):
        n = w1 - w0
        for c in range(8):
            mm(pbank[:, col0:col0 + n], hxT[:, c, slot * 128:(slot + 1) * 128], Win[:, c, w0:w1],
               c == 0, c == 7, [bhx[slot], bWin], [pbuf], inc=(c == 7))

    def proj_fm(pbank, pbuf, w0, N, nslots):
        for c in range(8):
            mm(pbank[:, 0:N], Win[:, c, w0:w0 + 128], hxT[:, c, 0:N], c == 0, c == 7,
               [bWin] + bhx[0:nslots], [pbuf], inc=(c == 7))

    def kv_tile(slot, kt, is_ctx, ti, sc):
        tmA, btmA, tmB, btmB, tmC, btmC, tmD, btmD = sc.tmA, sc.btmA, sc.tmB, sc.btmB, sc.tmC, sc.btmC, sc.tmD, sc.btmD
        tb16, btb16, tT, btT = sc.tb16, sc.btb16, sc.tT, sc.btT
        pT2, bpT2 = sc.pT, sc.bpT
        P_, bP = sc.P, sc.bP
        proj_tm(P_, bP, 0, slot, C_CKV, C_KR + 32)
        proj_tm(P_, bP, 160, slot, C_K2, C_V2 + 128)
        yield
        s, bs = sc.nsm()
        act(junk[:, 0:128], P_[:, 0:128], AF.Square, [bP], [bjunk, bs], accum_out=s[:, 0:1])
        rsqrt_cols(s[:, 1:2], s[:, 0:1], 1.0 / 128, bs, [])
        ts("dve", tb16[:, 0:128], P_[:, 0:128], s[:, 1:2], None, ALU.mult, None, [bP, bs], [btb16])
        tr(pT2[:, 0:128], tb16[:, 0:128], [btb16], [bpT2])
        cp("dve", tT[:, 0:128], pT2[:, 0:128], [bpT2], [btT])
        yield
        KV, bKVp = sc.Q, sc.bQ
        mm(KV[:, 0:512], tT[:, 0:128], Wukv[:, :], True, True, [btT, bWukv], [bKVp])
        KV3 = rr(KV[:, 0:512], "p (h d) -> p h d", h=4)
        yield
        V1k = V1a[:, kt, :]
        cp("act", rr(V1k[:, 0:384], "p (a n) -> p a n", a=2)[:, :, 0:64], KV3[:, 0:4:2, 64:128], [bKVp], [bKV[kt]])
        cp("act", rr(V1k[:, 0:384], "p (a n) -> p a n", a=2)[:, :, 128:192], KV3[:, 1:4:2, 64:128], [bKVp], [bKV[kt]])
        sq = rr(tmA[:, 0:256], "p (h d) -> p h d", h=4)
        act(sq, KV3[:, :, 0:64], AF.Square, [bKVp], [btmA])
        T.op("dve", lambda e: e.tensor_reduce(out=s[:, 4:8], in_=sq, axis=AX.X, op=ALU.add), reads=[btmA], writes=[bs])
        act(junk[:, 128:160], P_[:, 128:160], AF.Square, [bP], [bjunk, bs], accum_out=s[:, 2:3])
        ts("dve", s[:, 8:12], s[:, 4:8], s[:, 2:3], None, ALU.add, None, [bs], [bs])
        rsqrt_cols(s[:, 12:16], s[:, 8:12], 1.0 / 96, bs, [])
        yield
        k1tm = rr(tb16[:, 128:512], "p (h d) -> p h d", h=4)
        tmp1 = rr(tmB[:, 0:256], "p (h d) -> p h d", h=4)
        tt("dve", tmp1, KV3[:, :, 0:64], bc(gbc[:, GKN:GKN + 64].unsqueeze(1), [128, 4, 64]), ALU.mult,
           [bKVp, bgbc], [btmB])
        yield
        tt("pool", k1tm[:, :, 0:64], tmp1, bc(s[:, 12:16].unsqueeze(2), [128, 4, 64]), ALU.mult,
           [btmB, bs], [btb16])
        krg = rr(tmC[:, 0:32], "p (h d) -> p h d", h=1)
        tt("dve", krg, rr(P_[:, 128:160], "p (h d) -> p h d", h=1), gbc[:, GKN + 64:GKN + 96].unsqueeze(1), ALU.mult,
           [bP, bgbc], [btmC])
        if not is_ctx:
            krr = rr(tmC[:, 32:64], "p (h d) -> p h d", h=1)
            rope(krg, [(krr, slice(0, 1))], ropeA, ti, 1, 32, btmC, [btmC], tmD, btmD)
        else:
            krr = krg
        tt("pool", k1tm[:, :, 64:96], bc(krr, [128, 4, 32]), bc(s[:, 12:16].unsqueeze(2), [128, 4, 32]), ALU.mult,
           [btmC, bs], [btb16])
        for h in range(4):
            tr(pT2[0:96, 128 + h * 128:256 + h * 128], k1tm[:, h, :], [btb16], [bpT2])
        cp("dve", K1T[0:96, :, kt * 128:(kt + 1) * 128], rr(pT2[0:96, 128:640], "p (h n) -> p h n", h=4),
           [bpT2], [bKV[kt]])
        yield
        k2v = rr(P_[:, 160:288], "p (h d) -> p h d", h=2)
        sq2 = rr(tmA[:, 256:384], "p (h d) -> p h d", h=2)
        act(sq2, k2v, AF.Square, [bP], [btmA])
        s2, bs2 = sc.nsm()
        T.op("dve", lambda e: e.tensor_reduce(out=s2[:, 0:2], in_=sq2, axis=AX.X, op=ALU.add), reads=[btmA], writes=[bs2])
        rsqrt_cols(s2[:, 2:4], s2[:, 0:2], 1.0 / 64, bs2, [])
        yield
        tmp2 = rr(tmB[:, 256:384], "p (h d) -> p h d", h=2)
        tt("dve", tmp2, k2v, bc(gbc[:, GK2:GK2 + 64].unsqueeze(1), [128, 2, 64]), ALU.mult, [bP, bgbc], [btmB])
        k2r = rr(tT[:, 128:256], "p (h d) -> p h d", h=2)
        if not is_ctx:
            k2n = rr(tmC[:, 128:256], "p (h d) -> p h d", h=2)
            tt("pool", k2n, tmp2, bc(s2[:, 2:4].unsqueeze(2), [128, 2, 64]), ALU.mult, [btmB, bs2], [btmC])
            rope(k2n, [(k2r, slice(0, 2))], ropeB, ti, 2, 64, btmC, [btT], tmD, btmD)
        else:
            tt("pool", k2r, tmp2, bc(s2[:, 2:4].unsqueeze(2), [128, 2, 64]), ALU.mult, [btmB, bs2], [btT])
        tr(pT2[:, 640:768], tT[:, 128:256], [btT], [bpT2])
        yield
        cp("act", K2T[:, kt * 128:(kt + 1) * 128], pT2[:, 640:768], [bpT2], [bKV[kt]])
        cp("act", rr(V2a[:, kt, 64:320], "p (a n) -> p a n", a=2)[:, :, 0:64],
           rr(P_[:, 288:416], "p (h d) -> p h d", h=2), [bP], [bKV[kt]])

    def g_tile(slot, Gdst, Gbuf):
        pa = nxt("pa")
        for kc in range(2):
            mm(pA[pa][:, 0:512], fT[:, kc, slot * 128:(slot + 1) * 128], Wcs[:, kc, :], kc == 0, kc == 1,
               [bfT, bWcs], [bpA[pa]])
        cp("act", Gdst, pA[pa][:, 0:512], [bpA[pa]], [Gbuf])

    def run_gens(items):
        pending = list(items)
        active = []
        free_sets = [SCS[0], SCS[1]]
        while pending or active:
            k = 0
            while k < len(pending):
                needs, f = pending[k][0], pending[k][1]
                wgt = pending[k][2] if len(pending[k]) > 2 else 1
                rdy = pending[k][3] if len(pending[k]) > 3 else None
                if rdy is not None and not rdy():
                    k += 1
                    continue
                if needs:
                    if free_sets:
                        sc = free_sets.pop(0)
                        active.append((f(sc), sc, wgt))
                        pending.pop(k)
                        continue
                    k += 1
                else:
                    active.append((f(), None, wgt))
                    pending.pop(k)
                    continue
            for it in list(active):
                try:
                    for _ in range(it[2]):
                        next(it[0])
                except StopIteration:
                    active.remove(it)
                    if it[1] is not None:
                        free_sets.append(it[1])

    def phase1_block(src_tiles, v, is_ctx, kt0, ti0, yd_dst, yd_bufs, Gv, Gb, hxcol, need_fm=True):
        n = len(src_tiles)
        N = n * 128
        pTn = pO[1][:, :].bitcast(BF16)
        done = [0]

        def norm_all():
            ks = {}
            for i in range(min(2, n)):
                ks[i] = norm_load(*src_tiles[i])
            for i, (sap, sbuf_) in enumerate(src_tiles):
                def pre(i=i):
                    if i + 2 < n:
                        ks[i + 2] = norm_load(*src_tiles[i + 2])
                for _ in norm_tile_g(sap, sbuf_, v, i, k=ks[i], after_xn=pre, tb=pTn, btb=bpO[1]):
                    yield
                done[0] = i + 1
            if need_fm:
                hb = bHX.setdefault(hxcol, Buf("HX%d" % hxcol))
                T.dma("sp", "hs", hxs_d[:, :, hxcol:hxcol + N], hxT[:, :, 0:N], reads=[bhx[i_] for i_ in range(n)],
                      writes=[hb])

        def fm_gen():
            for oc in range(2):
                pb = 0
                proj_fm(pO[pb], bpO[pb], C_F + oc * 128, N, n)
                cp("dve", fT[:, oc, 0:N], pO[pb][:, 0:N], [bpO[pb]], [bfT])
                yield
            for oc in range(2):
                pb = 0
                proj_fm(pO[pb], bpO[pb], C_GD + oc * 128, N, n)
                act(yd_dst[:, oc, :], pO[pb][:, 0:N], AF.Silu, [bpO[pb]], yd_bufs)
                yield
        items = [(False, norm_all)]
        items += [(True, (lambda sc, i=i: kv_tile(i, kt0 + i, is_ctx, ti0 + i, sc)), 1, (lambda i=i: done[0] > i))
                  for i in range(n)]
        if need_fm:
            items.append((False, fm_gen, 1, (lambda: done[0] >= n)))
        run_gens(items)
        if need_fm:
            for i in range(n):
                g_tile(i, Gv[:, ti0 + i if not is_ctx else i, :], Gb[ti0 + i if not is_ctx else i])

    def fourier(tab_d, nkb, kb_n, na, Gv, Gb, ydv, ydbufs):
        ag = min(AG, na)
        for kb in range(nkb):
            P0, P1 = pO[0], pO[1]
            for a0 in range(0, na, ag):
                k = nxt("tab")
                tv = tabv[k] if kb_n == KB else rr(R[:, NT * 512 + k * AG * 2 * KB: NT * 512 + k * AG * 2 * KB + ag * 2 * kb_n],
                                                   "p (a j n) -> p a j n", a=ag, j=2)
                T.dma("sp", "tb%d" % k, tv[:, 0:ag, :, :], tab_d[kb, :, a0:a0 + ag, :, :], writes=[btab[k]])
                for a in range(a0, a0 + ag):
                    for cc, (PP, bPP) in enumerate(((P0, bpO[0]), (P1, bpO[1]))):
                        mm(PP[:, 0:kb_n], Gv[:, a, cc * 128:(cc + 1) * 128], tv[:, a - a0, 0, :],
                           a == 0, False, [Gb[a], btab[k]], [bPP], inc=False)
                        mm(PP[:, 0:kb_n], Gv[:, a, 256 + cc * 128:256 + (cc + 1) * 128], tv[:, a - a0, 1, :],
                           False, a == na - 1, [Gb[a], btab[k]], [bPP])
            for cc in range(2):
                tt("dve", ydv[:, cc, kb * kb_n:(kb + 1) * kb_n], pO[cc][:, 0:kb_n],
                   ydv[:, cc, kb * kb_n:(kb + 1) * kb_n], ALU.mult, [bpO[cc], ydbufs[kb]], [ydbufs[kb]])

    def fourier_half(Gv, Gb, ydv, ydbufs):
        Hh = S // 2
        KBh = min(512, Hh)
        nkb = Hh // KBh
        ag = min(AG, NT)
        acc = [(pO[0], bpO[0]), (pO[1], bpO[1]), (pA[0], bpA[0]), (pA[1], bpA[1])]
        for kb in range(nkb):
            for a0 in range(0, NT, ag):
                k = nxt("tab")
                base = NT * 512 + k * AG * 2 * KB
                tv = rr(R[:, base: base + ag * 2 * KBh], "p (a j n) -> p a j n", a=ag, j=2)
                T.dma("sp", "tb%d" % k, tv[:, 0:ag, :, :], tabH_d[kb, :, a0:a0 + ag, :, :], writes=[btab[k]])
                for a in range(a0, a0 + ag):
                    for part in range(2):
                        for cc in range(2):
                            PP, bPP = acc[part * 2 + cc]
                            mm(PP[:, 0:KBh], Gv[:, a, part * 256 + cc * 128: part * 256 + (cc + 1) * 128],
                               tv[:, a - a0, part, :], a == 0, a == NT - 1, [Gb[a], btab[k]], [bPP])
            k0 = kb * KBh
            for cc in range(2):
                sc = SCS[cc]
                Pc, bPc = acc[cc]
                Ps, bPs = acc[2 + cc]
                cp("act", sc.tmA[:, 0:KBh], Ps[:, 0:KBh], [bPs], [sc.btmA])
                tt("dve", sc.tmB[:, 0:KBh], Pc[:, 0:KBh], sc.tmA[:, 0:KBh], ALU.subtract, [bPc, sc.btmA], [sc.btmB])
                tt("dve", sc.tmC[:, 0:KBh], Pc[:, 0:KBh], sc.tmA[:, 0:KBh], ALU.add, [bPc, sc.btmA], [sc.btmC])
                fwd = ydv[:, cc, k0:k0 + KBh]
                tt("pool", fwd, sc.tmB[:, 0:KBh], fwd, ALU.mult, [sc.btmB] + list(ydbufs), list(ydbufs))
                j0 = 1 if kb == 0 else 0
                cnt = KBh - j0
                hi = S - k0 - j0
                rev = ydv[:, cc, hi:hi - cnt:-1]
                tt("dve", rev, sc.tmC[:, j0:KBh], rev, ALU.mult, [sc.btmC] + list(ydbufs), list(ydbufs))
        for cc in range(2):
            for a in range(NT):
                mm(pB[0][:, cc:cc + 1], Gv[:, a, cc * 128:(cc + 1) * 128], altc[:, 0:1], a == 0, a == NT - 1,
                   [Gb[a], baltc], [bpB[0]])
        for cc in range(2):
            tt("dve", ydv[:, cc, Hh:Hh + 1], pB[0][:, cc:cc + 1], ydv[:, cc, Hh:Hh + 1], ALU.mult,
               [bpB[0]] + list(ydbufs), list(ydbufs))

    def make_block(src_tiles, dst_tiles, v, is_ctx, ti0, key_tiles, ydv, ydoff, ydbufs, yset, hxcol):
        n = len(src_tiles)
        N = n * 128

        def N1():
            T.dma("sp", "hl", hxT[:, :, 0:N], hxs_d[:, :, hxcol:hxcol + N], reads=[bHX[hxcol]],
                  writes=[bhx[i_] for i_ in range(n)])
            yield
            for oc in range(2):
                pb = 1
                proj_fm(pB[pb], bpB[pb], C_U + oc * 128, N, n)
                cp("dve", ugT[:, oc, 0:N], pB[pb][:, 0:N], [bpB[pb]], [bugT])
                yield
                pb = 1
                proj_fm(pB[pb], bpB[pb], C_GC + oc * 128, N, n)
                act(sgc[:, 0:N], pB[pb][:, 0:N], AF.Silu, [bpB[pb]], [bsgc])
                tt("pool", ugT[:, oc, 0:N], ugT[:, oc, 0:N], sgc[:, 0:N], ALU.mult, [bugT, bsgc], [bugT])
                yield
        def gates_ab():
            for j, (w0, dstv, dbuf) in enumerate(((C_GA, gT[:, 0], bgT), (C_GA + 128, gT[:, 1], bgT),
                                                  (C_GB, gT[:, 2], bgT), (C_GB + 128, gT[:, 3], bgT))):
                pb = nxt("po")
                proj_fm(pO[pb], bpO[pb], w0, N, n)
                act(dstv[:, 0:N], pO[pb][:, 0:N], AF.Silu, [bpO[pb]], [dbuf])
                yield
        def q_tile(i, sc):
            ti = ti0 + i
            tmA, btmA, tmB, btmB, tmC, btmC, tmD, btmD = sc.tmA, sc.btmA, sc.tmB, sc.btmB, sc.tmC, sc.btmC, sc.tmD, sc.btmD
            tb16, btb16, tT, btT = sc.tb16, sc.btb16, sc.tT, sc.btT
            pT2, bpT2 = sc.pT, sc.bpT
            P_, bP = sc.P, sc.bP
            proj_tm(P_, bP, 0, i, C_CQ, C_CQ + 192)
            proj_tm(P_, bP, 192, i, C_Q2, C_Q2 + 256)
            yield
            s, bs = sc.nsm()
            act(junk[:, 0:192], P_[:, 0:192], AF.Square, [bP], [bjunk, bs], accum_out=s[:, 0:1])
            rsqrt_cols(s[:, 1:2], s[:, 0:1], 1.0 / 192, bs, [])
            ts("dve", tb16[:, 0:192], P_[:, 0:192], s[:, 1:2], None, ALU.mult, None, [bP, bs], [btb16])
            tr(pT2[:, 0:128], tb16[:, 0:128], [btb16], [bpT2])
            tr(pT2[0:64, 128:256], tb16[:, 128:192], [btb16], [bpT2])
            cp("dve", tT[:, 0:128], pT2[:, 0:128], [bpT2], [btT])
            cp("dve", tT[0:64, 128:256], pT2[0:64, 128:256], [bpT2], [btT])
            yield
            Q, bQ = sc.Q, sc.bQ
            mm(Q[:, 0:384], tT[:, 0:128], Wuq[:, 0, :], True, False, [btT, bWuq], [bQ], inc=False)
            mm(Q[:, 0:384], tT[0:64, 128:256], Wuq[0:64, 1, :], False, True, [btT, bWuq], [bQ])
            Q3 = rr(Q[:, 0:384], "p (h d) -> p h d", h=4)
            yield
            sq = rr(tmA[:, 0:384], "p (h d) -> p h d", h=4)
            act(sq, Q3, AF.Square, [bQ], [btmA])
            T.op("dve", lambda e, s=s, sq=sq: e.tensor_reduce(out=s[:, 4:8], in_=sq, axis=AX.X, op=ALU.add),
                 reads=[btmA], writes=[bs])
            rsqrt_cols(s[:, 8:12], s[:, 4:8], 1.0 / 96, bs, [])
            yield
            tmpq = rr(tmB[:, 0:384], "p (h d) -> p h d", h=4)
            tt("dve", tmpq, Q3, bc(gbc[:, GQN:GQN + 96].unsqueeze(1), [128, 4, 96]), ALU.mult, [bQ, bgbc], [btmB])
            yield
            q1tm = rr(tb16[:, 0:384], "p (h d) -> p h d", h=4)
            tt("pool", q1tm[:, :, 0:64], tmpq[:, :, 0:64], bc(s[:, 8:12].unsqueeze(2), [128, 4, 64]), ALU.mult,
               [btmB, bs], [btb16])
            if not is_ctx:
                qr = rr(tmC[:, 0:128], "p (h d) -> p h d", h=4)
                tt("pool", qr, tmpq[:, :, 64:96], bc(s[:, 8:12].unsqueeze(2), [128, 4, 32]), ALU.mult,
                   [btmB, bs], [btmC])
                rope(qr, [(q1tm[:, :, 64:96], slice(0, 4))], ropeA, ti, 4, 32, btmC, [btb16], tmD, btmD)
            else:
                tt("pool", q1tm[:, :, 64:96], tmpq[:, :, 64:96], bc(s[:, 8:12].unsqueeze(2), [128, 4, 32]), ALU.mult,
                   [btmB, bs], [btb16])
            for h in range(4):
                tr(pT2[0:96, 256 + h * 128:384 + h * 128], q1tm[:, h, :], [btb16], [bpT2])
            cp("act", q1T[0:96, :, i * 128:(i + 1) * 128], rr(pT2[0:96, 256:768], "p (h n) -> p h n", h=4),
               [bpT2], [bq1T])
            yield
            q2v = rr(P_[:, 192:448], "p (h d) -> p h d", h=4)
            sq2 = rr(tmA[:, 0:256], "p (h d) -> p h d", h=4)
            act(sq2, q2v, AF.Square, [bP], [btmA])
            s2, bs2 = sc.nsm()
            T.op("dve", lambda e, s2=s2, sq2=sq2: e.tensor_reduce(out=s2[:, 0:4], in_=sq2, axis=AX.X, op=ALU.add),
                 reads=[btmA], writes=[bs2])
            rsqrt_cols(s2[:, 4:8], s2[:, 0:4], 1.0 / 64, bs2, [])
            yield
            tmp2 = rr(tmB[:, 0:256], "p (h d) -> p h d", h=4)
            tt("dve", tmp2, q2v, bc(gbc[:, GQ2:GQ2 + 64].unsqueeze(1), [128, 4, 64]), ALU.mult, [bP, bgbc], [btmB])
            yield
            q2r = rr(tb16[:, 0:256], "p (h d) -> p h d", h=4)
            pieces = [(q2r[:, 0:2, :], slice(0, 4, 2)), (q2r[:, 2:4, :], slice(1, 4, 2))]
            if not is_ctx:
                q2n = rr(tmC[:, 0:256], "p (h d) -> p h d", h=4)
                tt("pool", q2n, tmp2, bc(s2[:, 4:8].unsqueeze(2), [128, 4, 64]), ALU.mult, [btmB, bs2], [btmC])
                rope(q2n, pieces, ropeB, ti, 4, 64, btmC, [btb16], tmD, btmD)
            else:
                for (ov, hs) in pieces:
                    tt("pool", ov, tmp2[:, hs, :], bc(s2[:, 4:8].unsqueeze(2), [128, 4, 64])[:, hs, :], ALU.mult,
                       [btmB, bs2], [btb16])
            for j in range(2):
                tr(pT2[:, 768 + j * 128:896 + j * 128], tb16[:, j * 128:(j + 1) * 128], [btb16], [bpT2])
            cp("act", q2T[:, :, i * 128:(i + 1) * 128], rr(pT2[:, 768:1024], "p (h n) -> p h n", h=2),
               [bpT2], [bq2T])
            yield
            PV, bPV = sc.P, sc.bP
            proj_tm(PV, bPV, 0, i, C_VC, C_VC + 256)
            yield
            T.op("dve", lambda e, s2=s2, PV=PV: e.bn_stats(out=s2[:, 8:14], in_=PV[:, 0:256]), reads=[bPV], writes=[bs2])
            T.op("dve", lambda e, s2=s2: e.bn_aggr(out=s2[:, 14:16], in_=s2[:, 8:14]), reads=[bs2], writes=[bs2])
            rsqrt_cols(s2[:, 16:17], s2[:, 15:16], 1.0, bs2, [])
            yield
            ts("dve", tmA[:, 0:256], PV[:, 0:256], s2[:, 14:15], s2[:, 16:17], ALU.subtract, ALU.mult,
               [bPV, bs2], [btmA])
            tt("pool", tmA[:, 256:512], tmA[:, 0:256], gbc[:, LNG:LNG + 256], ALU.mult, [btmA, bgbc], [btmA])
            tt("pool", tb16[:, 256:512], tmA[:, 256:512], gbc[:, LNB:LNB + 256], ALU.add, [btmA, bgbc], [btb16])
            yield
            PS, bPS = sc.Q, sc.bQ
            for g in range(4):
                mm(PS[:, g * 128:(g + 1) * 128], tb16[:, 256 + (g // 2) * 128:384 + (g // 2) * 128], WsT[:, g, :],
                   True, False, [btb16, bWsT], [bPS], inc=False)
                mm(PS[:, g * 128:(g + 1) * 128], ones_r[0:1, 0:128], brow[0:1, g * 128:(g + 1) * 128],
                   False, True, [bones, bbrow], [bPS], inc=(g == 3))
            for r in range(2):
                r0 = r * 64
                in0 = rr(PS[r0:r0 + 64, :], "p (c g n) -> p c g n", c=2, g=2)[:, :, r, :]
                tt("dve", ycT[r0:r0 + 64, yset, :, i * 128:(i + 1) * 128], in0,
                   ugT[r0:r0 + 64, :, i * 128:(i + 1) * 128], ALU.mult, [bPS, bugT], [byc[yset]])
        def N2_items():
            items = [(True, (lambda sc, i=i: q_tile(i, sc)), 2) for i in range(n)]
            items.insert(min(2, n), (False, gates_ab))
            return items

        v1slot = (0, 64, 192, 256)
        v2slot = (64, 0, 192, 128)
        specs = []
        for h in range(4):
            specs.append((h, (lambda kt, h=h: K1T[0:96, h, kt * 128:(kt + 1) * 128]), q1T[0:96, h, 0:N],
                          V1a, v1slot[h], h // 2, h // 2))
        for h in range(4):
            base = (h // 2) * 64
            specs.append((h, (lambda kt, base=base: K2T[base:base + 64, kt * 128:(kt + 1) * 128]),
                          q2T[base:base + 64, h % 2, 0:N], V2a, v2slot[h], 2 + h // 2, 2 + h // 2))
        its = [(sp, idx, kt) for sp in specs for idx, kt in enumerate(key_tiles)]
        nk = len(key_tiles)
        sbank = {}

        def issue_S(j):
            (h, KTv, qv, Vv, slot0, gate_c, out_c), idx, kt = its[j]
            sbank[j] = SRING[j % 3]
            Sx, bS = sbank[j]
            mm(Sx[:, 0:N], KTv(kt), qv, True, True, [bKV[kt], bq1T, bq2T], [bS])

        def issue_rest(j):
            (h, KTv, qv, Vv, slot0, gate_c, out_c), idx, kt = its[j]
            Sx, bS = sbank.pop(j)
            po = h % 2
            O, bO = pO[po], bpO[po]
            pr = nxt("pr", 4)
            act(probs[pr][:, 0:N], Sx[:, 0:N], AF.Exp, [bS], [bprobs[pr]])
            mm(O[:, 0:N], Vv[:, kt, slot0:slot0 + 128], probs[pr][:, 0:N], idx == 0, idx == nk - 1,
               [bKV[kt], bprobs[pr]], [bO])
            if idx == nk - 1:
                nr = (h % 2) * 64
                sr = 64 - nr
                T.op("dve", lambda e: e.reciprocal(out=rinv[sr:sr + 64, 0:N], in_=O[sr:sr + 64, 0:N]),
                     reads=[bO], writes=[brinv])
                tt("dve", tmpo[nr:nr + 64, 0:N], O[nr:nr + 64, 0:N], rinv[sr:sr + 64, 0:N], ALU.mult,
                   [bO, brinv], [btmpo])
                tt("pool", ccT[nr:nr + 64, out_c, 0:N], tmpo[nr:nr + 64, 0:N], gT[nr:nr + 64, gate_c, 0:N], ALU.mult,
                   [btmpo, bgT], [bcc[out_c]])

        def A():
            issue_S(0)
            if len(its) > 1:
                issue_S(1)
            for j in range(len(its)):
                if j + 2 < len(its):
                    issue_S(j + 2)
                issue_rest(j)
                yield
        def W():
            for i in range(n):
                k = nxt("xt")
                sap, sbuf_ = src_tiles[i]
                T.dma("sp", "xt%d" % k, xt[k][:], sap, reads=[sbuf_], writes=[bxt[k]])
                rk = 0
                for half in range(2):
                    pb = nxt("po"); Wp, bW = pO[pb], bpO[pb]
                    for c in range(8):
                        if c < 4:
                            lhs = ccT[:, c, i * 128:(i + 1) * 128]; rd = [bcc[c]]
                        elif c < 6:
                            lhs = ycT[:, yset, c - 4, i * 128:(i + 1) * 128]; rd = [byc[yset]]
                        else:
                            lhs = ydv[:, c - 6, ydoff + i * 128: ydoff + (i + 1) * 128]; rd = list(ydbufs)
                        mm(Wp[:, 0:512], lhs, Wout[:, c, half * 512:(half + 1) * 512], c == 0, c == 7,
                           rd + [bWout], [bW], inc=(c == 7))
                    tt("dve", res[rk][:, half * 512:(half + 1) * 512], Wp[:, 0:512],
                       gtbc[:, 0, half * 512:(half + 1) * 512], ALU.mult, [bW, bgt[0]], [bres[rk]])
                    yield
                tt("pool", res[rk][:], res[rk][:], xt[k][:], ALU.add, [bres[rk], bxt[k]], [bres[rk]])
                dap, dbuf = dst_tiles[i]
                T.dma("sp", "rs%d" % rk, dap, res[rk][:], reads=[bres[rk]], writes=[dbuf])
                yield

        return N1, N2_items, A, W

    bXd = [[Buf("xd%d_%d" % (b, t)) for t in range(NT)] for b in range(NB)]
    bCd = [[Buf("cd%d_%d" % (b, t)) for t in range(NC)] for b in range(NB)]
    TPB = QB // 128
    for l in range(DEPTH):
        load_layer(l)
        upd = l < DEPTH - 1
        for b in range(NB):
            xs = x_d if l == 0 else out_d
            cs = ctx_d if l == 0 else ctxw_d
            xsrc = [(xs[b, t * 128:(t + 1) * 128, :], bXd[b][t]) for t in range(NT)]
            xdst = [(out_d[b, t * 128:(t + 1) * 128, :], bXd[b][t]) for t in range(NT)]
            csrc = [(cs[b, t * 128:(t + 1) * 128, :], bCd[b][t]) for t in range(NC)]
            cdst = [(ctxw_d[b, t * 128:(t + 1) * 128, :], bCd[b][t]) for t in range(NC)]
            T.dma("sp", "gt", gtbc[:, 0, :], AP(mods_d.tensor, b * 3 * D + 2 * D, [[0, 128], [1, D]]), writes=[bgt[0]])
            for qb in range(NQB):
                phase1_block(xsrc[qb * TPB:(qb + 1) * TPB], b, False, qb * TPB, qb * TPB,
                             ydT[:, :, qb * QB:(qb + 1) * QB], bydT, G, bG, qb * QB)
            phase1_block(csrc, NB, True, NT, 0, ydTc[:, :, :], [bydTc], Gc, bGc, S, need_fm=upd)
            if USE_HALF:
                fourier_half(G, bG, ydT, bydT)
            else:
                fourier(tabL_d, NKB, KB, NT, G, bG, ydT, bydT)
            if upd:
                fourier(tabC_d, 1, CTX, NC, Gc, bGc, ydTc, [bydTc])
            T.barrier()
            allk = list(range(NK))
            blocks = []
            for qb in range(NQB):
                blocks.append((make_block(xsrc[qb * TPB:(qb + 1) * TPB], xdst[qb * TPB:(qb + 1) * TPB], b, False,
                                          qb * TPB, allk, ydT, qb * QB, bydT, qb % 2, qb * QB), False, allk))
            if upd:
                blocks.append((make_block(csrc, cdst, NB, True, 0, list(range(NT, NK)), ydTc, 0, [bydTc], NQB % 2, S), True, list(range(NT, NK))))
            run_gens([(False, blocks[0][0][0])])
            run_gens(blocks[0][0][1]())
            for k in range(len(blocks)):
                (N1_, N2_, A_, W_) = blocks[k][0]
                nb_ = blocks[k + 1] if k + 1 < len(blocks) else None
                nA = 8 * len(blocks[k][2])
                wA = max(1, nA // 56)
                run_gens([(False, A_, wA)] + ([(False, nb_[0][0])] if nb_ else []))
                run_gens((nb_[0][1]() if nb_ else []) + [(False, W_)])
                if nb_ is not None and nb_[1]:
                    T.dma("sp", "gt", gtbc[:, 0, :], AP(mods_d.tensor, NB * 3 * D + 2 * D, [[0, 128], [1, D]]),
                          writes=[bgt[0]])
            T.barrier()
    T.barrier()
    T.emit()
    es.close()
    return nc


def _tables(S, CTX):
    bf = ml_dtypes.bfloat16
    NT, NC = S // 128, CTX // 128
    KB = min(512, S)
    ident = np.eye(128, dtype=np.float32).astype(bf)
    k = np.arange(64)
    ang = 2 * np.pi * np.outer(k, k) / 64.0
    Ec = np.kron(np.eye(4), np.cos(ang)); Es = np.kron(np.eye(4), np.sin(ang))
    ecs = np.concatenate([Ec, Es], 1).astype(np.float32).astype(bf)
    s = np.arange(S)
    row = (s // GRID_W).astype(np.float64); col = (s % GRID_W).astype(np.float64)

    def rt(d):
        inv = THETA ** (-np.arange(0, d, 2, dtype=np.float64) / d)
        outc, outs = [], []
        for pos in (row, col):
            a = pos[:, None] * inv[None, :]
            outc.append(np.concatenate([np.cos(a), np.cos(a)], -1))
            outs.append(np.concatenate([-np.sin(a), np.sin(a)], -1))
        return np.concatenate(outc, -1), np.concatenate(outs, -1)

    def lay(c, sn):
        t = np.stack([c, sn], 1)
        t = t.reshape(NT, 128, 2, -1).transpose(1, 0, 2, 3)
        return np.ascontiguousarray(t).astype(np.float32).astype(bf)

    ropeA = lay(*rt(16)); ropeB = lay(*rt(32))

    def dft(n, kb):
        ss = np.arange(n, dtype=np.int64)
        m = np.outer(ss, ss) % n
        a = 2 * np.pi * m / n
        sc = 1.0 / np.sqrt(n * 64.0)
        t = np.stack([np.cos(a) * sc, -np.sin(a) * sc], 0)
        nkb = n // kb
        t = t.reshape(2, n // 128, 128, nkb, kb).transpose(3, 2, 1, 0, 4)
        return np.ascontiguousarray(t).astype(np.float32).astype(bf)

    def dft_half(n):
        h = n // 2
        kb = min(512, h)
        ss = np.arange(n, dtype=np.int64)
        kk = np.arange(h, dtype=np.int64)
        a = 2 * np.pi * (np.outer(ss, kk) % n) / n
        sc = 1.0 / np.sqrt(n * 64.0)
        t = np.stack([np.cos(a) * sc, np.sin(a) * sc], 0)
        t = t.reshape(2, n // 128, 128, h // kb, kb).transpose(3, 2, 1, 0, 4)
        return np.ascontiguousarray(t).astype(np.float32).astype(bf)

    altc = (((-1.0) ** np.arange(128)) / np.sqrt(S * 64.0)).reshape(128, 1).astype(np.float32).astype(bf)
    d_ = dict(ident=ident, ecs=ecs, ropeA=ropeA, ropeB=ropeB, tabH=dft_half(S), altc=altc, tabC=dft(CTX, CTX))
    if not USE_HALF:
        d_["tabL"] = dft(S, KB)
    return d_


_CACHE = {}


def run(inputs, ncores):
    x = np.asarray(inputs["x"], np.float32)
    B, S, _ = x.shape
    ctx = np.asarray(inputs["ctx"], np.float32)
    CTX = ctx.shape[1]
    NB = B // ncores
    key = (NB, S, CTX)
    if key not in _CACHE:
        _CACHE[key] = (build(NB, S, CTX), _tables(S, CTX))
    nc, tabs = _CACHE[key]
    c = np.asarray(inputs["c"], np.float32)
    c_ctx = np.asarray(inputs["c_ctx"], np.float32)
    shared = {k: np.ascontiguousarray(np.asarray(inputs[k], np.float32)) for k in (
        "norm_g", "w_mod", "b_mod", "w_in", "mla_q_norm", "mla_w_uq", "mla_kv_norm", "mla_w_ukv",
        "mla_qn", "mla_kn", "gqa_qn", "gqa_kn", "cm_ln_g", "cm_ln_b", "cm_w_s", "fnet_w", "w_out")}
    shared["cm_b_s"] = np.ascontiguousarray(np.asarray(inputs["cm_b_s"], np.float32).reshape(DEPTH, 512))
    shared.update(tabs)
    in_maps = []
    for i in range(ncores):
        m = dict(shared)
        m["x"] = np.ascontiguousarray(x[i * NB:(i + 1) * NB])
        m["ctx"] = np.ascontiguousarray(ctx[i * NB:(i + 1) * NB])
        m["c"] = np.ascontiguousarray(np.concatenate([c[i * NB:(i + 1) * NB], c_ctx[None, :]], 0))
        in_maps.append(m)
    r = run_bass_kernel_spmd(nc, in_maps, core_ids=list(range(ncores)))
    return np.concatenate([np.asarray(r.results[i]["out"], np.float32) for i in range(ncores)], 0)


def kernel(**inputs):
    return run(inputs, NCORES)
```
